# Optimizing a Trainium2 kernel written in Bass

```python
import math
import jax, jax.numpy as jnp
from jax import lax
import numpy as np

D_MODEL = 1024
BATCH = 8
SEQ = 8192
DEPTH = 2

GRID_W = 64
CTX_LEN = 256
N_MIXERS = 2
N_RWKV = (DEPTH + N_MIXERS - 1) // N_MIXERS
N_MLA = DEPTH // N_MIXERS
NORM_EPS = 1e-6
RWKV_HEAD = 64
RWKV_HEADS = D_MODEL // RWKV_HEAD
DECAY_LORA = 64
ICLR_LORA = 64
GN_EPS = 64e-5
N_DIR = 2
MLA_HEADS = 16
QK_NOPE = 64
QK_ROPE = 32
V_HEAD = 64
Q_LORA = 384
KV_LORA = 256
MLA_GATE = MLA_HEADS * V_HEAD
MLA_IN = Q_LORA + KV_LORA + QK_ROPE + MLA_GATE
ROPE_THETA = 10000.0
Q_BLOCK = 128

kernel_name = 'hybrid_rwkv7_mla_dit_prefix'


def rmsnorm(x, g, eps=NORM_EPS):
    xf = x.astype(jnp.float32)
    y = xf * lax.rsqrt(jnp.mean(xf * xf, axis=-1, keepdims=True) + eps)
    return (y * g.astype(jnp.float32)).astype(x.dtype)


def centred_shift(h):
    prev = jnp.pad(h[:, :-1], ((0, 0), (1, 0), (0, 0)))
    nxt = jnp.pad(h[:, 1:], ((0, 0), (0, 1), (0, 0)))
    return 0.5 * (prev + nxt) - h


def rwkv_prepare(h, mu, w_in, w0, w1, w2, a0, a1, a2, k_k, k_a):
    B, T, D = h.shape
    xx = centred_shift(h)
    lerp = lambda j: h + xx * mu[j]
    heads = lambda t: t.reshape(t.shape[:-1] + (RWKV_HEADS, RWKV_HEAD))
    r = heads(lerp(0) @ w_in[0])
    k = heads(lerp(1) @ w_in[1])
    v = heads(lerp(2) @ w_in[2])
    gate = lerp(3) @ w_in[3]
    xw, xa = lerp(4), lerp(5)
    wpre = w0[:, None, None, :] + jnp.einsum('nbtr,nrd->nbtd', jnp.tanh(jnp.einsum('btd,ndr->nbtr', xw, w1)), w2)
    wpre = wpre.astype(jnp.float32)
    logw = heads(-jnp.exp(-jax.nn.softplus(-wpre) - 0.5))
    a = heads(jax.nn.sigmoid(a0[:, None, None, :] + jnp.einsum('nbtr,nrd->nbtd', jnp.einsum('btd,ndr->nbtr', xa, a1), a2)))
    kkf = (k * k_k.reshape(RWKV_HEADS, RWKV_HEAD)).astype(jnp.float32)
    kk = (kkf / jnp.maximum(jnp.linalg.norm(kkf, axis=-1, keepdims=True), 1e-12)).astype(h.dtype)
    k_dir = k[None] * (1.0 + (a - 1.0) * k_a.reshape(RWKV_HEADS, RWKV_HEAD))
    b = kk[None] * a
    return r, logw, k_dir, v, -kk, b, gate


def orient_shared(t):
    return jnp.stack([t, jnp.flip(t, 1)])


def orient_dir(t):
    return jnp.stack([t[0], jnp.flip(t[1], 1)])


def wkv_scan(r, logw, k, v, a, b, s0):
    tm = lambda t: jnp.moveaxis(t.astype(jnp.float32), 2, 0)

    def step(s, inp):
        r_t, lw_t, k_t, v_t, a_t, b_t = inp
        sa = jnp.einsum('dbhij,dbhj->dbhi', s, a_t)
        s = s * jnp.exp(lw_t)[..., None, :] + sa[..., None] * b_t[..., None, :] + v_t[..., None] * k_t[..., None, :]
        return s, jnp.einsum('dbhij,dbhj->dbhi', s, r_t)

    s_fin, y = lax.scan(step, s0, (tm(r), tm(logw), tm(k), tm(v), tm(a), tm(b)))
    return jnp.moveaxis(y, 0, 2), s_fin


def rwkv_run(prep, s0):
    r, logw, k_dir, v, a_vec, b, gate = prep
    return wkv_scan(orient_shared(r), orient_dir(logw), orient_dir(k_dir), orient_shared(v),
                    orient_shared(a_vec), orient_dir(b), s0)


def rwkv_readout(y, prep, r_k, lnx_g, lnx_b, w_out):
    r, logw, k_dir, v, a_vec, b, gate = prep
    B, T, D = gate.shape
    yf = y[0] + jnp.flip(y[1], 1)
    mean = jnp.mean(yf, axis=-1, keepdims=True)
    var = jnp.mean(jnp.square(yf - mean), axis=-1, keepdims=True)
    yn = ((yf - mean) * lax.rsqrt(var + GN_EPS)).reshape(B, T, D).astype(gate.dtype) * lnx_g + lnx_b
    bonus = jnp.sum(r[None] * k_dir * r_k, axis=(0, -1))[..., None] * v
    o = (yn + bonus.reshape(B, T, D)) * jax.nn.silu(gate)
    return o @ w_out


def rwkv_mixer(h_x, h_c, mu, w_in, w0, w1, w2, a0, a1, a2, k_k, k_a, r_k, lnx_g, lnx_b, w_out, need_ctx):
    B = h_x.shape[0]
    p_c = rwkv_prepare(h_c, mu, w_in, w0, w1, w2, a0, a1, a2, k_k, k_a)
    p_x = rwkv_prepare(h_x, mu, w_in, w0, w1, w2, a0, a1, a2, k_k, k_a)
    s0 = jnp.zeros((N_DIR, B, RWKV_HEADS, RWKV_HEAD, RWKV_HEAD), jnp.float32)
    y_c, s_c = rwkv_run(p_c, s0)
    y_x, _ = rwkv_run(p_x, s_c)
    d_x = rwkv_readout(y_x, p_x, r_k, lnx_g, lnx_b, w_out)
    d_c = rwkv_readout(y_c, p_c, r_k, lnx_g, lnx_b, w_out) if need_ctx else None
    return d_x, d_c


def axial_rope_angles(T, rows):
    row = jnp.broadcast_to(jnp.arange(rows)[:, None], (rows, GRID_W)).reshape(T).astype(jnp.float32)
    col = jnp.broadcast_to(jnp.arange(GRID_W)[None, :], (rows, GRID_W)).reshape(T).astype(jnp.float32)
    n_axis = QK_ROPE // 2
    inv = 1.0 / (ROPE_THETA ** (jnp.arange(0, n_axis, 2, dtype=jnp.float32) / n_axis))
    ang = jnp.concatenate([row[:, None] * inv, col[:, None] * inv], axis=-1)
    return jnp.cos(ang), jnp.sin(ang)


def apply_rope(x, cos, sin):
    half = QK_ROPE // 2
    x1, x2 = x[..., :half], x[..., half:]
    c = cos[:, None, :].astype(x.dtype)
    s = sin[:, None, :].astype(x.dtype)
    return jnp.concatenate([x1 * c - x2 * s, x1 * s + x2 * c], axis=-1)


def mla_project(h, w_in, q_norm_g, w_qb, kv_norm_g, w_kvb, cos=None, sin=None):
    B, T, D = h.shape
    p = h @ w_in
    q_c, kv_c, k_pe, gate = jnp.split(p, [Q_LORA, Q_LORA + KV_LORA, Q_LORA + KV_LORA + QK_ROPE], axis=-1)
    q = (rmsnorm(q_c, q_norm_g) @ w_qb).reshape(B, T, MLA_HEADS, QK_NOPE + QK_ROPE)
    kv = (rmsnorm(kv_c, kv_norm_g) @ w_kvb).reshape(B, T, MLA_HEADS, QK_NOPE + V_HEAD)
    q_nope, q_pe = q[..., :QK_NOPE], q[..., QK_NOPE:]
    k_nope, v = kv[..., :QK_NOPE], kv[..., QK_NOPE:]
    k_pe = k_pe[:, :, None, :]
    if cos is not None:
        q_pe = apply_rope(q_pe, cos, sin)
        k_pe = apply_rope(k_pe, cos, sin)
    q = jnp.concatenate([q_nope, q_pe], axis=-1)
    k = jnp.concatenate([k_nope, jnp.broadcast_to(k_pe, (B, T, MLA_HEADS, QK_ROPE))], axis=-1)
    return q, k, v, gate


def attend(q, k, v):
    s = jnp.einsum('bqhd,bkhd->bhqk', q, k, preferred_element_type=jnp.float32) * (1.0 / math.sqrt(QK_NOPE + QK_ROPE))
    p = jax.nn.softmax(s, axis=-1).astype(v.dtype)
    return jnp.einsum('bhqk,bkhd->bqhd', p, v, preferred_element_type=jnp.float32).astype(q.dtype)


def mla_mixer(h_x, h_c, cos, sin, w_in, q_norm_g, w_qb, kv_norm_g, w_kvb, w_out, need_ctx):
    B, T, D = h_x.shape
    L = h_c.shape[1]
    q_x, k_x, v_x, g_x = mla_project(h_x, w_in, q_norm_g, w_qb, kv_norm_g, w_kvb, cos, sin)
    q_c, k_c, v_c, g_c = mla_project(h_c, w_in, q_norm_g, w_qb, kv_norm_g, w_kvb)
    k_all = jnp.concatenate([k_x, k_c], axis=1)
    v_all = jnp.concatenate([v_x, v_c], axis=1)
    qb = q_x.reshape(B, T // Q_BLOCK, Q_BLOCK, MLA_HEADS, QK_NOPE + QK_ROPE).swapaxes(0, 1)
    o = lax.map(lambda qblk: attend(qblk, k_all, v_all), qb)
    o = o.swapaxes(0, 1).reshape(B, T, MLA_GATE)
    d_x = (o * jax.nn.silu(g_x)) @ w_out
    d_c = None
    if need_ctx:
        o_c = attend(q_c, k_c, v_c).reshape(B, L, MLA_GATE)
        d_c = (o_c * jax.nn.silu(g_c)) @ w_out
    return d_x, d_c


def setup_inputs(seed: int = 0) -> dict:
    key = jax.random.key(seed)
    ks = iter(jax.random.split(key, 40))
    nrm = lambda shape, scale: scale * jax.random.normal(next(ks), shape, jnp.float32)
    D, NR, NM = D_MODEL, N_RWKV, N_MLA
    return {
        'x': nrm((BATCH, SEQ, D), 1.0),
        'c': nrm((BATCH, D), 1.0),
        'ctx': nrm((BATCH, CTX_LEN, D), 1.0),
        'c_ctx': nrm((D,), 1.0),
        'norm_g': 1.0 + nrm((DEPTH, D), 0.05),
        'mod_w': nrm((DEPTH, D, 3 * D), 0.5 * D ** -0.5),
        'mod_b': nrm((DEPTH, 3 * D), 0.05),
        'rwkv_mu': jax.random.uniform(next(ks), (NR, 6, D), jnp.float32),
        'rwkv_w_in': nrm((NR, 4, D, D), D ** -0.5),
        'rwkv_w0': jax.random.uniform(next(ks), (NR, N_DIR, D), jnp.float32, -6.0, -1.0),
        'rwkv_w1': nrm((NR, N_DIR, D, DECAY_LORA), D ** -0.5),
        'rwkv_w2': nrm((NR, N_DIR, DECAY_LORA, D), 0.5 * DECAY_LORA ** -0.5),
        'rwkv_a0': nrm((NR, N_DIR, D), 0.1),
        'rwkv_a1': nrm((NR, N_DIR, D, ICLR_LORA), D ** -0.5),
        'rwkv_a2': nrm((NR, N_DIR, ICLR_LORA, D), 0.5 * ICLR_LORA ** -0.5),
        'rwkv_k_k': 0.85 + nrm((NR, D), 0.05),
        'rwkv_k_a': 1.0 + nrm((NR, D), 0.05),
        'rwkv_r_k': nrm((NR, RWKV_HEADS, RWKV_HEAD), 0.1),
        'rwkv_lnx_g': 1.0 + nrm((NR, D), 0.05),
        'rwkv_lnx_b': nrm((NR, D), 0.02),
        'rwkv_w_out': nrm((NR, D, D), D ** -0.5),
        'mla_w_in': nrm((NM, D, MLA_IN), D ** -0.5),
        'mla_q_norm_g': 1.0 + nrm((NM, Q_LORA), 0.05),
        'mla_w_qb': nrm((NM, Q_LORA, MLA_HEADS * (QK_NOPE + QK_ROPE)), Q_LORA ** -0.5),
        'mla_kv_norm_g': 1.0 + nrm((NM, KV_LORA), 0.05),
        'mla_w_kvb': nrm((NM, KV_LORA, MLA_HEADS * (QK_NOPE + V_HEAD)), KV_LORA ** -0.5),
        'mla_w_out': nrm((NM, MLA_GATE, D), MLA_GATE ** -0.5),
        'final_g': 1.0 + nrm((D,), 0.05),
    }


def reference(x, c, ctx, c_ctx, norm_g, mod_w, mod_b, rwkv_mu, rwkv_w_in, rwkv_w0, rwkv_w1, rwkv_w2,
              rwkv_a0, rwkv_a1, rwkv_a2, rwkv_k_k, rwkv_k_a, rwkv_r_k, rwkv_lnx_g, rwkv_lnx_b, rwkv_w_out,
              mla_w_in, mla_q_norm_g, mla_w_qb, mla_kv_norm_g, mla_w_kvb, mla_w_out, final_g):
    T = x.shape[1]
    ROWS = T // GRID_W
    cos, sin = axial_rope_angles(T, ROWS)
    for i in range(DEPTH):
        last = i == DEPTH - 1
        mod_x = jax.nn.silu(c) @ mod_w[i] + mod_b[i]
        mod_c = jax.nn.silu(c_ctx) @ mod_w[i] + mod_b[i]
        sh_x, sc_x, g_x = jnp.split(mod_x[:, None, :], 3, axis=-1)
        sh_c, sc_c, g_c = jnp.split(mod_c, 3, axis=-1)
        h_x = rmsnorm(x, norm_g[i]) * (1.0 + sc_x) + sh_x
        h_c = rmsnorm(ctx, norm_g[i]) * (1.0 + sc_c) + sh_c
        j = i // N_MIXERS
        if i % N_MIXERS == 0:
            d_x, d_c = rwkv_mixer(h_x, h_c, rwkv_mu[j], rwkv_w_in[j], rwkv_w0[j], rwkv_w1[j], rwkv_w2[j],
                                  rwkv_a0[j], rwkv_a1[j], rwkv_a2[j], rwkv_k_k[j], rwkv_k_a[j], rwkv_r_k[j],
                                  rwkv_lnx_g[j], rwkv_lnx_b[j], rwkv_w_out[j], not last)
        else:
            d_x, d_c = mla_mixer(h_x, h_c, cos, sin, mla_w_in[j], mla_q_norm_g[j], mla_w_qb[j],
                                 mla_kv_norm_g[j], mla_w_kvb[j], mla_w_out[j], not last)
        x = x + g_x * d_x
        if not last:
            ctx = ctx + g_c * d_c
    return rmsnorm(x, final_g)
```

```python
import contextlib
import math
import numpy as np
import concourse.bass as bass
import concourse.mybir as mybir
from concourse.bass_utils import run_bass_kernel_spmd

F32 = mybir.dt.float32
F32R = mybir.dt.float32
BF16 = mybir.dt.bfloat16
ALU = mybir.AluOpType
AF = mybir.ActivationFunctionType
AX = mybir.AxisListType

N_DMA_SEMS = 8
DEBUG_SRC = False
R2CUT = 99
R2VAR = 0
NO_SELF_SYNC = ("tensor",)

D = 1024
NH = 16
C0 = math.exp(-0.5)


class V:
    __slots__ = ("buf", "ap")

    def __init__(self, buf, ap):
        self.buf = buf
        self.ap = ap

    def __getitem__(self, k):
        return V(self.buf, self.ap[k])

    def re(self, s, **kw):
        return V(self.buf, self.ap.rearrange(s, **kw))

    def bc(self, axis, shape):
        return V(self.buf, self.ap.unsqueeze(axis).to_broadcast(list(shape)))

    def cast(self, dt):
        return V(self.buf, self.ap.bitcast(dt))


class Buf:
    __slots__ = ("name", "w", "r", "ap")

    def __init__(self, name, ap):
        self.name = name
        self.w = None
        self.r = []
        self.ap = ap

    def __getitem__(self, k):
        return V(self, self.ap[k])

    @property
    def v(self):
        return V(self, self.ap)

    def re(self, s, **kw):
        return V(self, self.ap.rearrange(s, **kw))


def _v(x):
    return x.v if isinstance(x, Buf) else x


class Op:
    __slots__ = ("eng", "fn", "deps", "sig", "cnt", "dma_sem", "dma_val", "dma_prev", "src")

    def __init__(self, eng, fn):
        self.src = None
        self.eng = eng
        self.fn = fn
        self.deps = []
        self.sig = False
        self.cnt = 0
        self.dma_sem = None
        self.dma_val = 0
        self.dma_prev = None


class Eng:
    def __init__(self, name):
        self.name = name
        self.ops = []
        self.sem = None
        self.dma_sems = []
        self.dma_uses = [0] * N_DMA_SEMS
        self.dma_last = [None] * N_DMA_SEMS
        self.n_dma = 0


class Prog:
    def __init__(self, nc):
        self.nc = nc
        self.engs = {n: Eng(n) for n in ("tensor", "vector", "scalar", "gpsimd", "sync")}
        self.stack = contextlib.ExitStack()
        self.n_ops = 0
        self._rr = {}

    def sbuf(self, name, shape, dtype=F32):
        t = self.stack.enter_context(self.nc.sbuf_tensor(name, list(shape), dtype))
        return Buf(name, t[:])

    def psum(self, name, shape, dtype=F32):
        t = self.stack.enter_context(self.nc.psum_tensor(name, list(shape), dtype))
        return Buf(name, t[:])

    def dram(self, name, shape, dtype=F32, kind="Internal"):
        t = self.nc.dram_tensor(name, list(shape), dtype, kind=kind)
        return Buf(name, t.ap())

    def sub(self, buf, name, key):
        return Buf(name, buf.ap[key])

    def ring(self, name, n, shape, dtype=F32, space="sbuf"):
        mk = self.sbuf if space == "sbuf" else self.psum
        return Ring([mk("%s%d" % (name, i), shape, dtype) for i in range(n)])

    def op(self, eng, fn, reads=(), writes=()):
        e = self.engs[eng]
        o = Op(e, fn)
        if DEBUG_SRC:
            import sys as _s
            fr = _s._getframe(1)
            while fr.f_code.co_name in ("op", "_c", "mm", "tr", "act", "tt", "ts", "stt", "copy", "recip", "memset", "dma"):
                fr = fr.f_back
            o.src = fr.f_lineno
        deps = {}
        for b in reads:
            if b.w is not None:
                deps[id(b.w)] = b.w
        for b in writes:
            if b.w is not None:
                deps[id(b.w)] = b.w
            for r in b.r:
                deps[id(r)] = r
        for d in deps.values():
            if d.dma_sem is None and d.eng is e and e.name in NO_SELF_SYNC:
                continue
            o.deps.append(d)
            if d.dma_sem is None:
                d.sig = True
        for b in reads:
            b.r.append(o)
        for b in writes:
            b.w = o
            b.r = []
        e.ops.append(o)
        self.n_ops += 1
        return o

    def dma(self, eng, out, in_, **kw):
        out, in_ = _v(out), _v(in_)
        e = self.engs[eng]
        s = e.n_dma % N_DMA_SEMS
        e.n_dma += 1
        o = self.op(eng, ("dma", out.ap, in_.ap, kw), [in_.buf], [out.buf])
        o.dma_sem = s
        e.dma_uses[s] += 1
        o.dma_val = 16 * e.dma_uses[s]
        o.dma_prev = e.dma_last[s]
        e.dma_last[s] = o
        return o

    def _c(self, eng, method, outs, kw, extra_reads=()):
        reads, writes, args = list(extra_reads), [], {}
        for k, a in kw.items():
            if isinstance(a, (V, Buf)):
                a = _v(a)
                (writes if k in outs else reads).append(a.buf)
                args[k] = a.ap
            else:
                args[k] = a
        return self.op(eng, lambda q: getattr(q, method)(**args), reads, writes)

    def mm(self, out, lhsT, rhs, start=True, stop=True, skip=False):
        out, lhsT, rhs = _v(out), _v(lhsT), _v(rhs)
        oa, la, ra = out.ap, lhsT.ap, rhs.ap
        assert len(ra.shape) == 2 and len(la.shape) == 2 and len(oa.shape) == 2, (oa.shape, la.shape, ra.shape)
        return self.op("tensor", lambda q: q.matmul(oa, lhsT=la, rhs=ra, start=start, stop=stop,
                                                    skip_group_check=skip),
                       [lhsT.buf, rhs.buf], [out.buf])

    def tr(self, out, in_, ident):
        out, in_, ident = _v(out), _v(in_), _v(ident)
        oa, ia, da = out.ap, in_.ap, ident.ap
        return self.op("tensor", lambda q: q.transpose(out=oa, in_=ia, identity=da),
                       [in_.buf, ident.buf], [out.buf])

    def act(self, out, in_, func, bias=0.0, scale=1.0, accum_out=None, eng="scalar"):
        kw = dict(out=out, in_=in_, func=func, bias=bias, scale=scale)
        outs = ["out"]
        if accum_out is not None:
            kw["accum_out"] = accum_out
            outs.append("accum_out")
        return self._c("scalar", "activation", outs, kw)

    def tt(self, eng, out, in0, in1, op):
        return self._c(eng, "tensor_tensor", ["out"], dict(out=out, in0=in0, in1=in1, op=op))

    def ts(self, eng, out, in0, s1, op0, s2=None, op1=None):
        kw = dict(out=out, in0=in0, scalar1=s1, scalar2=s2, op0=op0)
        if op1 is not None:
            kw["op1"] = op1
        return self._c(eng, "tensor_scalar", ["out"], kw)

    def stt(self, eng, out, in0, scalar, in1, op0, op1):
        return self._c(eng, "scalar_tensor_tensor", ["out"],
                       dict(out=out, in0=in0, scalar=scalar, in1=in1, op0=op0, op1=op1))

    def copy(self, eng, out, in_):
        if eng == "scalar":
            return self._c("scalar", "copy", ["out"], dict(out=out, in_=in_))
        return self._c(eng, "tensor_copy", ["out"], dict(out=out, in_=in_))

    def recip(self, out, in_):
        return self._c("vector", "reciprocal", ["out"], dict(out=out, in_=in_))

    def memset(self, eng, out, val):
        out = _v(out)
        oa = out.ap
        return self.op(eng, lambda q: q.memset(oa, val), [], [out.buf])

    def rr(self, key, engs):
        i = self._rr.get(key, 0)
        self._rr[key] = i + 1
        return engs[i % len(engs)]

    def finish(self):
        nc = self.nc
        st = self.stack
        for e in self.engs.values():
            e.sem = st.enter_context(nc.semaphore("s_" + e.name))
            if e.n_dma:
                e.dma_sems = [st.enter_context(nc.semaphore("d_%s%d" % (e.name, i)))
                              for i in range(N_DMA_SEMS)]
        for e in self.engs.values():
            c = 0
            for o in e.ops:
                if o.dma_sem is None and o.sig:
                    c += 1
                    o.cnt = c
        block = st.enter_context(nc.Block())
        prog = self

        def replay(e, q):
            waited = {}

            def wait(sem, key, val):
                if waited.get(key, 0) < val:
                    q.wait_ge(sem, val)
                    waited[key] = val

            for o in e.ops:
                for d in o.deps:
                    if d.dma_sem is not None:
                        wait(d.eng.dma_sems[d.dma_sem], (d.eng.name, d.dma_sem), d.dma_val)
                    else:
                        wait(d.eng.sem, d.eng.name, d.cnt)
                if o.dma_sem is not None:
                    p = o.dma_prev
                    if p is not None:
                        wait(e.dma_sems[p.dma_sem], (e.name, p.dma_sem), p.dma_val)
                    _, oa, ia, kw = o.fn
                    q.dma_start(out=oa, in_=ia, **kw).then_inc(e.dma_sems[o.dma_sem], 16)
                elif o.fn is not None:
                    ins = o.fn(q)
                    if o.sig:
                        ins.then_inc(e.sem, 1)
            if e.name == "sync":
                for e2 in prog.engs.values():
                    if e2.n_dma:
                        for s in range(N_DMA_SEMS):
                            lo = e2.dma_last[s]
                            if lo is not None:
                                wait(e2.dma_sems[s], (e2.name, s), lo.dma_val)

        engs = self.engs

        @block.tensor
        def _(q):
            replay(engs["tensor"], q)

        @block.vector
        def _(q):
            replay(engs["vector"], q)

        @block.scalar
        def _(q):
            replay(engs["scalar"], q)

        @block.gpsimd
        def _(q):
            replay(engs["gpsimd"], q)

        @block.sync
        def _(q):
            replay(engs["sync"], q)

        st.close()


class Ring:
    def __init__(self, bufs):
        self.bufs = bufs
        self.i = 0

    def next(self):
        b = self.bufs[self.i % len(self.bufs)]
        self.i += 1
        return b


DT_SIZE = {F32: 4, F32R: 4, BF16: 2}


class Arena:
    def __init__(self, P, nbytes):
        self.P = P
        self.n4 = nbytes // 4
        t = P.stack.enter_context(P.nc.sbuf_tensor("arena", [128, self.n4], F32))
        self.ap = t[:]
        self.base = 0
        self.off = 0
        self.k = 0

    def persist(self):
        self.base = self.off

    def reset(self):
        self.off = self.base

    def alloc(self, name, shape, dtype=F32, parts=128):
        free = int(np.prod(shape[1:]))
        nb = free * DT_SIZE[dtype]
        n4 = (nb + 31) // 32 * 8
        assert self.off + n4 <= self.n4, "arena overflow at %s (%d KB)" % (name, (self.off + n4) * 4 // 1024)
        ap = self.ap[0:shape[0], self.off:self.off + n4]
        self.off += n4
        if dtype != F32:
            ap = ap.bitcast(dtype)
        ap = ap[:, 0:free]
        if len(shape) == 3:
            ap = ap.rearrange("p (a b) -> p a b", b=shape[2])
        elif len(shape) == 4:
            ap = ap.rearrange("p (a b c) -> p a b c", b=shape[2], c=shape[3])
        self.k += 1
        return Buf("%s_%d" % (name, self.k), ap)

    def ring(self, name, n, shape, dtype=F32):
        return Ring([self.alloc(name, shape, dtype) for _ in range(n)])


def barrier(P, tiny):
    firsts = [P.memset("vector", tiny[0], 0.0), P.memset("gpsimd", tiny[1], 0.0), P.act(tiny[2], tiny[3], AF.Copy)]
    dmas = []
    for e in P.engs.values():
        for s in range(N_DMA_SEMS):
            if e.dma_last[s] is not None:
                dmas.append(e.dma_last[s])
    for f in firsts:
        f.sig = True
    for name, e in P.engs.items():
        o = Op(e, None)
        o.deps = [f for f in firsts if f.eng is not e] + dmas
        e.ops.append(o)


CST_IDENT, CST_BONES, CST_M01, CST_MT, CST_HSEL, CST_SCAN, CST_N = 0, 128, 256, 768, 1024, 1152, 1408


def make_consts():
    c = np.zeros((128, CST_N), np.float32)
    p = np.arange(128)
    c[:, CST_IDENT:CST_IDENT + 128] = np.eye(128)
    c[:, CST_BONES:CST_BONES + 128] = (p[:, None] // 64 == p[None, :] // 64)
    s, t = p[:, None], p[None, :]
    c[:, CST_M01 + 0:CST_M01 + 128] = (t > s)
    c[:, CST_M01 + 128:CST_M01 + 256] = (t >= s)
    c[:, CST_M01 + 256:CST_M01 + 384] = (t < s)
    c[:, CST_M01 + 384:CST_M01 + 512] = (t <= s)
    c[:, CST_MT:CST_MT + 128] = (p[None, :] < p[:, None])
    c[:, CST_MT + 128:CST_MT + 256] = (p[None, :] > p[:, None])
    for g in range(8):
        for h in range(16):
            c[:, CST_HSEL + g * 16 + h] = (h == 2 * g + p // 64)
    sm = np.ones((128, 256), np.float32)
    sm[:, 0] = 0
    sm[:, 128] = 0
    c[:, CST_SCAN:CST_SCAN + 256] = sm
    return c


def fm(vec):
    v = np.asarray(vec, np.float32).reshape(-1, 128)
    return np.ascontiguousarray(v.T)


PV_NG0, PV_NG1, PV_MU, PV_W0, PV_A0, PV_KK, PV_KA, PV_RK, PV_QG, PV_KVG, PV_N = 0, 8, 16, 64, 80, 96, 104, 112, 120, 123, 128


def rope_tables(T):
    rows = T // 64
    row = np.repeat(np.arange(rows), 64).astype(np.float32)
    col = np.tile(np.arange(64), rows).astype(np.float32)
    inv = (1.0 / (10000.0 ** (np.arange(0, 16, 2, dtype=np.float32) / 16))).astype(np.float32)
    ang = np.concatenate([row[:, None] * inv, col[:, None] * inv], axis=-1).astype(np.float32)
    cos, sin = np.cos(ang).T.astype(np.float32), np.sin(ang).T.astype(np.float32)
    out = np.zeros((32, 2, T), np.float32)
    out[0:16, 0], out[16:32, 0] = cos, cos
    out[0:16, 1], out[16:32, 1] = -sin, sin
    return out


class Ctx:
    pass


def build(T, L, stages=("mod", "r1a", "r1b", "r2", "r3", "m1a", "m1b", "m2", "m3"), dbg=()):
    nc = bass.Bass("TRN2", target_bir_lowering=False)
    nc.dge_precook = False
    P = Prog(nc)
    K = Ctx()
    K.P, K.T, K.L = P, T, L
    NT = L + T
    NS = NT // 128
    K.NT, K.NS = NT, NS
    K.CO, K.XO, K.NTP = 1, L + 3, L + T + 4

    def ext(name, shape, dt=F32):
        return P.dram(name, shape, dt, kind="ExternalInput")

    K.x_d = ext("x", [T, D])
    K.ctx_d = ext("ctx", [L, D])
    K.cvec_d = ext("cvec", [128, 16])
    K.pvec_d = ext("pvec", [128, PV_N])
    K.cst_d = ext("cst", [128, CST_N])
    K.rows_d = ext("rows", [3, D])
    K.mod_w_d = ext("mod_w", [2, D, 3 * D])
    K.mod_b_d = ext("mod_b", [2, 3 * D])
    K.w_in_d = ext("rwkv_w_in", [4, D, D])
    K.w1_d = ext("rwkv_w1", [2, D, 64])
    K.w2_d = ext("rwkv_w2", [128, D])
    K.a1_d = ext("rwkv_a1", [2, D, 64])
    K.a2_d = ext("rwkv_a2", [128, D])
    K.wout1_d = ext("rwkv_w_out", [D, D])
    K.mw_in_d = ext("mla_w_in", [D, 1696 + 32])
    K.wqb_d = ext("mla_w_qb", [384, 16 * 128])
    K.wkvb_d = ext("mla_w_kvb", [256, 2048])
    K.wout2_d = ext("mla_w_out", [D, D])
    K.rope_d = ext("rope", [32, 2, T])
    K.out_d = P.dram("out", [T, D], F32, kind="ExternalOutput")

    okind = "ExternalOutput" if dbg else "Internal"

    def scr(name, shape, dt=F32):
        return P.dram(name, shape, dt, kind=("ExternalOutput" if name in dbg else "Internal"))

    K.modrow_d = scr("modrow", [2, 2, 3 * D])
    K.hT_d = scr("hT", [128, 8, K.NTP], BF16)
    K.fm_d = [scr("fm%d" % d, [128, 8, NS, 4, 128], BF16) for d in range(2)]
    K.tm_d = [scr("tm%d" % d, [NT, 3, D], F32R) for d in range(2)]
    K.v_d = scr("vtm", [NT, D], F32R)
    K.sg_d = scr("sg", [NT, D], BF16)
    K.bonus_d = scr("bonus", [NT, 16])
    K.y_d = [scr("y%d" % d, [NT, D]) for d in range(2)]
    K.x1_d = scr("x1", [T, D])
    K.ctx1_d = scr("ctx1", [L, D])
    K.sgT_d = scr("sgT", [128, 8, T], BF16)
    K.oT_d = scr("oT", [128, 8, T], BF16)

    A = Arena(P, 190 * 1024)
    K.A = A
    K.pb = [P.psum("pb%d" % i, [128, 512], F32) for i in range(8)]
    K.pbh = Buf("pbh", K.pb[7].ap.bitcast(BF16))

    K.cst = A.alloc("cst", [128, CST_N])
    K.pv = A.alloc("pv", [128, PV_N])
    K.identb = A.alloc("identb", [128, 128], BF16)
    K.cst_r = A.alloc("cstr", [128, CST_N], F32R)
    K.tiny = [A.alloc("tiny%d" % i, [128, 8]) for i in range(4)]
    K.modT = [A.alloc("modT%d" % i, [128, 24, 2]) for i in range(2)]
    K.modA = [[A.alloc("modA", [128, 8]) for w in range(2)] for i in range(2)]
    K.modB = [[A.alloc("modB", [128, 8]) for w in range(2)] for i in range(2)]
    K.negw0 = A.alloc("negw0", [128, 16])
    K.nega0 = A.alloc("nega0", [128, 16])
    K.omka = A.alloc("omka", [128, 8])
    K.gam = A.alloc("gam", [128, 2, 8, NS])
    K.zero = A.alloc("zero", [128, 8, 1], BF16)
    A.persist()
    P.dma("sync", K.cst, K.cst_d)
    P.dma("sync", K.pv, K.pvec_d)
    P.copy("vector", K.identb, K.cst[:, CST_IDENT:CST_IDENT + 128])
    P.copy("vector", K.cst_r, K.cst)
    for i in range(4):
        P.memset("vector", K.tiny[i], 0.0)
    P.memset("vector", K.zero, 0.0)
    P.ts("vector", K.negw0, K.pv[:, PV_W0:PV_W0 + 16], -1.0, ALU.mult)
    P.ts("vector", K.nega0, K.pv[:, PV_A0:PV_A0 + 16], -1.0, ALU.mult)
    P.ts("vector", K.omka, K.pv[:, PV_KA:PV_KA + 8], -1.0, ALU.mult, 1.0, ALU.add)

    def stage_end():
        barrier(P, K.tiny)
        A.reset()

    if "mod" in stages:
        stage_mod(K)
        stage_end()
    if "r1a" in stages:
        stage_hT(K, 0, K.x_d, K.ctx_d)
        stage_end()
    if "r1b" in stages:
        stage_r1b(K)
        stage_end()
    if "r2" in stages:
        stage_r2(K)
        stage_end()
    if "r3" in stages:
        stage_r3(K)
        stage_end()
    if "m1a" in stages:
        stage_hT(K, 1, K.x1_d, K.ctx1_d)
        stage_end()
    if "m1b" in stages:
        stage_mla(K, "m2" in stages, "m3" in stages)
    P.finish()
    return nc, P


def stage_mod(K):
    P, A = K.P, K.A
    cT = A.alloc("cT", [128, 16])
    sc = A.alloc("scT", [128, 8, 2])
    P.dma("sync", cT, K.cvec_d)
    sg = A.alloc("sgc", [128, 16])
    P.act(sg, cT, AF.Sigmoid)
    P.tt("vector", sc.re("p f w -> p w f"), cT.re("p (w f) -> p w f", w=2), sg.re("p (w f) -> p w f", w=2), ALU.mult)
    mwr = A.ring("mw", 3, [128, 512])
    mb = A.alloc("mb", [2, 3 * D])
    mrow = A.alloc("mrow", [2, 3 * D])
    for layer in range(2):
        P.dma("gpsimd", mb, V(K.mod_b_d, K.mod_b_d.ap[layer:layer + 1, :].partition_broadcast(2)))
        for cb in range(6):
            ps = K.pb[cb % 2]
            for f in range(8):
                mw = mwr.next()
                P.dma("sync", mw, K.mod_w_d[layer, f * 128:(f + 1) * 128, cb * 512:(cb + 1) * 512])
                P.mm(ps[0:2, :], sc[:, f, :], mw, start=(f == 0), stop=(f == 7))
            P.tt("vector", mrow[:, cb * 512:(cb + 1) * 512], ps[0:2, :], mb[:, cb * 512:(cb + 1) * 512], ALU.add)
        P.dma("sync", K.modrow_d[layer], mrow)
        pt = K.pb[2]
        for c in range(24):
            P.tr(pt[:, 2 * c:2 * c + 2], mrow[:, c * 128:(c + 1) * 128], K.cst[0:2, CST_IDENT:CST_IDENT + 2])
        P.copy("vector", K.modT[layer].re("p c w -> p (c w)"), pt[:, 0:48])
        for w in range(2):
            ng = K.pv[:, PV_NG0 + 8 * layer:PV_NG0 + 8 * layer + 8]
            P.stt("vector", K.modA[layer][w], K.modT[layer][:, 8:16, w], 1.0, ng, ALU.add, ALU.mult)
            P.copy("vector", K.modB[layer][w], K.modT[layer][:, 0:8, w])


def seq_tiles(K):
    out = []
    for i in range(K.L // 128):
        out.append((1, i, i * 128, i * 128, K.CO + i * 128))
    for i in range(K.T // 128):
        out.append((0, i, i * 128, K.L + i * 128, K.XO + i * 128))
    return out


def stage_hT(K, layer, x_d, ctx_d):
    P, A = K.P, K.A
    xr = A.ring("xt", 3, [128, D])
    jr = A.ring("junk", 2, [128, D], BF16)
    xnr = A.ring("xn", 2, [128, D], BF16)
    tmpr = A.ring("tmp", 2, [128, 8, 128])
    hr = A.ring("ht", 2, [128, 8, 128], BF16)
    ssr = A.ring("ss", 4, [128, 2])
    for col in (0, K.L + 1, K.L + 2, K.L + K.T + 3):
        P.dma("gpsimd", K.hT_d[:, :, col:col + 1], K.zero, allow_slow_non_contiguous=True)
    for (w, i, r0, tok, col) in seq_tiles(K):
        src = ctx_d if w else x_d
        xt = xr.next()
        P.dma("sync", xt, src[r0:r0 + 128, :])
        ss = ssr.next()
        P.act(jr.next(), xt, AF.Square, accum_out=ss[:, 0:1])
        P.act(ss[:, 1:2], ss[:, 0:1], AF.Ln, bias=1e-6, scale=1.0 / D)
        P.act(ss[:, 1:2], ss[:, 1:2], AF.Exp, scale=-0.5)
        xn = xnr.next()
        P.ts("gpsimd", xn, xt, ss[:, 1:2], ALU.mult)
        for f in range(8):
            P.tr(K.pbh[:, f * 128:(f + 1) * 128], xn[:, f * 128:(f + 1) * 128], K.identb)
        tmp = tmpr.next()
        ht = hr.next()
        P.tt("vector", tmp, K.pbh.re("p (f t) -> p f t", t=128), K.modA[layer][w].v.bc(2, [128, 8, 128]), ALU.mult)
        P.tt("gpsimd", ht, tmp, K.modB[layer][w].v.bc(2, [128, 8, 128]), ALU.add)
        P.dma("sync", K.hT_d[:, :, col:col + 128], ht)


def load_w_bf16(K, name, src_ap_v, shape):
    t = K.A.alloc(name, shape, BF16)
    K.P.dma("gpsimd", t, src_ap_v)
    return t


def stage_r1b(K):
    P, A, L, T = K.P, K.A, K.L, K.T
    NB = 256
    pv, cst = K.pv, K.cst
    W = [A.alloc("win%d" % j, [128, 8, D], BF16) for j in range(4)]
    for j in range(4):
        for f in range(8):
            P.dma("gpsimd", W[j][:, f, :], K.w_in_d[j, f * 128:(f + 1) * 128, :])
    W1c = A.alloc("w1c", [128, 8, 128], BF16)
    A1c = A.alloc("a1c", [128, 8, 128], BF16)
    for d in range(2):
        P.dma("gpsimd", W1c[:, :, 64 * d:64 * d + 64], K.w1_d.re("d (f p) r -> d p f r", p=128)[d])
        P.dma("gpsimd", A1c[:, :, 64 * d:64 * d + 64], K.a1_d.re("d (f p) r -> d p f r", p=128)[d])
    W2c = load_w_bf16(K, "w2c", K.w2_d, [128, D])
    A2c = load_w_bf16(K, "a2c", K.a2_d, [128, D])
    bones_r = K.cst_r[:, CST_BONES:CST_BONES + 128]
    scanm = cst[:, CST_SCAN:CST_SCAN + 256]
    hbr = A.ring("hb", 1, [128, 8, NB + 2], BF16)
    xx = A.alloc("xx", [128, 8, NB])
    tmpx = A.ring("tmpx", 2, [128, NB])
    lerp = [A.alloc("lerp%d" % j, [128, 8, NB], BF16) for j in range(6)]
    twb = A.alloc("twb", [128, NB], BF16)
    tab = A.alloc("tab", [128, NB], BF16)
    prod = A.alloc("prod", [128, 2, 8, NB], BF16)
    hselb = A.alloc("hselb", [128, 128], BF16)
    P.copy("vector", hselb, cst[:, CST_HSEL:CST_HSEL + 128])
    vtm = A.ring("vtm", 1, [128, D], F32R)
    sgt = A.ring("sgt", 1, [128, D], BF16)
    bon = A.ring("bon", 2, [128, 16])
    w = lambda nm, n=2, dt=F32: A.ring(nm, n, [128, NB], dt)
    r_sb, k_sb, kkf, sq, nmx, rn, kk = w("r_sb"), w("k_sb"), w("kkf", 1), w("sq", 1, F32R), w("nmx", 1), w("rn", 1), w("kk")
    ew, sw, ea, av, cum, epos, eneg, dprev, eprev = (w("ew", 1), w("sw"), w("ea", 1), w("av"), w("cum"), w("epos"),
                                                     w("eneg"), w("dprev", 1), w("eprev"))
    mfac, kd, bv, ktil, btil, atil = w("mfac", 1), w("kd"), w("bv"), w("ktil"), w("btil"), w("atil")
    fmo = A.ring("fmo", 2, [128, 2, 4, 128], BF16)
    tmo = A.ring("tmo", 2, [128, 2, 3, 128], F32R)

    blocks = [(1, 0, 0, K.CO)]
    for i in range(T // NB):
        blocks.append((0, L + i * NB, L + i * NB, K.XO + i * NB))
    blocks[0] = (1, 0, 0, K.CO)
    assert L == NB

    for (wh, t0, _, c0) in blocks:
        s0 = t0 // 128
        hb = hbr.next()
        P.dma("sync", hb, K.hT_d[:, :, c0 - 1:c0 + NB + 1])
        for f in range(8):
            tx = tmpx.next()
            P.tt("gpsimd", tx, hb[:, f, 0:NB], hb[:, f, 2:NB + 2], ALU.add)
            P.ts("gpsimd", tx, tx, 0.5, ALU.mult)
            P.tt("gpsimd", xx[:, f, :], tx, hb[:, f, 1:NB + 1], ALU.subtract)
            for j in range(6):
                if P.rr("lerp", ["vector", "gpsimd", "vector"]) == "vector":
                    P.stt("vector", lerp[j][:, f, :], xx[:, f, :],
                          pv[:, PV_MU + 8 * j + f:PV_MU + 8 * j + f + 1], hb[:, f, 1:NB + 1], ALU.mult, ALU.add)
                else:
                    tl = tmpx.next()
                    P.ts("gpsimd", tl, xx[:, f, :], pv[:, PV_MU + 8 * j + f:PV_MU + 8 * j + f + 1], ALU.mult)
                    P.tt("gpsimd", lerp[j][:, f, :], tl, hb[:, f, 1:NB + 1], ALU.add)
        for tt in range(NB // 128):
            for (j, kind) in ((2, "v"), (3, "g")):
                dst = vtm.next() if kind == "v" else sgt.next()
                for ch in range(2):
                    ps = K.pb[ch]
                    for f in range(8):
                        P.mm(ps, lerp[j][:, f, tt * 128:(tt + 1) * 128], W[j][:, f, ch * 512:(ch + 1) * 512],
                             start=(f == 0), stop=(f == 7))
                    if kind == "v":
                        P.copy("scalar", dst[:, ch * 512:(ch + 1) * 512], ps)
                    else:
                        P.act(dst[:, ch * 512:(ch + 1) * 512], ps, AF.Silu)
                if kind == "v":
                    P.dma("sync", K.v_d[t0 + tt * 128:t0 + (tt + 1) * 128, :], dst)
                else:
                    P.dma("sync", K.sg_d[t0 + tt * 128:t0 + (tt + 1) * 128, :], dst)
        ps = K.pb[2]
        for f in range(8):
            P.mm(ps[:, 0:NB], W1c[:, f, :], lerp[4][:, f, :], start=(f == 0), stop=(f == 7))
        P.act(twb, ps[:, 0:NB], AF.Tanh)
        for f in range(8):
            P.mm(ps[:, NB:2 * NB], A1c[:, f, :], lerp[5][:, f, :], start=(f == 0), stop=(f == 7))
        P.copy("vector", tab, ps[:, NB:2 * NB])
        for g in range(8):
            gs = slice(g * 128, (g + 1) * 128)
            pr = K.pb[3]
            for f in range(8):
                P.mm(pr[:, 0:NB], W[0][:, f, gs], lerp[0][:, f, :], start=(f == 0), stop=(f == 7))
            for f in range(8):
                P.mm(pr[:, NB:2 * NB], W[1][:, f, gs], lerp[1][:, f, :], start=(f == 0), stop=(f == 7))
            r_, k_ = r_sb.next(), k_sb.next()
            P.copy("scalar", r_, pr[:, 0:NB])
            P.copy("scalar", k_, pr[:, NB:2 * NB])
            kf, sq_, nm, rn_, kk_ = kkf.next(), sq.next(), nmx.next(), rn.next(), kk.next()
            P.ts("gpsimd", kf, k_, pv[:, PV_KK + g:PV_KK + g + 1], ALU.mult)
            P.tt("gpsimd", sq_, kf, kf, ALU.mult)
            pn = K.pb[4]
            P.mm(pn[:, 0:NB], bones_r, sq_)
            P.ts("vector", nm, pn[:, 0:NB], 1e-24, ALU.max)
            P.act(rn_, nm, AF.Ln)
            P.act(rn_, rn_, AF.Exp, scale=-0.5)
            P.tt("gpsimd", kk_, kf, rn_, ALU.mult)
            for d in range(2):
                hs = slice(64 * d, 64 * d + 64)
                pl = K.pb[5 + (d % 2)]
                P.mm(pl[:, 0:NB], W2c[hs, gs], twb[hs, :])
                P.mm(pl[:, NB:2 * NB], A2c[hs, gs], tab[hs, :])
                ew_, sw_, ea_, a_ = ew.next(), sw.next(), ea.next(), av.next()
                P.act(ew_, pl[:, 0:NB], AF.Exp, bias=K.negw0[:, 8 * d + g:8 * d + g + 1], scale=-1.0)
                P.act(ea_, pl[:, NB:2 * NB], AF.Exp, bias=K.nega0[:, 8 * d + g:8 * d + g + 1], scale=-1.0)
                P.ts("gpsimd", ew_, ew_, 1.0, ALU.add)
                P.recip(sw_, ew_)
                P.ts("gpsimd", ea_, ea_, 1.0, ALU.add)
                P.recip(a_, ea_)
                cm = cum.next()
                if d == 0:
                    P._c("vector", "tensor_tensor_scan", ["out"],
                         dict(out=cm, data0=scanm, data1=sw_, initial=0.0, op0=ALU.mult, op1=ALU.add))
                else:
                    P._c("vector", "tensor_tensor_scan", ["out"],
                         dict(out=cm[:, NB - 1::-1], data0=scanm, data1=sw_[:, NB - 1::-1], initial=0.0,
                              op0=ALU.mult, op1=ALU.add))
                ep, en, dp, epv = epos.next(), eneg.next(), dprev.next(), eprev.next()
                P.act(ep, cm, AF.Exp, scale=-C0)
                P.act(en, cm, AF.Exp, scale=C0)
                P.tt("gpsimd", dp, cm, sw_, ALU.subtract)
                P.act(epv, dp, AF.Exp, scale=-C0)
                for s in range(NB // 128):
                    cc = s * 128 + (127 if d == 0 else 0)
                    P.copy("gpsimd", K.gam[:, d, g, s0 + s:s0 + s + 1], ep[:, cc:cc + 1])
                mf, kd_, b_ = mfac.next(), kd.next(), bv.next()
                P.ts("gpsimd", mf, a_, pv[:, PV_KA + g:PV_KA + g + 1], ALU.mult, K.omka[:, g:g + 1], ALU.add)
                P.tt("gpsimd", kd_, k_, mf, ALU.mult)
                P.tt("gpsimd", b_, kk_, a_, ALU.mult)
                kt_, bt_, at_ = ktil.next(), btil.next(), atil.next()
                P.tt("vector", kt_, kd_, en, ALU.mult)
                P.tt("vector", bt_, b_, en, ALU.mult)
                P.stt("vector", at_, kk_, -1.0, epv, ALU.mult, ALU.mult)
                tp = tmpx.next()
                P.ts("gpsimd", tp, r_, pv[:, PV_RK + g:PV_RK + g + 1], ALU.mult)
                P.tt("gpsimd", prod[:, d, g, :], tp, kd_, ALU.mult)
                fo = fmo.next()
                fo3 = lambda arr: fo[:, :, arr, :]
                P.copy("gpsimd", fo3(0), bt_.re("p (s t) -> p s t", t=128))
                P.copy("gpsimd", fo3(1), kt_.re("p (s t) -> p s t", t=128))
                P.copy("gpsimd", fo3(2), at_.re("p (s t) -> p s t", t=128))
                P.tt("vector", fo3(3), r_.re("p (s t) -> p s t", t=128), ep.re("p (s t) -> p s t", t=128), ALU.mult)
                P.dma("sync", K.fm_d[d][:, g, s0:s0 + NB // 128, :, :], fo)
                to = tmo.next()
                pt, pq = K.pb[d % 2], K.pb[2 + (d % 2)]
                for tt in range(NB // 128):
                    for ai, src in enumerate((at_, bt_, kt_)):
                        idx = tt * 3 + ai
                        dst = pt[:, idx * 128:(idx + 1) * 128] if idx < 4 else pq[:, (idx - 4) * 128:(idx - 3) * 128]
                        P.tr(dst, src[:, tt * 128:(tt + 1) * 128], cst[:, CST_IDENT:CST_IDENT + 128])
                tof = to.re("p t a f -> p (t a f)")
                P.copy("scalar", tof[:, 0:512], pt)
                P.copy("scalar", tof[:, 512:768], pq[:, 0:256])
                for tt in range(NB // 128):
                    P.dma("sync", K.tm_d[d][t0 + tt * 128:t0 + (tt + 1) * 128, :, gs], to[:, tt, :, :])
        for tt in range(NB // 128):
            pbn = K.pb[4]
            n = 0
            for d in range(2):
                for g in range(8):
                    P.mm(pbn[:, 256 + 16 * tt:256 + 16 * tt + 16], prod[:, d, g, tt * 128:(tt + 1) * 128],
                         hselb[:, 16 * g:16 * g + 16], start=(n == 0), stop=(n == 15))
                    n += 1
            b_t = bon.next()
            P.copy("vector", b_t, pbn[:, 256 + 16 * tt:256 + 16 * tt + 16])
            P.dma("sync", K.bonus_d[t0 + tt * 128:t0 + (tt + 1) * 128, :], b_t)


def stage_r2(K):
    P, A, L, T, NS = K.P, K.A, K.L, K.T, K.NS
    cst = K.cst
    m512 = [A.alloc("m512", [128, 2, 256]) for d in range(2)]
    for d in range(2):
        for r in range(2):
            P.copy("gpsimd", m512[d][:, r, :], cst[:, CST_M01 + 256 * d:CST_M01 + 256 * d + 256])
    i64 = A.alloc("i64", [128, 64])
    for p in range(2):
        P.copy("gpsimd", i64[64 * p:64 * p + 64, :], cst[64 * p:64 * p + 64, CST_IDENT + 64 * p:CST_IDENT + 64 * p + 64])
    tmbr = A.ring("tmb", 2, [128, 3, D], F32R)
    vbr = A.ring("vb", 2, [128, D], F32R)
    fmbr = A.ring("fmb", 3, [128, 4, 128], BF16)
    sabr = A.ring("sab", 2, [128, 2, 512], F32)
    p0r = A.ring("p0", 2, [128, 2, 128], F32)
    xr = A.ring("xp", 4, [128, 2, 128], F32)
    rpr = A.ring("rp", 3, [128, 2, 256], F32)
    igr = A.ring("ig", 2, [128, 128], F32R)
    ahr = A.ring("ah", 2, [128, 128], F32)
    rhr = A.ring("rh", 2, [128, 256], F32R)
    ysr = A.ring("ys", 2, [128, D])
    Mst = [A.alloc("Mst", [128, 8, 2, 64], F32R) for _ in range(2)]
    for b in igr.bufs + rhr.bufs + Mst:
        P.memset("gpsimd", b, 0.0)
    pb = K.pb
    PAB = [pb[p].v for p in range(2)]
    PC = [pb[2][:, p * 128:(p + 1) * 128] for p in range(2)]
    PXi = [pb[2][:, 256 + p * 64:256 + (p + 1) * 64] for p in range(2)]
    PG = pb[2][:, 384:512]
    PX = [pb[3][:, p * 128:(p + 1) * 128] for p in range(2)]
    PRh = pb[3][:, 256:512]
    PRP = [pb[4][:, p * 256:(p + 1) * 256] for p in range(2)]
    PY = [pb[5], pb[6]]
    PM = pb[7][:, 0:128]
    nctx = L // 128
    for d in range(2):
        order = list(range(NS)) if d == 0 else (list(range(nctx - 1, -1, -1)) + list(range(NS - 1, nctx - 1, -1)))
        mi = 0
        P.memset("gpsimd", Mst[0], 0.0)
        P.memset("gpsimd", Mst[1], 0.0)
        for s in order:
            Mcur, Mnew = Mst[mi % 2], Mst[(mi + 1) % 2]
            mi += 1
            tmb, vb = tmbr.next(), vbr.next()
            P.dma("sync", tmb, K.tm_d[d][s * 128:(s + 1) * 128, :, :])
            P.dma("sync", vb, K.v_d[s * 128:(s + 1) * 128, :])
            for g in range(8):
                fmb = fmbr.next()
                P.dma("sync", fmb, K.fm_d[d][:, g, s, :, :])
                sab, p0, x = sabr.next(), p0r.next(), xr.next()
                if R2CUT < 1:
                    continue
                for p in range(2):
                    bs = slice(64 * p, 64 * p + 64)
                    ar = fmb.re("p a t -> p (a t)")[bs, 256:512]
                    P.mm(PAB[p][:, 0:256], fmb[bs, 0, :], ar)
                    P.mm(PAB[p][:, 256:512], fmb[bs, 1, :], ar)

                if R2CUT < 1.2:
                    continue
                for p in range(2):
                    hc = slice(64 * (2 * g + p), 64 * (2 * g + p) + 64)
                    P.tt("vector", sab[:, p, :], PAB[p], m512[d].re("p r c -> p (r c)"), ALU.mult)
                    P.tr(PC[p], sab[:, p, 0:128], cst[:, CST_IDENT:CST_IDENT + 128])
                    P.copy("vector", p0[:, p, :], PC[p])
                    if R2CUT < 1.4:
                        continue
                    P.copy("gpsimd", x[:, p, 0:64], tmb[:, 0, hc])
                    if R2CUT < 1.6:
                        continue
                    P.mm(PXi[p], sab[:, p, 256:384], vb[:, hc].cast(F32))
                    P.copy("scalar", x[:, p, 64:128], PXi[p])
                Rk = [sab[:, p, 0:128] for p in range(2)]
                Pk = [p0[:, p, :] for p in range(2)]
                if R2CUT < 2:
                    continue
                for k in range(7):
                    xn = xr.next()
                    rp = rpr.next() if k < 6 else None
                    for p in range(2):
                        P.mm(PX[p], Rk[p], x[:, p, :])
                        if k < 6:
                            P.mm(PRP[p][:, 0:128], Pk[p], Rk[p])
                            if k < 5:
                                P.mm(PRP[p][:, 128:256], Rk[p], Pk[p])
                    for p in range(2):
                        P.tt("vector", xn[:, p, :], PX[p], x[:, p, :], ALU.add)
                        if k < 5:
                            P.copy("scalar", rp[:, p, :], PRP[p])
                        elif k == 5:
                            P.copy("scalar", rp[:, p, 0:128], PRP[p][:, 0:128])
                    x = xn
                    if k < 6:
                        Rk = [rp[:, p, 0:128] for p in range(2)]
                        Pk = [rp[:, p, 128:256] for p in range(2)]
                if R2CUT < 3:
                    continue
                ig, rh = igr.next(), rhr.next()
                gc = slice(g * 128, (g + 1) * 128)
                ah = ahr.next()
                for p in range(2):
                    P.copy("gpsimd", ah[:, 64 * p:64 * p + 64], x[:, p, 0:64])
                P.mm(PG, ah, tmb[:, 1, gc].cast(F32))
                for p in range(2):
                    P.mm(PRh[:, 128 * p:128 * p + 128], ah, sab[:, p, 128:256])
                for p in range(2):
                    bs = slice(64 * p, 64 * p + 64)
                    P.tt("vector", ig[bs, 64 * p:64 * p + 64], PG[bs, 64 * p:64 * p + 64], i64[bs, :], ALU.add)
                    P.tt("vector", rh[bs, 128 * p:128 * p + 128], PRh[bs, 128 * p:128 * p + 128], fmb[bs, 3, :], ALU.add)
                if R2CUT < 4:
                    continue
                for p in range(2):
                    h = 2 * g + p
                    hc = slice(64 * h, 64 * h + 64)
                    py = PY[h // 8][:, (h % 8) * 64:(h % 8) * 64 + 64]
                    P.mm(py, sab[:, p, 128:256], x[:, p, 64:128], start=True, stop=False)
                    P.mm(py, sab[:, p, 384:512], vb[:, hc].cast(F32), start=False, stop=False)
                    P.mm(py, rh[:, 128 * p:128 * p + 128], Mcur[:, g, p, :], start=False, stop=True)
                if R2CUT < 5:
                    continue
                P.mm(PM[:, 0:64], tmb[:, 1, gc].cast(F32), x[:, 0, 64:128], start=True, stop=False, skip=True)
                P.mm(PM[:, 64:128], tmb[:, 1, gc].cast(F32), x[:, 1, 64:128], start=False, stop=False, skip=True)
                P.mm(PM, tmb[:, 2, gc], vb[:, gc], start=False, stop=False, skip=True)
                P.mm(PM, ig, Mcur.re("p g a i -> p g (a i)")[:, g, :], start=False, stop=True, skip=True)
                for p in range(2):
                    bs = slice(64 * p, 64 * p + 64)
                    P.act(Mnew[bs, g, p, :], PM[bs, 64 * p:64 * p + 64], AF.Identity, scale=K.gam[bs, d, g, s:s + 1])
            if R2CUT < 4:
                continue
            ys = ysr.next()
            P.copy("scalar", ys[:, 0:512], PY[0])
            P.copy("vector", ys[:, 512:1024], PY[1])
            P.dma("sync", K.y_d[d][s * 128:(s + 1) * 128, :], ys)


def bcast_row(K, name, src_v):
    n = src_v.ap.shape[-1]
    t = K.A.alloc(name, [128, n])
    K.P.dma("gpsimd", t, V(src_v.buf, src_v.ap.partition_broadcast(128)))
    return t


def stage_r3(K):
    P, A, L, T = K.P, K.A, K.L, K.T
    Wo = A.alloc("wo1", [128, 8, D], BF16)
    for f in range(8):
        P.dma("gpsimd", Wo[:, f, :], K.wout1_d[f * 128:(f + 1) * 128, :])
    lng = bcast_row(K, "lng", K.rows_d[0:1, :])
    lnb = bcast_row(K, "lnb", K.rows_d[1:2, :])
    gx = [bcast_row(K, "gres%d" % w, K.modrow_d[0, w:w + 1, 2 * D:3 * D]) for w in range(2)]
    y0r, y1r, vr, xr_ = (A.ring(n, 2, [128, D]) for n in ("y0", "y1", "vv", "xres"))
    sgr = A.ring("sgl", 2, [128, D], BF16)
    bnr = A.ring("bnl", 2, [128, 16])
    yfr, sqr, t1r = A.ring("yf", 2, [128, D]), A.ring("ysq", 2, [128, D]), A.ring("t1", 2, [128, D])
    obr = A.ring("ob", 2, [128, D], BF16)
    otr = A.ring("oT", 2, [128, 8, 128], BF16)
    str_ = A.ring("stat", 2, [128, 4, 16])
    outr = A.ring("xo", 2, [128, D])
    for (w, i, r0, tok, col) in seq_tiles(K):
        y0, y1, vv, xres, sg, bn = y0r.next(), y1r.next(), vr.next(), xr_.next(), sgr.next(), bnr.next()
        ts_ = slice(tok, tok + 128)
        P.dma("sync", y0, K.y_d[0][ts_, :])
        P.dma("sync", y1, K.y_d[1][ts_, :])
        P.dma("sync", vv, V(K.v_d, K.v_d.ap[ts_, :].bitcast(F32)))
        P.dma("sync", sg, K.sg_d[ts_, :])
        P.dma("sync", bn, K.bonus_d[ts_, :])
        P.dma("sync", xres, (K.ctx_d if w else K.x_d)[r0:r0 + 128, :])
        yf, sq, t1, st = yfr.next(), sqr.next(), t1r.next(), str_.next()
        P.tt("gpsimd", yf, y0, y1, ALU.add)
        P.act(sq, yf, AF.Square)
        h3 = lambda b: b.re("p (h k) -> p h k", k=64)
        P._c("vector", "tensor_reduce", ["out"], dict(out=st[:, 0, :], in_=h3(yf), axis=AX.X, op=ALU.add))
        P._c("vector", "tensor_reduce", ["out"], dict(out=st[:, 1, :], in_=h3(sq), axis=AX.X, op=ALU.add))
        P.ts("vector", st[:, 0, :], st[:, 0, :], 1.0 / 64, ALU.mult)
        P.tt("vector", st[:, 2, :], st[:, 0, :], st[:, 0, :], ALU.mult)
        P.stt("vector", st[:, 1, :], st[:, 1, :], 1.0 / 64, st[:, 2, :], ALU.mult, ALU.subtract)
        P.act(st[:, 3, :], st[:, 1, :], AF.Ln, bias=64e-5)
        P.act(st[:, 3, :], st[:, 3, :], AF.Exp, scale=-0.5)
        bc = lambda v_: v_.bc(2, [128, 16, 64])
        P.tt("vector", h3(t1), h3(yf), bc(st[:, 0, :]), ALU.subtract)
        P.tt("gpsimd", h3(t1), h3(t1), bc(st[:, 3, :]), ALU.mult)
        P.tt("vector", t1, t1, lng, ALU.mult)
        P.tt("gpsimd", t1, t1, lnb, ALU.add)
        P.tt("vector", h3(vv), h3(vv), bc(bn.v), ALU.mult)
        P.tt("gpsimd", t1, t1, vv, ALU.add)
        ob = obr.next()
        P.tt("vector", ob, t1, sg, ALU.mult)
        for f in range(8):
            P.tr(K.pbh[:, f * 128:(f + 1) * 128], ob[:, f * 128:(f + 1) * 128], K.identb)
        oT = otr.next()
        P.copy("scalar", oT.re("p f t -> p (f t)"), K.pbh)
        for ch in range(2):
            ps = K.pb[ch]
            for f in range(8):
                P.mm(ps, oT[:, f, :], Wo[:, f, ch * 512:(ch + 1) * 512], start=(f == 0), stop=(f == 7))
        xo = outr.next()
        for ch in range(2):
            cs = slice(ch * 512, (ch + 1) * 512)
            P.tt("vector", xo[:, cs], K.pb[ch], gx[w][:, cs], ALU.mult)
        P.tt("gpsimd", xo, xo, xres, ALU.add)
        P.dma("sync", (K.ctx1_d if w else K.x1_d)[r0:r0 + 128, :], xo)


def stage_mla(K, do_attn=True, do_final=True):
    return stage_mla_impl(K, do_attn, do_final)


def host_inputs(inp, b, T, L):
    f32 = lambda a: np.ascontiguousarray(np.asarray(a, np.float32))
    pv = np.zeros((128, PV_N), np.float32)
    pv[:, PV_NG0:PV_NG0 + 8] = fm(inp["norm_g"][0])
    pv[:, PV_NG1:PV_NG1 + 8] = fm(inp["norm_g"][1])
    for j in range(6):
        pv[:, PV_MU + 8 * j:PV_MU + 8 * j + 8] = fm(inp["rwkv_mu"][0, j])
    for d in range(2):
        pv[:, PV_W0 + 8 * d:PV_W0 + 8 * d + 8] = fm(inp["rwkv_w0"][0, d])
        pv[:, PV_A0 + 8 * d:PV_A0 + 8 * d + 8] = fm(inp["rwkv_a0"][0, d])
    pv[:, PV_KK:PV_KK + 8] = fm(inp["rwkv_k_k"][0])
    pv[:, PV_KA:PV_KA + 8] = fm(inp["rwkv_k_a"][0])
    pv[:, PV_RK:PV_RK + 8] = fm(np.asarray(inp["rwkv_r_k"][0]).reshape(-1))
    pv[:, PV_QG:PV_QG + 3] = fm(inp["mla_q_norm_g"][0])
    pv[:, PV_KVG:PV_KVG + 2] = fm(inp["mla_kv_norm_g"][0])
    cvec = np.concatenate([fm(inp["c"][b]), fm(inp["c_ctx"])], axis=1)
    w_in2 = np.asarray(inp["mla_w_in"][0], np.float32)
    w_in2 = np.concatenate([w_in2, w_in2[:, 656:672], w_in2[:, 640:656]], axis=1)
    wqb = np.asarray(inp["mla_w_qb"][0], np.float32).reshape(384, 16, 96)
    wqb = np.concatenate([wqb, wqb[:, :, 80:96], wqb[:, :, 64:80]], axis=2).reshape(384, 16 * 128)
    return {
        "x": f32(inp["x"][b][:T]), "ctx": f32(inp["ctx"][b][:L]), "cvec": f32(cvec), "pvec": pv,
        "cst": make_consts(),
        "rows": f32(np.stack([inp["rwkv_lnx_g"][0], inp["rwkv_lnx_b"][0], inp["final_g"]])),
        "mod_w": f32(inp["mod_w"]), "mod_b": f32(inp["mod_b"]),
        "rwkv_w_in": f32(inp["rwkv_w_in"][0]), "rwkv_w1": f32(inp["rwkv_w1"][0]),
        "rwkv_w2": f32(np.asarray(inp["rwkv_w2"][0]).reshape(128, D)),
        "rwkv_a1": f32(inp["rwkv_a1"][0]), "rwkv_a2": f32(np.asarray(inp["rwkv_a2"][0]).reshape(128, D)),
        "rwkv_w_out": f32(inp["rwkv_w_out"][0]),
        "mla_w_in": f32(w_in2), "mla_w_qb": f32(wqb), "mla_w_kvb": f32(inp["mla_w_kvb"][0]),
        "mla_w_out": f32(inp["mla_w_out"][0]), "rope": rope_tables(T),
    }


_CACHE = {}


def kernel(**inputs):
    B, T, _ = inputs["x"].shape
    L = inputs["ctx"].shape[1]
    key = (T, L)
    if key not in _CACHE:
        _CACHE[key] = build(T, L)[0]
    nc = _CACHE[key]
    in_maps = [host_inputs(inputs, b, T, L) for b in range(B)]
    res = run_bass_kernel_spmd(nc, in_maps, core_ids=list(range(B)))
    return np.stack([np.asarray(r["out"], np.float32) for r in res.results], axis=0)


def stage_mla_impl(K, do_attn=True, do_final=True):
    P, A, L, T, NT, NS = K.P, K.A, K.L, K.T, K.NT, K.NS
    pv, cst, pb = K.pv, K.cst, K.pb
    SCL = 1.0 / math.sqrt(96.0)
    qnT = A.alloc("qnT", [128, 3, T], BF16)
    kvnT = A.alloc("kvnT", [128, 2, NT], BF16)
    KT = A.alloc("KT", [128, NT], BF16)
    ones_r = A.alloc("ones_r", [128, 128], F32R)
    ones_f = A.alloc("ones_f", [128, 64])
    P.memset("vector", ones_r, 1.0)
    P.memset("vector", ones_f, 1.0)
    mark = A.off
    Win = A.alloc("mwin", [128, 8, 1728], BF16)
    for f in range(8):
        P.dma("gpsimd", Win[:, f, :], K.mw_in_d[f * 128:(f + 1) * 128, :])
    hbr = A.ring("mhb", 1, [128, 8, 512], BF16)
    qc = A.alloc("qc", [128, 5, 512])
    sqr = A.ring("msq", 1, [128, 512], F32R)
    rsr = A.ring("mrs", 1, [128, 512])
    rpr = A.ring("mrp", 1, [128, 2, 512])
    t1r = A.ring("mt1", 1, [128, 512])
    t2r = A.ring("mt2", 1, [128, 512])
    sgo = A.ring("msg", 1, [128, 8, 512], BF16)
    blocks = [(1, 0, K.CO, L, 0)] + [(0, L + i * 512, K.XO + i * 512, 512, i * 512) for i in range(T // 512)]
    for (wh, t0, c0, nb, xt0) in blocks:
        hb = hbr.next()
        P.dma("sync", hb[:, :, 0:nb], K.hT_d[:, :, c0:c0 + nb])
        tiles = ([] if wh else [0, 1, 2]) + [3, 4]
        for m in tiles:
            for f in range(8):
                P.mm(pb[m][:, 0:nb], Win[:, f, m * 128:(m + 1) * 128], hb[:, f, 0:nb], start=(f == 0), stop=(f == 7))
            P.copy("scalar", qc[:, m, 0:nb], pb[m][:, 0:nb])
        for (ms, nfeat, dst, gcol) in (([] if wh else [0, 1, 2], 384.0, qnT, PV_QG), ([3, 4], 256.0, kvnT, PV_KVG)):
            if not ms:
                continue
            for i, m in enumerate(ms):
                sq = sqr.next()
                P.tt("gpsimd", sq[:, 0:nb], qc[:, m, 0:nb], qc[:, m, 0:nb], ALU.mult)
                P.mm(pb[5][:, 0:nb], ones_r, sq[:, 0:nb], start=(i == 0), stop=(i == len(ms) - 1))
            rs = rsr.next()
            P.act(rs[:, 0:nb], pb[5][:, 0:nb], AF.Ln, bias=1e-6, scale=1.0 / nfeat)
            P.act(rs[:, 0:nb], rs[:, 0:nb], AF.Exp, scale=-0.5)
            for i, m in enumerate(ms):
                o_ = dst[:, i, xt0:xt0 + nb] if dst is qnT else dst[:, i, t0:t0 + nb]
                P.stt("vector", o_, qc[:, m, 0:nb], pv[:, gcol + i:gcol + i + 1], rs[:, 0:nb], ALU.mult, ALU.mult)
        pe, sw = pb[6][64:96, 0:nb], pb[7][64:96, 0:nb]
        for f in range(8):
            P.mm(pe, Win[:, f, 640:672], hb[:, f, 0:nb], start=(f == 0), stop=(f == 7))
        if wh:
            P.copy("scalar", KT[64:96, t0:t0 + nb], pe)
        else:
            for f in range(8):
                P.mm(sw, Win[:, f, 1696:1728], hb[:, f, 0:nb], start=(f == 0), stop=(f == 7))
            rp = rpr.next()
            P.dma("sync", rp[64:96, :, :], K.rope_d[:, :, xt0:xt0 + nb])
            t1, t2 = t1r.next(), t2r.next()
            P.tt("vector", t1[64:96, :], pe, rp[64:96, 0, :], ALU.mult)
            P.tt("vector", t2[64:96, :], sw, rp[64:96, 1, :], ALU.mult)
            P.tt("gpsimd", KT[64:96, t0:t0 + nb], t1[64:96, :], t2[64:96, :], ALU.add)
            so = sgo.next()
            for g in range(8):
                ps = pb[g % 4]
                for f in range(8):
                    P.mm(ps, Win[:, f, 672 + g * 128:672 + (g + 1) * 128], hb[:, f, :], start=(f == 0), stop=(f == 7))
                P.act(so[:, g, :], ps, AF.Silu)
            P.dma("sync", K.sgT_d[:, :, xt0:xt0 + 512], so)
    barrier(P, K.tiny)
    A.off = mark
    if not do_attn:
        return
    QT = A.alloc("QT", [128, T], BF16)
    Vh = A.alloc("Vh", [128, NS, 65], BF16)
    P.memset("vector", Vh, 1.0)
    wqr = A.ring("wq", 2, [128, 3, 128], BF16)
    wkr = A.ring("wk", 2, [128, 2, 128], BF16)
    ptr = A.ring("pt", 3, [128, 512], BF16)
    rpr = A.ring("arp", 2, [128, 2, 512])
    t1r = A.ring("at1", 2, [128, 512])
    t2r = A.ring("at2", 2, [128, 512])
    recr = A.ring("rec", 2, [128, 512])
    bcr = A.ring("bcs", 2, [64, 512])
    otr = A.ring("oth", 2, [64, 512], BF16)
    kblocks = [(k0, min(512, NT - k0)) for k0 in range(0, NT, 512)]
    for h in range(NH):
        wq, wk = wqr.next(), wkr.next()
        P.dma("gpsimd", wq, K.wqb_d.re("(k p) n -> p k n", p=128)[:, :, h * 128:(h + 1) * 128])
        P.dma("gpsimd", wk, K.wkvb_d.re("(k p) n -> p k n", p=128)[:, :, h * 128:(h + 1) * 128])
        for (k0, nk) in kblocks:
            for kt in range(2):
                P.mm(pb[0][0:64, 0:nk], wk[:, kt, 0:64], kvnT[:, kt, k0:k0 + nk], start=(kt == 0), stop=(kt == 1))
            P.copy("scalar", KT[0:64, k0:k0 + nk], pb[0][0:64, 0:nk])
        for j0 in range(0, NS, 8):
            nj = min(8, NS - j0)
            for j in range(nj):
                for kt in range(2):
                    P.mm(pb[1][:, j * 64:(j + 1) * 64], kvnT[:, kt, (j0 + j) * 128:(j0 + j + 1) * 128], wk[:, kt, 64:128],
                         start=(kt == 0), stop=(kt == 1))
            P.copy("vector", Vh[:, j0:j0 + nj, 0:64], pb[1][:, 0:nj * 64].re("p (j c) -> p j c", c=64))
        for qb in range(T // 512):
            qs = slice(qb * 512, (qb + 1) * 512)
            for kt in range(3):
                P.mm(pb[2][0:96, :], wq[:, kt, 0:96], qnT[:, kt, qs], start=(kt == 0), stop=(kt == 2))
            for kt in range(3):
                P.mm(pb[3][64:96, :], wq[:, kt, 96:128], qnT[:, kt, qs], start=(kt == 0), stop=(kt == 2))
            rp = rpr.next()
            P.dma("sync", rp[64:96, :, :], K.rope_d[:, :, qs])
            t1, t2 = t1r.next(), t2r.next()
            P.copy("scalar", QT[0:64, qs], pb[2][0:64, :])
            P.tt("vector", t1[64:96, :], pb[2][64:96, :], rp[64:96, 0, :], ALU.mult)
            P.tt("vector", t2[64:96, :], pb[3][64:96, :], rp[64:96, 1, :], ALU.mult)
            P.tt("gpsimd", QT[64:96, qs], t1[64:96, :], t2[64:96, :], ALU.add)
        for qb in range(T // 512):
            qs = slice(qb * 512, (qb + 1) * 512)
            po = pb[4 + qb % 2]
            for kt in range(NS):
                ps = pb[kt % 4]
                P.mm(ps, KT[0:96, kt * 128:(kt + 1) * 128], QT[0:96, qs])
                pt = ptr.next()
                P.act(pt, ps, AF.Exp, scale=SCL)
                P.mm(po[0:65, :], Vh[:, kt, :], pt, start=(kt == 0), stop=(kt == NS - 1))
            rec, bcs, ot = recr.next(), bcr.next(), otr.next()
            P.recip(rec[64:65, :], po[64:65, :])
            P.mm(pb[6][0:64, :], ones_f[64:65, :], rec[64:65, :])
            P.copy("scalar", bcs, pb[6][0:64, :])
            P.tt("vector", ot, po[0:64, :], bcs, ALU.mult)
            P.dma("sync", K.oT_d[64 * (h % 2):64 * (h % 2) + 64, h // 2, qs], ot)
    barrier(P, K.tiny)
    A.off = A.base
    if not do_final:
        return
    Wo = A.alloc("wo2", [128, 8, D], BF16)
    for f in range(8):
        P.dma("gpsimd", Wo[:, f, :], K.wout2_d[f * 128:(f + 1) * 128, :])
    gx2 = bcast_row(K, "gx2", K.modrow_d[1, 0:1, 2 * D:3 * D])
    fing = bcast_row(K, "fing", K.rows_d[2:3, :])
    obr = A.ring("fo", 2, [128, 8, 512], BF16)
    sbr = A.ring("fs", 2, [128, 8, 512], BF16)
    ogr = A.ring("fg", 2, [128, 8, 512], BF16)
    x1r = A.ring("fx", 2, [128, D])
    x2r = A.ring("fx2", 2, [128, D])
    jr = A.ring("fj", 1, [128, D], BF16)
    ssr = A.ring("fss", 4, [128, 2])
    for qb in range(T // 512):
        qs = slice(qb * 512, (qb + 1) * 512)
        ob, sb, og = obr.next(), sbr.next(), ogr.next()
        P.dma("sync", ob, K.oT_d[:, :, qs])
        P.dma("sync", sb, K.sgT_d[:, :, qs])
        P.tt("gpsimd", og, ob, sb, ALU.mult)
        for tt in range(4):
            r0 = qb * 512 + tt * 128
            x1 = x1r.next()
            P.dma("sync", x1, K.x1_d[r0:r0 + 128, :])
            for ch in range(2):
                for f in range(8):
                    P.mm(pb[ch], og[:, f, tt * 128:(tt + 1) * 128], Wo[:, f, ch * 512:(ch + 1) * 512], start=(f == 0), stop=(f == 7))
            x2 = x2r.next()
            for ch in range(2):
                cs = slice(ch * 512, (ch + 1) * 512)
                P.tt("vector", x2[:, cs], pb[ch], gx2[:, cs], ALU.mult)
            P.tt("gpsimd", x2, x2, x1, ALU.add)
            ss = ssr.next()
            P.act(jr.next(), x2, AF.Square, accum_out=ss[:, 0:1])
            P.act(ss[:, 1:2], ss[:, 0:1], AF.Ln, bias=1e-6, scale=1.0 / D)
            P.act(ss[:, 1:2], ss[:, 1:2], AF.Exp, scale=-0.5)
            P.stt("vector", x2, x2, ss[:, 1:2], fing, ALU.mult, ALU.mult)
            P.dma("sync", K.out_d[r0:r0 + 128, :], x2)
```

```python
import contextlib
import math
import numpy as np
import concourse.bass as bass
import concourse.mybir as mybir
from concourse.bass_utils import run_bass_kernel_spmd

F32 = mybir.dt.float32
F32R = mybir.dt.float32
BF16 = mybir.dt.bfloat16
ALU = mybir.AluOpType
AF = mybir.ActivationFunctionType
AX = mybir.AxisListType

N_DMA_SEMS = 8
DEBUG_SRC = False
R2CUT = 99
ATT_LOOK = 2
R2_SPLIT = False
R2VAR = 0
NO_SELF_SYNC = ("tensor",)

D = 1024
NH = 16
C0 = math.exp(-0.5)


class V:
    __slots__ = ("buf", "ap")

    def __init__(self, buf, ap):
        self.buf = buf
        self.ap = ap

    def __getitem__(self, k):
        return V(self.buf, self.ap[k])

    def re(self, s, **kw):
        return V(self.buf, self.ap.rearrange(s, **kw))

    def bc(self, axis, shape):
        return V(self.buf, self.ap.unsqueeze(axis).to_broadcast(list(shape)))

    def cast(self, dt):
        return V(self.buf, self.ap.bitcast(dt))


class Buf:
    __slots__ = ("name", "w", "r", "ap")

    def __init__(self, name, ap):
        self.name = name
        self.w = None
        self.r = []
        self.ap = ap

    def __getitem__(self, k):
        return V(self, self.ap[k])

    @property
    def v(self):
        return V(self, self.ap)

    def re(self, s, **kw):
        return V(self, self.ap.rearrange(s, **kw))


def _v(x):
    return x.v if isinstance(x, Buf) else x


class Op:
    __slots__ = ("eng", "fn", "deps", "sig", "cnt", "dma_sem", "dma_val", "dma_prev", "src")

    def __init__(self, eng, fn):
        self.src = None
        self.eng = eng
        self.fn = fn
        self.deps = []
        self.sig = False
        self.cnt = 0
        self.dma_sem = None
        self.dma_val = 0
        self.dma_prev = None


class Eng:
    def __init__(self, name):
        self.name = name
        self.ops = []
        self.sem = None
        self.dma_sems = []
        self.dma_uses = [0] * N_DMA_SEMS
        self.dma_last = [None] * N_DMA_SEMS
        self.n_dma = 0


class Prog:
    def __init__(self, nc):
        self.nc = nc
        self.engs = {n: Eng(n) for n in ("tensor", "vector", "scalar", "gpsimd", "sync")}
        self.stack = contextlib.ExitStack()
        self.n_ops = 0
        self._rr = {}

    def sbuf(self, name, shape, dtype=F32):
        t = self.stack.enter_context(self.nc.sbuf_tensor(name, list(shape), dtype))
        return Buf(name, t[:])

    def psum(self, name, shape, dtype=F32):
        t = self.stack.enter_context(self.nc.psum_tensor(name, list(shape), dtype))
        return Buf(name, t[:])

    def dram(self, name, shape, dtype=F32, kind="Internal"):
        t = self.nc.dram_tensor(name, list(shape), dtype, kind=kind)
        return Buf(name, t.ap())

    def sub(self, buf, name, key):
        return Buf(name, buf.ap[key])

    def ring(self, name, n, shape, dtype=F32, space="sbuf"):
        mk = self.sbuf if space == "sbuf" else self.psum
        return Ring([mk("%s%d" % (name, i), shape, dtype) for i in range(n)])

    def op(self, eng, fn, reads=(), writes=()):
        e = self.engs[eng]
        o = Op(e, fn)
        if DEBUG_SRC:
            import sys as _s
            fr = _s._getframe(1)
            while fr.f_code.co_name in ("op", "_c", "mm", "tr", "act", "tt", "ts", "stt", "copy", "recip", "memset", "dma"):
                fr = fr.f_back
            o.src = fr.f_lineno
        deps = {}
        for b in reads:
            if b.w is not None:
                deps[id(b.w)] = b.w
        for b in writes:
            if b.w is not None:
                deps[id(b.w)] = b.w
            for r in b.r:
                deps[id(r)] = r
        for d in deps.values():
            if d.dma_sem is None and d.eng is e and e.name in NO_SELF_SYNC:
                continue
            o.deps.append(d)
            if d.dma_sem is None:
                d.sig = True
        for b in reads:
            b.r.append(o)
        for b in writes:
            b.w = o
            b.r = []
        e.ops.append(o)
        self.n_ops += 1
        return o

    def dma(self, eng, out, in_, **kw):
        out, in_ = _v(out), _v(in_)
        e = self.engs[eng]
        s = e.n_dma % N_DMA_SEMS
        e.n_dma += 1
        o = self.op(eng, ("dma", out.ap, in_.ap, kw), [in_.buf], [out.buf])
        o.dma_sem = s
        e.dma_uses[s] += 1
        o.dma_val = 16 * e.dma_uses[s]
        o.dma_prev = e.dma_last[s]
        e.dma_last[s] = o
        return o

    def _c(self, eng, method, outs, kw, extra_reads=()):
        reads, writes, args = list(extra_reads), [], {}
        for k, a in kw.items():
            if isinstance(a, (V, Buf)):
                a = _v(a)
                (writes if k in outs else reads).append(a.buf)
                args[k] = a.ap
            else:
                args[k] = a
        return self.op(eng, lambda q: getattr(q, method)(**args), reads, writes)

    def mm(self, out, lhsT, rhs, start=True, stop=True, skip=False):
        out, lhsT, rhs = _v(out), _v(lhsT), _v(rhs)
        oa, la, ra = out.ap, lhsT.ap, rhs.ap
        assert len(ra.shape) == 2 and len(la.shape) == 2 and len(oa.shape) == 2, (oa.shape, la.shape, ra.shape)
        return self.op("tensor", lambda q: q.matmul(oa, lhsT=la, rhs=ra, start=start, stop=stop,
                                                    skip_group_check=skip),
                       [lhsT.buf, rhs.buf], [out.buf])

    def tr(self, out, in_, ident):
        out, in_, ident = _v(out), _v(in_), _v(ident)
        oa, ia, da = out.ap, in_.ap, ident.ap
        return self.op("tensor", lambda q: q.transpose(out=oa, in_=ia, identity=da),
                       [in_.buf, ident.buf], [out.buf])

    def act(self, out, in_, func, bias=0.0, scale=1.0, accum_out=None, eng="scalar"):
        kw = dict(out=out, in_=in_, func=func, bias=bias, scale=scale)
        outs = ["out"]
        if accum_out is not None:
            kw["accum_out"] = accum_out
            outs.append("accum_out")
        return self._c("scalar", "activation", outs, kw)

    def tt(self, eng, out, in0, in1, op):
        return self._c(eng, "tensor_tensor", ["out"], dict(out=out, in0=in0, in1=in1, op=op))

    def ts(self, eng, out, in0, s1, op0, s2=None, op1=None):
        kw = dict(out=out, in0=in0, scalar1=s1, scalar2=s2, op0=op0)
        if op1 is not None:
            kw["op1"] = op1
        return self._c(eng, "tensor_scalar", ["out"], kw)

    def stt(self, eng, out, in0, scalar, in1, op0, op1):
        return self._c(eng, "scalar_tensor_tensor", ["out"],
                       dict(out=out, in0=in0, scalar=scalar, in1=in1, op0=op0, op1=op1))

    def copy(self, eng, out, in_):
        if eng == "scalar":
            return self._c("scalar", "copy", ["out"], dict(out=out, in_=in_))
        return self._c(eng, "tensor_copy", ["out"], dict(out=out, in_=in_))

    def recip(self, out, in_):
        return self._c("vector", "reciprocal", ["out"], dict(out=out, in_=in_))

    def memset(self, eng, out, val):
        out = _v(out)
        oa = out.ap
        return self.op(eng, lambda q: q.memset(oa, val), [], [out.buf])

    def rr(self, key, engs):
        i = self._rr.get(key, 0)
        self._rr[key] = i + 1
        return engs[i % len(engs)]

    def finish(self):
        nc = self.nc
        st = self.stack
        for e in self.engs.values():
            e.sem = st.enter_context(nc.semaphore("s_" + e.name))
            if e.n_dma:
                e.dma_sems = [st.enter_context(nc.semaphore("d_%s%d" % (e.name, i)))
                              for i in range(N_DMA_SEMS)]
        for e in self.engs.values():
            c = 0
            for o in e.ops:
                if o.dma_sem is None and o.sig:
                    c += 1
                    o.cnt = c
        block = st.enter_context(nc.Block())
        prog = self

        def replay(e, q):
            waited = {}

            def wait(sem, key, val):
                if waited.get(key, 0) < val:
                    q.wait_ge(sem, val)
                    waited[key] = val

            for o in e.ops:
                for d in o.deps:
                    if d.dma_sem is not None:
                        wait(d.eng.dma_sems[d.dma_sem], (d.eng.name, d.dma_sem), d.dma_val)
                    else:
                        wait(d.eng.sem, d.eng.name, d.cnt)
                if o.dma_sem is not None:
                    p = o.dma_prev
                    if p is not None:
                        wait(e.dma_sems[p.dma_sem], (e.name, p.dma_sem), p.dma_val)
                    _, oa, ia, kw = o.fn
                    q.dma_start(out=oa, in_=ia, **kw).then_inc(e.dma_sems[o.dma_sem], 16)
                elif o.fn is not None:
                    ins = o.fn(q)
                    if o.sig:
                        ins.then_inc(e.sem, 1)
            if e.name == "sync":
                for e2 in prog.engs.values():
                    if e2.n_dma:
                        for s in range(N_DMA_SEMS):
                            lo = e2.dma_last[s]
                            if lo is not None:
                                wait(e2.dma_sems[s], (e2.name, s), lo.dma_val)

        engs = self.engs

        @block.tensor
        def _(q):
            replay(engs["tensor"], q)

        @block.vector
        def _(q):
            replay(engs["vector"], q)

        @block.scalar
        def _(q):
            replay(engs["scalar"], q)

        @block.gpsimd
        def _(q):
            replay(engs["gpsimd"], q)

        @block.sync
        def _(q):
            replay(engs["sync"], q)

        st.close()


class Ring:
    def __init__(self, bufs):
        self.bufs = bufs
        self.i = 0

    def next(self):
        b = self.bufs[self.i % len(self.bufs)]
        self.i += 1
        return b


DT_SIZE = {F32: 4, F32R: 4, BF16: 2}


class Arena:
    def __init__(self, P, nbytes):
        self.P = P
        self.n4 = nbytes // 4
        t = P.stack.enter_context(P.nc.sbuf_tensor("arena", [128, self.n4], F32))
        self.ap = t[:]
        self.base = 0
        self.off = 0
        self.k = 0

    def persist(self):
        self.base = self.off

    def reset(self):
        self.off = self.base

    def alloc(self, name, shape, dtype=F32, parts=128):
        free = int(np.prod(shape[1:]))
        nb = free * DT_SIZE[dtype]
        n4 = (nb + 31) // 32 * 8
        assert self.off + n4 <= self.n4, "arena overflow at %s (%d KB)" % (name, (self.off + n4) * 4 // 1024)
        ap = self.ap[0:shape[0], self.off:self.off + n4]
        self.off += n4
        if dtype != F32:
            ap = ap.bitcast(dtype)
        ap = ap[:, 0:free]
        if len(shape) == 3:
            ap = ap.rearrange("p (a b) -> p a b", b=shape[2])
        elif len(shape) == 4:
            ap = ap.rearrange("p (a b c) -> p a b c", b=shape[2], c=shape[3])
        self.k += 1
        return Buf("%s_%d" % (name, self.k), ap)

    def ring(self, name, n, shape, dtype=F32):
        return Ring([self.alloc(name, shape, dtype) for _ in range(n)])


def barrier(P, tiny):
    firsts = [P.memset("vector", tiny[0], 0.0), P.memset("gpsimd", tiny[1], 0.0), P.act(tiny[2], tiny[3], AF.Copy)]
    dmas = []
    for e in P.engs.values():
        for s in range(N_DMA_SEMS):
            if e.dma_last[s] is not None:
                dmas.append(e.dma_last[s])
    for f in firsts:
        f.sig = True
    for name, e in P.engs.items():
        o = Op(e, None)
        o.deps = [f for f in firsts if f.eng is not e] + dmas
        e.ops.append(o)


CST_IDENT, CST_BONES, CST_M01, CST_MT, CST_HSEL, CST_SCAN, CST_N = 0, 128, 256, 768, 1024, 1152, 1408


def make_consts():
    c = np.zeros((128, CST_N), np.float32)
    p = np.arange(128)
    c[:, CST_IDENT:CST_IDENT + 128] = np.eye(128)
    c[:, CST_BONES:CST_BONES + 128] = (p[:, None] // 64 == p[None, :] // 64)
    s, t = p[:, None], p[None, :]
    c[:, CST_M01 + 0:CST_M01 + 128] = (t > s)
    c[:, CST_M01 + 128:CST_M01 + 256] = (t >= s)
    c[:, CST_M01 + 256:CST_M01 + 384] = (t < s)
    c[:, CST_M01 + 384:CST_M01 + 512] = (t <= s)
    c[:, CST_MT:CST_MT + 128] = (p[None, :] < p[:, None])
    c[:, CST_MT + 128:CST_MT + 256] = (p[None, :] > p[:, None])
    for g in range(8):
        for h in range(16):
            c[:, CST_HSEL + g * 16 + h] = (h == 2 * g + p // 64)
    sm = np.ones((128, 256), np.float32)
    sm[:, 0] = 0
    sm[:, 128] = 0
    c[:, CST_SCAN:CST_SCAN + 256] = sm
    return c


def fm(vec):
    v = np.asarray(vec, np.float32).reshape(-1, 128)
    return np.ascontiguousarray(v.T)


PV_NG0, PV_NG1, PV_MU, PV_W0, PV_A0, PV_KK, PV_KA, PV_RK, PV_QG, PV_KVG, PV_N = 0, 8, 16, 64, 80, 96, 104, 112, 120, 123, 128


def rope_tables(T):
    rows = T // 64
    row = np.repeat(np.arange(rows), 64).astype(np.float32)
    col = np.tile(np.arange(64), rows).astype(np.float32)
    inv = (1.0 / (10000.0 ** (np.arange(0, 16, 2, dtype=np.float32) / 16))).astype(np.float32)
    ang = np.concatenate([row[:, None] * inv, col[:, None] * inv], axis=-1).astype(np.float32)
    cos, sin = np.cos(ang).T.astype(np.float32), np.sin(ang).T.astype(np.float32)
    out = np.zeros((32, 2, T), np.float32)
    out[0:16, 0], out[16:32, 0] = cos, cos
    out[0:16, 1], out[16:32, 1] = -sin, sin
    return out


class Ctx:
    pass


def build(T, L, stages=("mod", "r1a", "r1b", "r2", "r3", "m1a", "m1b", "m2", "m3"), dbg=()):
    nc = bass.Bass("TRN2", target_bir_lowering=False)
    nc.dge_precook = False
    P = Prog(nc)
    K = Ctx()
    K.P, K.T, K.L = P, T, L
    NT = L + T
    NS = NT // 128
    K.NT, K.NS = NT, NS
    K.CO, K.XO, K.NTP = 1, L + 3, L + T + 4

    def ext(name, shape, dt=F32):
        return P.dram(name, shape, dt, kind="ExternalInput")

    K.x_d = ext("x", [T, D])
    K.ctx_d = ext("ctx", [L, D])
    K.cvec_d = ext("cvec", [128, 16])
    K.pvec_d = ext("pvec", [128, PV_N])
    K.cst_d = ext("cst", [128, CST_N])
    K.rows_d = ext("rows", [3, D])
    K.mod_w_d = ext("mod_w", [2, D, 3 * D])
    K.mod_b_d = ext("mod_b", [2, 3 * D])
    K.w_in_d = ext("rwkv_w_in", [4, D, D])
    K.w1_d = ext("rwkv_w1", [2, D, 64])
    K.w2_d = ext("rwkv_w2", [128, D])
    K.a1_d = ext("rwkv_a1", [2, D, 64])
    K.a2_d = ext("rwkv_a2", [128, D])
    K.wout1_d = ext("rwkv_w_out", [D, D])
    K.mw_in_d = ext("mla_w_in", [D, 1696 + 32])
    K.wqb_d = ext("mla_w_qb", [384, 16 * 128])
    K.wkvb_d = ext("mla_w_kvb", [256, 2048])
    K.wout2_d = ext("mla_w_out", [D, D])
    K.rope_d = ext("rope", [32, 2, T])
    K.out_d = P.dram("out", [T, D], F32, kind="ExternalOutput")

    okind = "ExternalOutput" if dbg else "Internal"

    def scr(name, shape, dt=F32):
        return P.dram(name, shape, dt, kind=("ExternalOutput" if name in dbg else "Internal"))

    K.modrow_d = scr("modrow", [2, 2, 3 * D])
    K.hT_d = scr("hT", [128, 8, K.NTP], BF16)
    K.fm_d = [scr("fm%d" % d, [128, 8, NS, 4, 128], BF16) for d in range(2)]
    K.tm_d = [scr("tm%d" % d, [NT, 3, D], F32R) for d in range(2)]
    K.v_d = scr("vtm", [NT, D], F32R)
    K.sg_d = scr("sg", [NT, D], BF16)
    K.bonus_d = scr("bonus", [NT, 16])
    K.y_d = [scr("y%d" % d, [NT, D]) for d in range(2)]
    K.x1_d = scr("x1", [T, D])
    K.ctx1_d = scr("ctx1", [L, D])
    K.sgT_d = scr("sgT", [128, 8, T], BF16)
    K.oT_d = scr("oT", [128, 8, T], BF16)

    A = Arena(P, 190 * 1024)
    K.A = A
    K.pb = [P.psum("pb%d" % i, [128, 512], F32) for i in range(8)]
    K.pbh = Buf("pbh", K.pb[7].ap.bitcast(BF16))

    K.cst = A.alloc("cst", [128, CST_N])
    K.pv = A.alloc("pv", [128, PV_N])
    K.identb = A.alloc("identb", [128, 128], BF16)
    K.cst_r = A.alloc("cstr", [128, CST_N], F32R)
    K.tiny = [A.alloc("tiny%d" % i, [128, 8]) for i in range(4)]
    K.modT = [A.alloc("modT%d" % i, [128, 24, 2]) for i in range(2)]
    K.modA = [[A.alloc("modA", [128, 8]) for w in range(2)] for i in range(2)]
    K.modB = [[A.alloc("modB", [128, 8]) for w in range(2)] for i in range(2)]
    K.negw0 = A.alloc("negw0", [128, 16])
    K.nega0 = A.alloc("nega0", [128, 16])
    K.omka = A.alloc("omka", [128, 8])
    K.gam = A.alloc("gam", [128, 2, 8, NS])
    K.zero = A.alloc("zero", [128, 8, 1], BF16)
    A.persist()
    P.dma("sync", K.cst, K.cst_d)
    P.dma("sync", K.pv, K.pvec_d)
    P.copy("vector", K.identb, K.cst[:, CST_IDENT:CST_IDENT + 128])
    P.copy("vector", K.cst_r, K.cst)
    for i in range(4):
        P.memset("vector", K.tiny[i], 0.0)
    P.memset("vector", K.zero, 0.0)
    P.ts("vector", K.negw0, K.pv[:, PV_W0:PV_W0 + 16], -1.0, ALU.mult)
    P.ts("vector", K.nega0, K.pv[:, PV_A0:PV_A0 + 16], -1.0, ALU.mult)
    P.ts("vector", K.omka, K.pv[:, PV_KA:PV_KA + 8], -1.0, ALU.mult, 1.0, ALU.add)

    def stage_end():
        barrier(P, K.tiny)
        A.reset()

    if "mod" in stages:
        stage_mod(K)
        stage_end()
    if "r1a" in stages:
        stage_hT(K, 0, K.x_d, K.ctx_d)
        stage_end()
    if "r1b" in stages:
        stage_r1b(K)
        stage_end()
    if "r2" in stages:
        stage_r2(K)
        stage_end()
    if "r3" in stages:
        stage_r3(K)
        stage_end()
    if "m1a" in stages:
        stage_hT(K, 1, K.x1_d, K.ctx1_d)
        stage_end()
    if "m1b" in stages:
        stage_mla(K, "m2" in stages, "m3" in stages)
    P.finish()
    return nc, P


def stage_mod(K):
    P, A = K.P, K.A
    cT = A.alloc("cT", [128, 16])
    sc = A.alloc("scT", [128, 8, 2])
    P.dma("sync", cT, K.cvec_d)
    sg = A.alloc("sgc", [128, 16])
    P.act(sg, cT, AF.Sigmoid)
    P.tt("vector", sc.re("p f w -> p w f"), cT.re("p (w f) -> p w f", w=2), sg.re("p (w f) -> p w f", w=2), ALU.mult)
    mwr = A.ring("mw", 3, [128, 512])
    mb = A.alloc("mb", [2, 3 * D])
    mrow = A.alloc("mrow", [2, 3 * D])
    for layer in range(2):
        P.dma("gpsimd", mb, V(K.mod_b_d, K.mod_b_d.ap[layer:layer + 1, :].partition_broadcast(2)))
        for cb in range(6):
            ps = K.pb[cb % 2]
            for f in range(8):
                mw = mwr.next()
                P.dma("sync", mw, K.mod_w_d[layer, f * 128:(f + 1) * 128, cb * 512:(cb + 1) * 512])
                P.mm(ps[0:2, :], sc[:, f, :], mw, start=(f == 0), stop=(f == 7))
            P.tt("vector", mrow[:, cb * 512:(cb + 1) * 512], ps[0:2, :], mb[:, cb * 512:(cb + 1) * 512], ALU.add)
        P.dma("sync", K.modrow_d[layer], mrow)
        pt = K.pb[2]
        for c in range(24):
            P.tr(pt[:, 2 * c:2 * c + 2], mrow[:, c * 128:(c + 1) * 128], K.cst[0:2, CST_IDENT:CST_IDENT + 2])
        P.copy("vector", K.modT[layer].re("p c w -> p (c w)"), pt[:, 0:48])
        for w in range(2):
            ng = K.pv[:, PV_NG0 + 8 * layer:PV_NG0 + 8 * layer + 8]
            P.stt("vector", K.modA[layer][w], K.modT[layer][:, 8:16, w], 1.0, ng, ALU.add, ALU.mult)
            P.copy("vector", K.modB[layer][w], K.modT[layer][:, 0:8, w])


def seq_tiles(K):
    out = []
    for i in range(K.L // 128):
        out.append((1, i, i * 128, i * 128, K.CO + i * 128))
    for i in range(K.T // 128):
        out.append((0, i, i * 128, K.L + i * 128, K.XO + i * 128))
    return out


def stage_hT(K, layer, x_d, ctx_d):
    P, A = K.P, K.A
    xr = A.ring("xt", 3, [128, D])
    jr = A.ring("junk", 2, [128, D], BF16)
    xnr = A.ring("xn", 2, [128, D], BF16)
    tmpr = A.ring("tmp", 2, [128, 8, 128])
    hr = A.ring("ht", 2, [128, 8, 128], BF16)
    ssr = A.ring("ss", 4, [128, 2])
    for col in (0, K.L + 1, K.L + 2, K.L + K.T + 3):
        P.dma("gpsimd", K.hT_d[:, :, col:col + 1], K.zero, allow_slow_non_contiguous=True)
    for (w, i, r0, tok, col) in seq_tiles(K):
        src = ctx_d if w else x_d
        xt = xr.next()
        P.dma("sync", xt, src[r0:r0 + 128, :])
        ss = ssr.next()
        P.act(jr.next(), xt, AF.Square, accum_out=ss[:, 0:1])
        P.act(ss[:, 1:2], ss[:, 0:1], AF.Ln, bias=1e-6, scale=1.0 / D)
        P.act(ss[:, 1:2], ss[:, 1:2], AF.Exp, scale=-0.5)
        xn = xnr.next()
        P.ts("gpsimd", xn, xt, ss[:, 1:2], ALU.mult)
        for f in range(8):
            P.tr(K.pbh[:, f * 128:(f + 1) * 128], xn[:, f * 128:(f + 1) * 128], K.identb)
        tmp = tmpr.next()
        ht = hr.next()
        P.tt("vector", tmp, K.pbh.re("p (f t) -> p f t", t=128), K.modA[layer][w].v.bc(2, [128, 8, 128]), ALU.mult)
        P.tt("gpsimd", ht, tmp, K.modB[layer][w].v.bc(2, [128, 8, 128]), ALU.add)
        P.dma("sync", K.hT_d[:, :, col:col + 128], ht)


def load_w_bf16(K, name, src_ap_v, shape):
    t = K.A.alloc(name, shape, BF16)
    K.P.dma("gpsimd", t, src_ap_v)
    return t


def stage_r1b(K):
    P, A, L, T = K.P, K.A, K.L, K.T
    NB = 256
    pv, cst = K.pv, K.cst
    W = [A.alloc("win%d" % j, [128, 8, D], BF16) for j in range(4)]
    for j in range(4):
        for f in range(8):
            P.dma("gpsimd", W[j][:, f, :], K.w_in_d[j, f * 128:(f + 1) * 128, :])
    W1c = A.alloc("w1c", [128, 8, 128], BF16)
    A1c = A.alloc("a1c", [128, 8, 128], BF16)
    for d in range(2):
        P.dma("gpsimd", W1c[:, :, 64 * d:64 * d + 64], K.w1_d.re("d (f p) r -> d p f r", p=128)[d])
        P.dma("gpsimd", A1c[:, :, 64 * d:64 * d + 64], K.a1_d.re("d (f p) r -> d p f r", p=128)[d])
    W2c = load_w_bf16(K, "w2c", K.w2_d, [128, D])
    A2c = load_w_bf16(K, "a2c", K.a2_d, [128, D])
    bones_r = K.cst_r[:, CST_BONES:CST_BONES + 128]
    scanm = cst[:, CST_SCAN:CST_SCAN + 256]
    hbr = A.ring("hb", 1, [128, 8, NB + 2], BF16)
    xx = A.alloc("xx", [128, 8, NB])
    tmpx = A.ring("tmpx", 2, [128, NB])
    lerp = [A.alloc("lerp%d" % j, [128, 8, NB], BF16) for j in range(6)]
    twb = A.alloc("twb", [128, NB], BF16)
    tab = A.alloc("tab", [128, NB], BF16)
    prod = A.alloc("prod", [128, 2, 8, NB], BF16)
    hselb = A.alloc("hselb", [128, 128], BF16)
    P.copy("vector", hselb, cst[:, CST_HSEL:CST_HSEL + 128])
    vtm = A.ring("vtm", 1, [128, D], F32R)
    sgt = A.ring("sgt", 1, [128, D], BF16)
    bon = A.ring("bon", 2, [128, 16])
    w = lambda nm, n=2, dt=F32: A.ring(nm, n, [128, NB], dt)
    r_sb, k_sb, kkf, sq, nmx, rn, kk = w("r_sb"), w("k_sb"), w("kkf", 1), w("sq", 1, F32R), w("nmx", 1), w("rn", 1), w("kk")
    ew, sw, ea, av, cum, epos, eneg, dprev, eprev = (w("ew", 1), w("sw"), w("ea", 1), w("av"), w("cum"), w("epos"),
                                                     w("eneg"), w("dprev", 1), w("eprev"))
    mfac, kd, bv, ktil, btil, atil = w("mfac", 1), w("kd"), w("bv"), w("ktil"), w("btil"), w("atil")
    fmo = A.ring("fmo", 2, [128, 2, 4, 128], BF16)
    tmo = A.ring("tmo", 2, [128, 2, 3, 128], F32R)

    blocks = [(1, 0, 0, K.CO)]
    for i in range(T // NB):
        blocks.append((0, L + i * NB, L + i * NB, K.XO + i * NB))
    blocks[0] = (1, 0, 0, K.CO)
    assert L == NB

    for (wh, t0, _, c0) in blocks:
        s0 = t0 // 128
        hb = hbr.next()
        P.dma("sync", hb, K.hT_d[:, :, c0 - 1:c0 + NB + 1])
        for f in range(8):
            tx = tmpx.next()
            P.tt("gpsimd", tx, hb[:, f, 0:NB], hb[:, f, 2:NB + 2], ALU.add)
            P.ts("gpsimd", tx, tx, 0.5, ALU.mult)
            P.tt("gpsimd", xx[:, f, :], tx, hb[:, f, 1:NB + 1], ALU.subtract)
            for j in range(6):
                if P.rr("lerp", ["vector", "gpsimd", "vector"]) == "vector":
                    P.stt("vector", lerp[j][:, f, :], xx[:, f, :],
                          pv[:, PV_MU + 8 * j + f:PV_MU + 8 * j + f + 1], hb[:, f, 1:NB + 1], ALU.mult, ALU.add)
                else:
                    tl = tmpx.next()
                    P.ts("gpsimd", tl, xx[:, f, :], pv[:, PV_MU + 8 * j + f:PV_MU + 8 * j + f + 1], ALU.mult)
                    P.tt("gpsimd", lerp[j][:, f, :], tl, hb[:, f, 1:NB + 1], ALU.add)
        for tt in range(NB // 128):
            for (j, kind) in ((2, "v"), (3, "g")):
                dst = vtm.next() if kind == "v" else sgt.next()
                for ch in range(2):
                    ps = K.pb[ch]
                    for f in range(8):
                        P.mm(ps, lerp[j][:, f, tt * 128:(tt + 1) * 128], W[j][:, f, ch * 512:(ch + 1) * 512],
                             start=(f == 0), stop=(f == 7))
                    if kind == "v":
                        P.copy("scalar", dst[:, ch * 512:(ch + 1) * 512], ps)
                    else:
                        P.act(dst[:, ch * 512:(ch + 1) * 512], ps, AF.Silu)
                if kind == "v":
                    P.dma("sync", K.v_d[t0 + tt * 128:t0 + (tt + 1) * 128, :], dst)
                else:
                    P.dma("sync", K.sg_d[t0 + tt * 128:t0 + (tt + 1) * 128, :], dst)
        ps = K.pb[2]
        for f in range(8):
            P.mm(ps[:, 0:NB], W1c[:, f, :], lerp[4][:, f, :], start=(f == 0), stop=(f == 7))
        P.act(twb, ps[:, 0:NB], AF.Tanh)
        for f in range(8):
            P.mm(ps[:, NB:2 * NB], A1c[:, f, :], lerp[5][:, f, :], start=(f == 0), stop=(f == 7))
        P.copy("vector", tab, ps[:, NB:2 * NB])
        for g in range(8):
            gs = slice(g * 128, (g + 1) * 128)
            pr = K.pb[3]
            for f in range(8):
                P.mm(pr[:, 0:NB], W[0][:, f, gs], lerp[0][:, f, :], start=(f == 0), stop=(f == 7))
            for f in range(8):
                P.mm(pr[:, NB:2 * NB], W[1][:, f, gs], lerp[1][:, f, :], start=(f == 0), stop=(f == 7))
            r_, k_ = r_sb.next(), k_sb.next()
            P.copy("scalar", r_, pr[:, 0:NB])
            P.copy("scalar", k_, pr[:, NB:2 * NB])
            kf, sq_, nm, rn_, kk_ = kkf.next(), sq.next(), nmx.next(), rn.next(), kk.next()
            P.ts("gpsimd", kf, k_, pv[:, PV_KK + g:PV_KK + g + 1], ALU.mult)
            P.tt("gpsimd", sq_, kf, kf, ALU.mult)
            pn = K.pb[4]
            P.mm(pn[:, 0:NB], bones_r, sq_)
            P.ts("vector", nm, pn[:, 0:NB], 1e-24, ALU.max)
            P.act(rn_, nm, AF.Ln)
            P.act(rn_, rn_, AF.Exp, scale=-0.5)
            P.tt("gpsimd", kk_, kf, rn_, ALU.mult)
            for d in range(2):
                hs = slice(64 * d, 64 * d + 64)
                pl = K.pb[5 + (d % 2)]
                P.mm(pl[:, 0:NB], W2c[hs, gs], twb[hs, :])
                P.mm(pl[:, NB:2 * NB], A2c[hs, gs], tab[hs, :])
                ew_, sw_, ea_, a_ = ew.next(), sw.next(), ea.next(), av.next()
                P.act(ew_, pl[:, 0:NB], AF.Exp, bias=K.negw0[:, 8 * d + g:8 * d + g + 1], scale=-1.0)
                P.act(ea_, pl[:, NB:2 * NB], AF.Exp, bias=K.nega0[:, 8 * d + g:8 * d + g + 1], scale=-1.0)
                P.ts("gpsimd", ew_, ew_, 1.0, ALU.add)
                P.recip(sw_, ew_)
                P.ts("gpsimd", ea_, ea_, 1.0, ALU.add)
                P.recip(a_, ea_)
                cm = cum.next()
                if d == 0:
                    P._c("vector", "tensor_tensor_scan", ["out"],
                         dict(out=cm, data0=scanm, data1=sw_, initial=0.0, op0=ALU.mult, op1=ALU.add))
                else:
                    P._c("vector", "tensor_tensor_scan", ["out"],
                         dict(out=cm[:, NB - 1::-1], data0=scanm, data1=sw_[:, NB - 1::-1], initial=0.0,
                              op0=ALU.mult, op1=ALU.add))
                ep, en, dp, epv = epos.next(), eneg.next(), dprev.next(), eprev.next()
                P.act(ep, cm, AF.Exp, scale=-C0)
                P.act(en, cm, AF.Exp, scale=C0)
                P.tt("gpsimd", dp, cm, sw_, ALU.subtract)
                P.act(epv, dp, AF.Exp, scale=-C0)
                for s in range(NB // 128):
                    cc = s * 128 + (127 if d == 0 else 0)
                    P.copy("gpsimd", K.gam[:, d, g, s0 + s:s0 + s + 1], ep[:, cc:cc + 1])
                mf, kd_, b_ = mfac.next(), kd.next(), bv.next()
                P.ts("gpsimd", mf, a_, pv[:, PV_KA + g:PV_KA + g + 1], ALU.mult, K.omka[:, g:g + 1], ALU.add)
                P.tt("gpsimd", kd_, k_, mf, ALU.mult)
                P.tt("gpsimd", b_, kk_, a_, ALU.mult)
                kt_, bt_, at_ = ktil.next(), btil.next(), atil.next()
                P.tt("vector", kt_, kd_, en, ALU.mult)
                P.tt("vector", bt_, b_, en, ALU.mult)
                P.stt("vector", at_, kk_, -1.0, epv, ALU.mult, ALU.mult)
                tp = tmpx.next()
                P.ts("gpsimd", tp, r_, pv[:, PV_RK + g:PV_RK + g + 1], ALU.mult)
                P.tt("gpsimd", prod[:, d, g, :], tp, kd_, ALU.mult)
                fo = fmo.next()
                fo3 = lambda arr: fo[:, :, arr, :]
                P.copy("gpsimd", fo3(0), bt_.re("p (s t) -> p s t", t=128))
                P.copy("gpsimd", fo3(1), kt_.re("p (s t) -> p s t", t=128))
                P.copy("gpsimd", fo3(2), at_.re("p (s t) -> p s t", t=128))
                P.tt("vector", fo3(3), r_.re("p (s t) -> p s t", t=128), ep.re("p (s t) -> p s t", t=128), ALU.mult)
                P.dma("sync", K.fm_d[d][:, g, s0:s0 + NB // 128, :, :], fo)
                to = tmo.next()
                pt, pq = K.pb[d % 2], K.pb[2 + (d % 2)]
                for tt in range(NB // 128):
                    for ai, src in enumerate((at_, bt_, kt_)):
                        idx = tt * 3 + ai
                        dst = pt[:, idx * 128:(idx + 1) * 128] if idx < 4 else pq[:, (idx - 4) * 128:(idx - 3) * 128]
                        P.tr(dst, src[:, tt * 128:(tt + 1) * 128], cst[:, CST_IDENT:CST_IDENT + 128])
                tof = to.re("p t a f -> p (t a f)")
                P.copy("scalar", tof[:, 0:512], pt)
                P.copy("scalar", tof[:, 512:768], pq[:, 0:256])
                for tt in range(NB // 128):
                    P.dma("sync", K.tm_d[d][t0 + tt * 128:t0 + (tt + 1) * 128, :, gs], to[:, tt, :, :])
        for tt in range(NB // 128):
            pbn = K.pb[4]
            n = 0
            for d in range(2):
                for g in range(8):
                    P.mm(pbn[:, 256 + 16 * tt:256 + 16 * tt + 16], prod[:, d, g, tt * 128:(tt + 1) * 128],
                         hselb[:, 16 * g:16 * g + 16], start=(n == 0), stop=(n == 15))
                    n += 1
            b_t = bon.next()
            P.copy("vector", b_t, pbn[:, 256 + 16 * tt:256 + 16 * tt + 16])
            P.dma("sync", K.bonus_d[t0 + tt * 128:t0 + (tt + 1) * 128, :], b_t)


def stage_r2(K):
    P, A, L, T, NS = K.P, K.A, K.L, K.T, K.NS
    cst = K.cst
    m512 = [A.alloc("m512", [128, 2, 256]) for d in range(2)]
    for d in range(2):
        for r in range(2):
            P.copy("gpsimd", m512[d][:, r, :], cst[:, CST_M01 + 256 * d:CST_M01 + 256 * d + 256])
    i64 = A.alloc("i64", [128, 64])
    for p in range(2):
        P.copy("gpsimd", i64[64 * p:64 * p + 64, :], cst[64 * p:64 * p + 64, CST_IDENT + 64 * p:CST_IDENT + 64 * p + 64])
    tmbr = A.ring("tmb", 2, [128, 3, D], F32R)
    vbr = A.ring("vb", 2, [128, D], F32R)
    fmbr = A.ring("fmb", 3, [128, 4, 128], BF16)
    sabr = A.ring("sab", 2, [128, 2, 512], F32)
    p0r = A.ring("p0", 2, [128, 2, 128], F32)
    xr = A.ring("xp", 4, [128, 2, 128], F32)
    rpr = A.ring("rp", 3, [128, 2, 256], F32)
    igr = A.ring("ig", 2, [128, 128], F32R)
    ahr = A.ring("ah", 2, [128, 128], F32)
    rhr = A.ring("rh", 2, [128, 256], F32R)
    ysr = A.ring("ys", 2, [128, D])
    Mst = [A.alloc("Mst", [128, 8, 2, 64], F32R) for _ in range(2)]
    for b in igr.bufs + rhr.bufs + Mst:
        P.memset("gpsimd", b, 0.0)
    pb = K.pb
    PAB = [pb[p].v for p in range(2)]
    PC = [pb[2][:, p * 128:(p + 1) * 128] for p in range(2)]
    PXi = [pb[2][:, 256 + p * 64:256 + (p + 1) * 64] for p in range(2)]
    PG = pb[2][:, 384:512]
    if R2_SPLIT:
        PX = [pb[3 + p][:, 0:128] for p in range(2)]
        PRP = [pb[3 + p][:, 128:384] for p in range(2)]
        PRh = [pb[3 + p][:, 384:512] for p in range(2)]
    else:
        PX = [pb[3][:, p * 128:(p + 1) * 128] for p in range(2)]
        PRh = [pb[3][:, 256 + 128 * p:384 + 128 * p] for p in range(2)]
        PRP = [pb[4][:, p * 256:(p + 1) * 256] for p in range(2)]
    PY = [pb[5], pb[6]]
    PM = pb[7][:, 0:128]
    nctx = L // 128
    for d in range(2):
        order = list(range(NS)) if d == 0 else (list(range(nctx - 1, -1, -1)) + list(range(NS - 1, nctx - 1, -1)))
        mi = 0
        P.memset("gpsimd", Mst[0], 0.0)
        P.memset("gpsimd", Mst[1], 0.0)
        for s in order:
            Mcur, Mnew = Mst[mi % 2], Mst[(mi + 1) % 2]
            mi += 1
            tmb, vb = tmbr.next(), vbr.next()
            P.dma("sync", tmb, K.tm_d[d][s * 128:(s + 1) * 128, :, :])
            P.dma("sync", vb, K.v_d[s * 128:(s + 1) * 128, :])
            for g in range(8):
                fmb = fmbr.next()
                P.dma("sync", fmb, K.fm_d[d][:, g, s, :, :])
                sab, p0, x = sabr.next(), p0r.next(), xr.next()
                if R2CUT < 1:
                    continue
                for p in range(2):
                    bs = slice(64 * p, 64 * p + 64)
                    ar = fmb.re("p a t -> p (a t)")[bs, 256:512]
                    P.mm(PAB[p][:, 0:256], fmb[bs, 0, :], ar)
                    P.mm(PAB[p][:, 256:512], fmb[bs, 1, :], ar)

                if R2CUT < 1.2:
                    continue
                for p in range(2):
                    hc = slice(64 * (2 * g + p), 64 * (2 * g + p) + 64)
                    P.tt("vector", sab[:, p, :], PAB[p], m512[d].re("p r c -> p (r c)"), ALU.mult)
                    P.tr(PC[p], sab[:, p, 0:128], cst[:, CST_IDENT:CST_IDENT + 128])
                    P.copy("vector", p0[:, p, :], PC[p])
                    if R2CUT < 1.4:
                        continue
                    P.copy("gpsimd", x[:, p, 0:64], tmb[:, 0, hc])
                    if R2CUT < 1.6:
                        continue
                    P.mm(PXi[p], sab[:, p, 256:384], vb[:, hc].cast(F32))
                    P.copy("scalar", x[:, p, 64:128], PXi[p])
                Rk = [sab[:, p, 0:128] for p in range(2)]
                Pk = [p0[:, p, :] for p in range(2)]
                if R2CUT < 2:
                    continue
                for k in range(7):
                    xn = xr.next()
                    rp = rpr.next() if k < 6 else None
                    for p in range(2):
                        P.mm(PX[p], Rk[p], x[:, p, :])
                        if k < 6:
                            P.mm(PRP[p][:, 0:128], Pk[p], Rk[p])
                            if k < 5:
                                P.mm(PRP[p][:, 128:256], Rk[p], Pk[p])
                    for p in range(2):
                        P.tt("vector", xn[:, p, :], PX[p], x[:, p, :], ALU.add)
                        if k < 5:
                            P.copy("scalar", rp[:, p, :], PRP[p])
                        elif k == 5:
                            P.copy("scalar", rp[:, p, 0:128], PRP[p][:, 0:128])
                    x = xn
                    if k < 6:
                        Rk = [rp[:, p, 0:128] for p in range(2)]
                        Pk = [rp[:, p, 128:256] for p in range(2)]
                if R2CUT < 3:
                    continue
                ig, rh = igr.next(), rhr.next()
                gc = slice(g * 128, (g + 1) * 128)
                ah = ahr.next()
                for p in range(2):
                    P.copy("gpsimd", ah[:, 64 * p:64 * p + 64], x[:, p, 0:64])
                P.mm(PG, ah, tmb[:, 1, gc].cast(F32))
                for p in range(2):
                    P.mm(PRh[p], ah, sab[:, p, 128:256])
                for p in range(2):
                    bs = slice(64 * p, 64 * p + 64)
                    P.tt("vector", ig[bs, 64 * p:64 * p + 64], PG[bs, 64 * p:64 * p + 64], i64[bs, :], ALU.add)
                    P.tt("vector", rh[bs, 128 * p:128 * p + 128], PRh[p][bs, :], fmb[bs, 3, :], ALU.add)
                if R2CUT < 4:
                    continue
                for p in range(2):
                    h = 2 * g + p
                    hc = slice(64 * h, 64 * h + 64)
                    py = PY[h // 8][:, (h % 8) * 64:(h % 8) * 64 + 64]
                    P.mm(py, sab[:, p, 128:256], x[:, p, 64:128], start=True, stop=False)
                    P.mm(py, sab[:, p, 384:512], vb[:, hc].cast(F32), start=False, stop=False)
                    P.mm(py, rh[:, 128 * p:128 * p + 128], Mcur[:, g, p, :], start=False, stop=True)
                if R2CUT < 5:
                    continue
                P.mm(PM[:, 0:64], tmb[:, 1, gc].cast(F32), x[:, 0, 64:128], start=True, stop=False, skip=True)
                P.mm(PM[:, 64:128], tmb[:, 1, gc].cast(F32), x[:, 1, 64:128], start=False, stop=False, skip=True)
                P.mm(PM, tmb[:, 2, gc], vb[:, gc], start=False, stop=False, skip=True)
                P.mm(PM, ig, Mcur.re("p g a i -> p g (a i)")[:, g, :], start=False, stop=True, skip=True)
                for p in range(2):
                    bs = slice(64 * p, 64 * p + 64)
                    P.act(Mnew[bs, g, p, :], PM[bs, 64 * p:64 * p + 64], AF.Identity, scale=K.gam[bs, d, g, s:s + 1])
            if R2CUT < 4:
                continue
            ys = ysr.next()
            P.copy("scalar", ys[:, 0:512], PY[0])
            P.copy("vector", ys[:, 512:1024], PY[1])
            P.dma("sync", K.y_d[d][s * 128:(s + 1) * 128, :], ys)


def bcast_row(K, name, src_v):
    n = src_v.ap.shape[-1]
    t = K.A.alloc(name, [128, n])
    K.P.dma("gpsimd", t, V(src_v.buf, src_v.ap.partition_broadcast(128)))
    return t


def stage_r3(K):
    P, A, L, T = K.P, K.A, K.L, K.T
    Wo = A.alloc("wo1", [128, 8, D], BF16)
    for f in range(8):
        P.dma("gpsimd", Wo[:, f, :], K.wout1_d[f * 128:(f + 1) * 128, :])
    lng = bcast_row(K, "lng", K.rows_d[0:1, :])
    lnb = bcast_row(K, "lnb", K.rows_d[1:2, :])
    gx = [bcast_row(K, "gres%d" % w, K.modrow_d[0, w:w + 1, 2 * D:3 * D]) for w in range(2)]
    y0r, y1r, vr, xr_ = (A.ring(n, 2, [128, D]) for n in ("y0", "y1", "vv", "xres"))
    sgr = A.ring("sgl", 2, [128, D], BF16)
    bnr = A.ring("bnl", 2, [128, 16])
    yfr, sqr, t1r = A.ring("yf", 2, [128, D]), A.ring("ysq", 2, [128, D]), A.ring("t1", 2, [128, D])
    obr = A.ring("ob", 2, [128, D], BF16)
    otr = A.ring("oT", 2, [128, 8, 128], BF16)
    str_ = A.ring("stat", 2, [128, 4, 16])
    outr = A.ring("xo", 2, [128, D])
    for (w, i, r0, tok, col) in seq_tiles(K):
        y0, y1, vv, xres, sg, bn = y0r.next(), y1r.next(), vr.next(), xr_.next(), sgr.next(), bnr.next()
        ts_ = slice(tok, tok + 128)
        P.dma("sync", y0, K.y_d[0][ts_, :])
        P.dma("sync", y1, K.y_d[1][ts_, :])
        P.dma("sync", vv, V(K.v_d, K.v_d.ap[ts_, :].bitcast(F32)))
        P.dma("sync", sg, K.sg_d[ts_, :])
        P.dma("sync", bn, K.bonus_d[ts_, :])
        P.dma("sync", xres, (K.ctx_d if w else K.x_d)[r0:r0 + 128, :])
        yf, sq, t1, st = yfr.next(), sqr.next(), t1r.next(), str_.next()
        P.tt("gpsimd", yf, y0, y1, ALU.add)
        P.act(sq, yf, AF.Square)
        h3 = lambda b: b.re("p (h k) -> p h k", k=64)
        P._c("vector", "tensor_reduce", ["out"], dict(out=st[:, 0, :], in_=h3(yf), axis=AX.X, op=ALU.add))
        P._c("vector", "tensor_reduce", ["out"], dict(out=st[:, 1, :], in_=h3(sq), axis=AX.X, op=ALU.add))
        P.ts("vector", st[:, 0, :], st[:, 0, :], 1.0 / 64, ALU.mult)
        P.tt("vector", st[:, 2, :], st[:, 0, :], st[:, 0, :], ALU.mult)
        P.stt("vector", st[:, 1, :], st[:, 1, :], 1.0 / 64, st[:, 2, :], ALU.mult, ALU.subtract)
        P.act(st[:, 3, :], st[:, 1, :], AF.Ln, bias=64e-5)
        P.act(st[:, 3, :], st[:, 3, :], AF.Exp, scale=-0.5)
        bc = lambda v_: v_.bc(2, [128, 16, 64])
        P.tt("vector", h3(t1), h3(yf), bc(st[:, 0, :]), ALU.subtract)
        P.tt("gpsimd", h3(t1), h3(t1), bc(st[:, 3, :]), ALU.mult)
        P.tt("vector", t1, t1, lng, ALU.mult)
        P.tt("gpsimd", t1, t1, lnb, ALU.add)
        P.tt("vector", h3(vv), h3(vv), bc(bn.v), ALU.mult)
        P.tt("gpsimd", t1, t1, vv, ALU.add)
        ob = obr.next()
        P.tt("vector", ob, t1, sg, ALU.mult)
        for f in range(8):
            P.tr(K.pbh[:, f * 128:(f + 1) * 128], ob[:, f * 128:(f + 1) * 128], K.identb)
        oT = otr.next()
        P.copy("scalar", oT.re("p f t -> p (f t)"), K.pbh)
        for ch in range(2):
            ps = K.pb[ch]
            for f in range(8):
                P.mm(ps, oT[:, f, :], Wo[:, f, ch * 512:(ch + 1) * 512], start=(f == 0), stop=(f == 7))
        xo = outr.next()
        for ch in range(2):
            cs = slice(ch * 512, (ch + 1) * 512)
            P.tt("vector", xo[:, cs], K.pb[ch], gx[w][:, cs], ALU.mult)
        P.tt("gpsimd", xo, xo, xres, ALU.add)
        P.dma("sync", (K.ctx1_d if w else K.x1_d)[r0:r0 + 128, :], xo)


def stage_mla(K, do_attn=True, do_final=True):
    return stage_mla_impl(K, do_attn, do_final)


def host_inputs(inp, b, T, L):
    f32 = lambda a: np.ascontiguousarray(np.asarray(a, np.float32))
    pv = np.zeros((128, PV_N), np.float32)
    pv[:, PV_NG0:PV_NG0 + 8] = fm(inp["norm_g"][0])
    pv[:, PV_NG1:PV_NG1 + 8] = fm(inp["norm_g"][1])
    for j in range(6):
        pv[:, PV_MU + 8 * j:PV_MU + 8 * j + 8] = fm(inp["rwkv_mu"][0, j])
    for d in range(2):
        pv[:, PV_W0 + 8 * d:PV_W0 + 8 * d + 8] = fm(inp["rwkv_w0"][0, d])
        pv[:, PV_A0 + 8 * d:PV_A0 + 8 * d + 8] = fm(inp["rwkv_a0"][0, d])
    pv[:, PV_KK:PV_KK + 8] = fm(inp["rwkv_k_k"][0])
    pv[:, PV_KA:PV_KA + 8] = fm(inp["rwkv_k_a"][0])
    pv[:, PV_RK:PV_RK + 8] = fm(np.asarray(inp["rwkv_r_k"][0]).reshape(-1))
    pv[:, PV_QG:PV_QG + 3] = fm(inp["mla_q_norm_g"][0])
    pv[:, PV_KVG:PV_KVG + 2] = fm(inp["mla_kv_norm_g"][0])
    cvec = np.concatenate([fm(inp["c"][b]), fm(inp["c_ctx"])], axis=1)
    w_in2 = np.asarray(inp["mla_w_in"][0], np.float32)
    w_in2 = np.concatenate([w_in2, w_in2[:, 656:672], w_in2[:, 640:656]], axis=1)
    wqb = np.asarray(inp["mla_w_qb"][0], np.float32).reshape(384, 16, 96)
    wqb = np.concatenate([wqb, wqb[:, :, 80:96], wqb[:, :, 64:80]], axis=2).reshape(384, 16 * 128)
    return {
        "x": f32(inp["x"][b][:T]), "ctx": f32(inp["ctx"][b][:L]), "cvec": f32(cvec), "pvec": pv,
        "cst": make_consts(),
        "rows": f32(np.stack([inp["rwkv_lnx_g"][0], inp["rwkv_lnx_b"][0], inp["final_g"]])),
        "mod_w": f32(inp["mod_w"]), "mod_b": f32(inp["mod_b"]),
        "rwkv_w_in": f32(inp["rwkv_w_in"][0]), "rwkv_w1": f32(inp["rwkv_w1"][0]),
        "rwkv_w2": f32(np.asarray(inp["rwkv_w2"][0]).reshape(128, D)),
        "rwkv_a1": f32(inp["rwkv_a1"][0]), "rwkv_a2": f32(np.asarray(inp["rwkv_a2"][0]).reshape(128, D)),
        "rwkv_w_out": f32(inp["rwkv_w_out"][0]),
        "mla_w_in": f32(w_in2), "mla_w_qb": f32(wqb), "mla_w_kvb": f32(inp["mla_w_kvb"][0]),
        "mla_w_out": f32(inp["mla_w_out"][0]), "rope": rope_tables(T),
    }


_CACHE = {}


def kernel(**inputs):
    B, T, _ = inputs["x"].shape
    L = inputs["ctx"].shape[1]
    key = (T, L)
    if key not in _CACHE:
        _CACHE[key] = build(T, L)[0]
    nc = _CACHE[key]
    in_maps = [host_inputs(inputs, b, T, L) for b in range(B)]
    res = run_bass_kernel_spmd(nc, in_maps, core_ids=list(range(B)))
    return np.stack([np.asarray(r["out"], np.float32) for r in res.results], axis=0)


def stage_mla_impl(K, do_attn=True, do_final=True):
    P, A, L, T, NT, NS = K.P, K.A, K.L, K.T, K.NT, K.NS
    pv, cst, pb = K.pv, K.cst, K.pb
    SCL = 1.0 / math.sqrt(96.0)
    qnT = A.alloc("qnT", [128, 3, T], BF16)
    kvnT = A.alloc("kvnT", [128, 2, NT], BF16)
    KT = A.alloc("KT", [128, NT], BF16)
    ones_r = A.alloc("ones_r", [128, 128], F32R)
    ones_f = A.alloc("ones_f", [128, 64])
    P.memset("vector", ones_r, 1.0)
    P.memset("vector", ones_f, 1.0)
    mark = A.off
    Win = A.alloc("mwin", [128, 8, 1728], BF16)
    for f in range(8):
        P.dma("gpsimd", Win[:, f, :], K.mw_in_d[f * 128:(f + 1) * 128, :])
    hbr = A.ring("mhb", 1, [128, 8, 512], BF16)
    qc = A.alloc("qc", [128, 5, 512])
    sqr = A.ring("msq", 1, [128, 512], F32R)
    rsr = A.ring("mrs", 1, [128, 512])
    rpr = A.ring("mrp", 1, [128, 2, 512])
    t1r = A.ring("mt1", 1, [128, 512])
    t2r = A.ring("mt2", 1, [128, 512])
    sgo = A.ring("msg", 1, [128, 8, 512], BF16)
    blocks = [(1, 0, K.CO, L, 0)] + [(0, L + i * 512, K.XO + i * 512, 512, i * 512) for i in range(T // 512)]
    for (wh, t0, c0, nb, xt0) in blocks:
        hb = hbr.next()
        P.dma("sync", hb[:, :, 0:nb], K.hT_d[:, :, c0:c0 + nb])
        tiles = ([] if wh else [0, 1, 2]) + [3, 4]
        for m in tiles:
            for f in range(8):
                P.mm(pb[m][:, 0:nb], Win[:, f, m * 128:(m + 1) * 128], hb[:, f, 0:nb], start=(f == 0), stop=(f == 7))
            P.copy("scalar", qc[:, m, 0:nb], pb[m][:, 0:nb])
        for (ms, nfeat, dst, gcol) in (([] if wh else [0, 1, 2], 384.0, qnT, PV_QG), ([3, 4], 256.0, kvnT, PV_KVG)):
            if not ms:
                continue
            for i, m in enumerate(ms):
                sq = sqr.next()
                P.tt("gpsimd", sq[:, 0:nb], qc[:, m, 0:nb], qc[:, m, 0:nb], ALU.mult)
                P.mm(pb[5][:, 0:nb], ones_r, sq[:, 0:nb], start=(i == 0), stop=(i == len(ms) - 1))
            rs = rsr.next()
            P.act(rs[:, 0:nb], pb[5][:, 0:nb], AF.Ln, bias=1e-6, scale=1.0 / nfeat)
            P.act(rs[:, 0:nb], rs[:, 0:nb], AF.Exp, scale=-0.5)
            for i, m in enumerate(ms):
                o_ = dst[:, i, xt0:xt0 + nb] if dst is qnT else dst[:, i, t0:t0 + nb]
                P.stt("vector", o_, qc[:, m, 0:nb], pv[:, gcol + i:gcol + i + 1], rs[:, 0:nb], ALU.mult, ALU.mult)
        pe, sw = pb[6][64:96, 0:nb], pb[7][64:96, 0:nb]
        for f in range(8):
            P.mm(pe, Win[:, f, 640:672], hb[:, f, 0:nb], start=(f == 0), stop=(f == 7))
        if wh:
            P.copy("scalar", KT[64:96, t0:t0 + nb], pe)
        else:
            for f in range(8):
                P.mm(sw, Win[:, f, 1696:1728], hb[:, f, 0:nb], start=(f == 0), stop=(f == 7))
            rp = rpr.next()
            P.dma("sync", rp[64:96, :, :], K.rope_d[:, :, xt0:xt0 + nb])
            t1, t2 = t1r.next(), t2r.next()
            P.tt("vector", t1[64:96, :], pe, rp[64:96, 0, :], ALU.mult)
            P.tt("vector", t2[64:96, :], sw, rp[64:96, 1, :], ALU.mult)
            P.tt("gpsimd", KT[64:96, t0:t0 + nb], t1[64:96, :], t2[64:96, :], ALU.add)
            so = sgo.next()
            for g in range(8):
                ps = pb[g % 4]
                for f in range(8):
                    P.mm(ps, Win[:, f, 672 + g * 128:672 + (g + 1) * 128], hb[:, f, :], start=(f == 0), stop=(f == 7))
                P.act(so[:, g, :], ps, AF.Silu)
            P.dma("sync", K.sgT_d[:, :, xt0:xt0 + 512], so)
    barrier(P, K.tiny)
    A.off = mark
    if not do_attn:
        return
    QT = A.alloc("QT", [128, T], BF16)
    Vh = A.alloc("Vh", [128, NS, 65], BF16)
    P.memset("vector", Vh, 1.0)
    wqr = A.ring("wq", 2, [128, 3, 128], BF16)
    wkr = A.ring("wk", 2, [128, 2, 128], BF16)
    ptr = A.ring("pt", 3, [128, 512], BF16)
    rpr = A.ring("arp", 2, [128, 2, 512])
    t1r = A.ring("at1", 2, [128, 512])
    t2r = A.ring("at2", 2, [128, 512])
    recr = A.ring("rec", 2, [128, 512])
    bcr = A.ring("bcs", 2, [64, 512])
    otr = A.ring("oth", 2, [64, 512], BF16)
    kblocks = [(k0, min(512, NT - k0)) for k0 in range(0, NT, 512)]
    for h in range(NH):
        wq, wk = wqr.next(), wkr.next()
        P.dma("gpsimd", wq, K.wqb_d.re("(k p) n -> p k n", p=128)[:, :, h * 128:(h + 1) * 128])
        P.dma("gpsimd", wk, K.wkvb_d.re("(k p) n -> p k n", p=128)[:, :, h * 128:(h + 1) * 128])
        for (k0, nk) in kblocks:
            for kt in range(2):
                P.mm(pb[0][0:64, 0:nk], wk[:, kt, 0:64], kvnT[:, kt, k0:k0 + nk], start=(kt == 0), stop=(kt == 1))
            P.copy("scalar", KT[0:64, k0:k0 + nk], pb[0][0:64, 0:nk])
        for j0 in range(0, NS, 8):
            nj = min(8, NS - j0)
            for j in range(nj):
                for kt in range(2):
                    P.mm(pb[1][:, j * 64:(j + 1) * 64], kvnT[:, kt, (j0 + j) * 128:(j0 + j + 1) * 128], wk[:, kt, 64:128],
                         start=(kt == 0), stop=(kt == 1))
            P.copy("vector", Vh[:, j0:j0 + nj, 0:64], pb[1][:, 0:nj * 64].re("p (j c) -> p j c", c=64))
        for qb in range(T // 512):
            qs = slice(qb * 512, (qb + 1) * 512)
            for kt in range(3):
                P.mm(pb[2][0:96, :], wq[:, kt, 0:96], qnT[:, kt, qs], start=(kt == 0), stop=(kt == 2))
            for kt in range(3):
                P.mm(pb[3][64:96, :], wq[:, kt, 96:128], qnT[:, kt, qs], start=(kt == 0), stop=(kt == 2))
            rp = rpr.next()
            P.dma("sync", rp[64:96, :, :], K.rope_d[:, :, qs])
            t1, t2 = t1r.next(), t2r.next()
            P.copy("scalar", QT[0:64, qs], pb[2][0:64, :])
            P.tt("vector", t1[64:96, :], pb[2][64:96, :], rp[64:96, 0, :], ALU.mult)
            P.tt("vector", t2[64:96, :], pb[3][64:96, :], rp[64:96, 1, :], ALU.mult)
            P.tt("gpsimd", QT[64:96, qs], t1[64:96, :], t2[64:96, :], ALU.add)
        for qb in range(T // 512):
            qs = slice(qb * 512, (qb + 1) * 512)
            po = pb[4 + qb % 2]
            LOOK = ATT_LOOK
            for kt in range(min(LOOK, NS)):
                P.mm(pb[kt % 4], KT[0:96, kt * 128:(kt + 1) * 128], QT[0:96, qs])
            for kt in range(NS):
                if kt + LOOK < NS:
                    k2 = kt + LOOK
                    P.mm(pb[k2 % 4], KT[0:96, k2 * 128:(k2 + 1) * 128], QT[0:96, qs])

                pt = ptr.next()
                P.act(pt, pb[kt % 4], AF.Exp, scale=SCL)
                P.mm(po[0:65, :], Vh[:, kt, :], pt, start=(kt == 0), stop=(kt == NS - 1))
            rec, bcs, ot = recr.next(), bcr.next(), otr.next()
            P.recip(rec[64:65, :], po[64:65, :])
            P.mm(pb[6][0:64, :], ones_f[64:65, :], rec[64:65, :])
            P.copy("scalar", bcs, pb[6][0:64, :])
            P.tt("vector", ot, po[0:64, :], bcs, ALU.mult)
            P.dma("sync", K.oT_d[64 * (h % 2):64 * (h % 2) + 64, h // 2, qs], ot)
    barrier(P, K.tiny)
    A.off = A.base
    if not do_final:
        return
    Wo = A.alloc("wo2", [128, 8, D], BF16)
    for f in range(8):
        P.dma("gpsimd", Wo[:, f, :], K.wout2_d[f * 128:(f + 1) * 128, :])
    gx2 = bcast_row(K, "gx2", K.modrow_d[1, 0:1, 2 * D:3 * D])
    fing = bcast_row(K, "fing", K.rows_d[2:3, :])
    obr = A.ring("fo", 2, [128, 8, 512], BF16)
    sbr = A.ring("fs", 2, [128, 8, 512], BF16)
    ogr = A.ring("fg", 2, [128, 8, 512], BF16)
    x1r = A.ring("fx", 2, [128, D])
    x2r = A.ring("fx2", 2, [128, D])
    jr = A.ring("fj", 1, [128, D], BF16)
    ssr = A.ring("fss", 4, [128, 2])
    for qb in range(T // 512):
        qs = slice(qb * 512, (qb + 1) * 512)
        ob, sb, og = obr.next(), sbr.next(), ogr.next()
        P.dma("sync", ob, K.oT_d[:, :, qs])
        P.dma("sync", sb, K.sgT_d[:, :, qs])
        P.tt("gpsimd", og, ob, sb, ALU.mult)
        for tt in range(4):
            r0 = qb * 512 + tt * 128
            x1 = x1r.next()
            P.dma("sync", x1, K.x1_d[r0:r0 + 128, :])
            for ch in range(2):
                for f in range(8):
                    P.mm(pb[ch], og[:, f, tt * 128:(tt + 1) * 128], Wo[:, f, ch * 512:(ch + 1) * 512], start=(f == 0), stop=(f == 7))
            x2 = x2r.next()
            for ch in range(2):
                cs = slice(ch * 512, (ch + 1) * 512)
                P.tt("vector", x2[:, cs], pb[ch], gx2[:, cs], ALU.mult)
            P.tt("gpsimd", x2, x2, x1, ALU.add)
            ss = ssr.next()
            P.act(jr.next(), x2, AF.Square, accum_out=ss[:, 0:1])
            P.act(ss[:, 1:2], ss[:, 0:1], AF.Ln, bias=1e-6, scale=1.0 / D)
            P.act(ss[:, 1:2], ss[:, 1:2], AF.Exp, scale=-0.5)
            P.stt("vector", x2, x2, ss[:, 1:2], fing, ALU.mult, ALU.mult)
            P.dma("sync", K.out_d[r0:r0 + 128, :], x2)
```

```python
import contextlib
import math
import numpy as np
import concourse.bass as bass
import concourse.mybir as mybir
from concourse.bass_utils import run_bass_kernel_spmd

F32 = mybir.dt.float32
F32R = mybir.dt.float32
BF16 = mybir.dt.bfloat16
ALU = mybir.AluOpType
AF = mybir.ActivationFunctionType
AX = mybir.AxisListType

N_DMA_SEMS = 8
DEBUG_SRC = False
R2CUT = 99
ATT_LOOK = 3
R2_SPLIT = False
R2VAR = 0
NO_SELF_SYNC = ("tensor",)

D = 1024
NH = 16
C0 = math.exp(-0.5)


class V:
    __slots__ = ("buf", "ap")

    def __init__(self, buf, ap):
        self.buf = buf
        self.ap = ap

    def __getitem__(self, k):
        return V(self.buf, self.ap[k])

    def re(self, s, **kw):
        return V(self.buf, self.ap.rearrange(s, **kw))

    def bc(self, axis, shape):
        return V(self.buf, self.ap.unsqueeze(axis).to_broadcast(list(shape)))

    def cast(self, dt):
        return V(self.buf, self.ap.bitcast(dt))


class Buf:
    __slots__ = ("name", "w", "r", "ap")

    def __init__(self, name, ap):
        self.name = name
        self.w = None
        self.r = []
        self.ap = ap

    def __getitem__(self, k):
        return V(self, self.ap[k])

    @property
    def v(self):
        return V(self, self.ap)

    def re(self, s, **kw):
        return V(self, self.ap.rearrange(s, **kw))


def _v(x):
    return x.v if isinstance(x, Buf) else x


class Op:
    __slots__ = ("eng", "fn", "deps", "sig", "cnt", "dma_sem", "dma_val", "dma_prev", "src")

    def __init__(self, eng, fn):
        self.src = None
        self.eng = eng
        self.fn = fn
        self.deps = []
        self.sig = False
        self.cnt = 0
        self.dma_sem = None
        self.dma_val = 0
        self.dma_prev = None


class Eng:
    def __init__(self, name):
        self.name = name
        self.ops = []
        self.sem = None
        self.dma_sems = []
        self.dma_uses = [0] * N_DMA_SEMS
        self.dma_last = [None] * N_DMA_SEMS
        self.n_dma = 0


class Prog:
    def __init__(self, nc):
        self.nc = nc
        self.engs = {n: Eng(n) for n in ("tensor", "vector", "scalar", "gpsimd", "sync")}
        self.stack = contextlib.ExitStack()
        self.n_ops = 0
        self._rr = {}

    def sbuf(self, name, shape, dtype=F32):
        t = self.stack.enter_context(self.nc.sbuf_tensor(name, list(shape), dtype))
        return Buf(name, t[:])

    def psum(self, name, shape, dtype=F32):
        t = self.stack.enter_context(self.nc.psum_tensor(name, list(shape), dtype))
        return Buf(name, t[:])

    def dram(self, name, shape, dtype=F32, kind="Internal"):
        t = self.nc.dram_tensor(name, list(shape), dtype, kind=kind)
        return Buf(name, t.ap())

    def sub(self, buf, name, key):
        return Buf(name, buf.ap[key])

    def ring(self, name, n, shape, dtype=F32, space="sbuf"):
        mk = self.sbuf if space == "sbuf" else self.psum
        return Ring([mk("%s%d" % (name, i), shape, dtype) for i in range(n)])

    def op(self, eng, fn, reads=(), writes=()):
        e = self.engs[eng]
        o = Op(e, fn)
        if DEBUG_SRC:
            import sys as _s
            fr = _s._getframe(1)
            while fr.f_code.co_name in ("op", "_c", "mm", "tr", "act", "tt", "ts", "stt", "copy", "recip", "memset", "dma"):
                fr = fr.f_back
            o.src = fr.f_lineno
        deps = {}
        for b in reads:
            if b.w is not None:
                deps[id(b.w)] = b.w
        for b in writes:
            if b.w is not None:
                deps[id(b.w)] = b.w
            for r in b.r:
                deps[id(r)] = r
        for d in deps.values():
            if d.dma_sem is None and d.eng is e and e.name in NO_SELF_SYNC:
                continue
            o.deps.append(d)
            if d.dma_sem is None:
                d.sig = True
        for b in reads:
            b.r.append(o)
        for b in writes:
            b.w = o
            b.r = []
        e.ops.append(o)
        self.n_ops += 1
        return o

    def dma(self, eng, out, in_, **kw):
        out, in_ = _v(out), _v(in_)
        e = self.engs[eng]
        s = e.n_dma % N_DMA_SEMS
        e.n_dma += 1
        o = self.op(eng, ("dma", out.ap, in_.ap, kw), [in_.buf], [out.buf])
        o.dma_sem = s
        e.dma_uses[s] += 1
        o.dma_val = 16 * e.dma_uses[s]
        o.dma_prev = e.dma_last[s]
        e.dma_last[s] = o
        return o

    def _c(self, eng, method, outs, kw, extra_reads=()):
        reads, writes, args = list(extra_reads), [], {}
        for k, a in kw.items():
            if isinstance(a, (V, Buf)):
                a = _v(a)
                (writes if k in outs else reads).append(a.buf)
                args[k] = a.ap
            else:
                args[k] = a
        return self.op(eng, lambda q: getattr(q, method)(**args), reads, writes)

    def mm(self, out, lhsT, rhs, start=True, stop=True, skip=False):
        out, lhsT, rhs = _v(out), _v(lhsT), _v(rhs)
        oa, la, ra = out.ap, lhsT.ap, rhs.ap
        assert len(ra.shape) == 2 and len(la.shape) == 2 and len(oa.shape) == 2, (oa.shape, la.shape, ra.shape)
        return self.op("tensor", lambda q: q.matmul(oa, lhsT=la, rhs=ra, start=start, stop=stop,
                                                    skip_group_check=skip),
                       [lhsT.buf, rhs.buf], [out.buf])

    def tr(self, out, in_, ident):
        out, in_, ident = _v(out), _v(in_), _v(ident)
        oa, ia, da = out.ap, in_.ap, ident.ap
        return self.op("tensor", lambda q: q.transpose(out=oa, in_=ia, identity=da),
                       [in_.buf, ident.buf], [out.buf])

    def act(self, out, in_, func, bias=0.0, scale=1.0, accum_out=None, eng="scalar"):
        kw = dict(out=out, in_=in_, func=func, bias=bias, scale=scale)
        outs = ["out"]
        if accum_out is not None:
            kw["accum_out"] = accum_out
            outs.append("accum_out")
        return self._c("scalar", "activation", outs, kw)

    def tt(self, eng, out, in0, in1, op):
        return self._c(eng, "tensor_tensor", ["out"], dict(out=out, in0=in0, in1=in1, op=op))

    def ts(self, eng, out, in0, s1, op0, s2=None, op1=None):
        kw = dict(out=out, in0=in0, scalar1=s1, scalar2=s2, op0=op0)
        if op1 is not None:
            kw["op1"] = op1
        return self._c(eng, "tensor_scalar", ["out"], kw)

    def stt(self, eng, out, in0, scalar, in1, op0, op1):
        return self._c(eng, "scalar_tensor_tensor", ["out"],
                       dict(out=out, in0=in0, scalar=scalar, in1=in1, op0=op0, op1=op1))

    def copy(self, eng, out, in_):
        if eng == "scalar":
            return self._c("scalar", "copy", ["out"], dict(out=out, in_=in_))
        return self._c(eng, "tensor_copy", ["out"], dict(out=out, in_=in_))

    def recip(self, out, in_):
        return self._c("vector", "reciprocal", ["out"], dict(out=out, in_=in_))

    def memset(self, eng, out, val):
        out = _v(out)
        oa = out.ap
        return self.op(eng, lambda q: q.memset(oa, val), [], [out.buf])

    def rr(self, key, engs):
        i = self._rr.get(key, 0)
        self._rr[key] = i + 1
        return engs[i % len(engs)]

    def finish(self):
        nc = self.nc
        st = self.stack
        for e in self.engs.values():
            e.sem = st.enter_context(nc.semaphore("s_" + e.name))
            if e.n_dma:
                e.dma_sems = [st.enter_context(nc.semaphore("d_%s%d" % (e.name, i)))
                              for i in range(N_DMA_SEMS)]
        for e in self.engs.values():
            c = 0
            for o in e.ops:
                if o.dma_sem is None and o.sig:
                    c += 1
                    o.cnt = c
        block = st.enter_context(nc.Block())
        prog = self

        def replay(e, q):
            waited = {}

            def wait(sem, key, val):
                if waited.get(key, 0) < val:
                    q.wait_ge(sem, val)
                    waited[key] = val

            for o in e.ops:
                for d in o.deps:
                    if d.dma_sem is not None:
                        wait(d.eng.dma_sems[d.dma_sem], (d.eng.name, d.dma_sem), d.dma_val)
                    else:
                        wait(d.eng.sem, d.eng.name, d.cnt)
                if o.dma_sem is not None:
                    p = o.dma_prev
                    if p is not None:
                        wait(e.dma_sems[p.dma_sem], (e.name, p.dma_sem), p.dma_val)
                    _, oa, ia, kw = o.fn
                    q.dma_start(out=oa, in_=ia, **kw).then_inc(e.dma_sems[o.dma_sem], 16)
                elif o.fn is not None:
                    ins = o.fn(q)
                    if o.sig:
                        ins.then_inc(e.sem, 1)
            if e.name == "sync":
                for e2 in prog.engs.values():
                    if e2.n_dma:
                        for s in range(N_DMA_SEMS):
                            lo = e2.dma_last[s]
                            if lo is not None:
                                wait(e2.dma_sems[s], (e2.name, s), lo.dma_val)

        engs = self.engs

        @block.tensor
        def _(q):
            replay(engs["tensor"], q)

        @block.vector
        def _(q):
            replay(engs["vector"], q)

        @block.scalar
        def _(q):
            replay(engs["scalar"], q)

        @block.gpsimd
        def _(q):
            replay(engs["gpsimd"], q)

        @block.sync
        def _(q):
            replay(engs["sync"], q)

        st.close()


class Ring:
    def __init__(self, bufs):
        self.bufs = bufs
        self.i = 0

    def next(self):
        b = self.bufs[self.i % len(self.bufs)]
        self.i += 1
        return b


DT_SIZE = {F32: 4, F32R: 4, BF16: 2}


class Arena:
    def __init__(self, P, nbytes):
        self.P = P
        self.n4 = nbytes // 4
        t = P.stack.enter_context(P.nc.sbuf_tensor("arena", [128, self.n4], F32))
        self.ap = t[:]
        self.base = 0
        self.off = 0
        self.k = 0

    def persist(self):
        self.base = self.off

    def reset(self):
        self.off = self.base

    def alloc(self, name, shape, dtype=F32, parts=128):
        free = int(np.prod(shape[1:]))
        nb = free * DT_SIZE[dtype]
        n4 = (nb + 31) // 32 * 8
        assert self.off + n4 <= self.n4, "arena overflow at %s (%d KB)" % (name, (self.off + n4) * 4 // 1024)
        ap = self.ap[0:shape[0], self.off:self.off + n4]
        self.off += n4
        if dtype != F32:
            ap = ap.bitcast(dtype)
        ap = ap[:, 0:free]
        if len(shape) == 3:
            ap = ap.rearrange("p (a b) -> p a b", b=shape[2])
        elif len(shape) == 4:
            ap = ap.rearrange("p (a b c) -> p a b c", b=shape[2], c=shape[3])
        self.k += 1
        return Buf("%s_%d" % (name, self.k), ap)

    def ring(self, name, n, shape, dtype=F32):
        return Ring([self.alloc(name, shape, dtype) for _ in range(n)])


def barrier(P, tiny):
    firsts = [P.memset("vector", tiny[0], 0.0), P.memset("gpsimd", tiny[1], 0.0), P.act(tiny[2], tiny[3], AF.Copy)]
    dmas = []
    for e in P.engs.values():
        for s in range(N_DMA_SEMS):
            if e.dma_last[s] is not None:
                dmas.append(e.dma_last[s])
    for f in firsts:
        f.sig = True
    for name, e in P.engs.items():
        o = Op(e, None)
        o.deps = [f for f in firsts if f.eng is not e] + dmas
        e.ops.append(o)


CST_IDENT, CST_BONES, CST_M01, CST_MT, CST_HSEL, CST_SCAN, CST_N = 0, 128, 256, 768, 1024, 1152, 1408


def make_consts():
    c = np.zeros((128, CST_N), np.float32)
    p = np.arange(128)
    c[:, CST_IDENT:CST_IDENT + 128] = np.eye(128)
    c[:, CST_BONES:CST_BONES + 128] = (p[:, None] // 64 == p[None, :] // 64)
    s, t = p[:, None], p[None, :]
    c[:, CST_M01 + 0:CST_M01 + 128] = (t > s)
    c[:, CST_M01 + 128:CST_M01 + 256] = (t >= s)
    c[:, CST_M01 + 256:CST_M01 + 384] = (t < s)
    c[:, CST_M01 + 384:CST_M01 + 512] = (t <= s)
    c[:, CST_MT:CST_MT + 128] = (p[None, :] < p[:, None])
    c[:, CST_MT + 128:CST_MT + 256] = (p[None, :] > p[:, None])
    for g in range(8):
        for h in range(16):
            c[:, CST_HSEL + g * 16 + h] = (h == 2 * g + p // 64)
    sm = np.ones((128, 256), np.float32)
    sm[:, 0] = 0
    sm[:, 128] = 0
    c[:, CST_SCAN:CST_SCAN + 256] = sm
    return c


def fm(vec):
    v = np.asarray(vec, np.float32).reshape(-1, 128)
    return np.ascontiguousarray(v.T)


PV_NG0, PV_NG1, PV_MU, PV_W0, PV_A0, PV_KK, PV_KA, PV_RK, PV_QG, PV_KVG, PV_N = 0, 8, 16, 64, 80, 96, 104, 112, 120, 123, 128


def rope_tables(T):
    rows = T // 64
    row = np.repeat(np.arange(rows), 64).astype(np.float32)
    col = np.tile(np.arange(64), rows).astype(np.float32)
    inv = (1.0 / (10000.0 ** (np.arange(0, 16, 2, dtype=np.float32) / 16))).astype(np.float32)
    ang = np.concatenate([row[:, None] * inv, col[:, None] * inv], axis=-1).astype(np.float32)
    cos, sin = np.cos(ang).T.astype(np.float32), np.sin(ang).T.astype(np.float32)
    out = np.zeros((32, 2, T), np.float32)
    out[0:16, 0], out[16:32, 0] = cos, cos
    out[0:16, 1], out[16:32, 1] = -sin, sin
    return out


class Ctx:
    pass


def build(T, L, stages=("mod", "r1a", "r1b", "r2", "r3", "m1a", "m1b", "m2", "m3"), dbg=()):
    nc = bass.Bass("TRN2", target_bir_lowering=False)
    nc.dge_precook = False
    P = Prog(nc)
    K = Ctx()
    K.P, K.T, K.L = P, T, L
    NT = L + T
    NS = NT // 128
    K.NT, K.NS = NT, NS
    K.CO, K.XO, K.NTP = 1, L + 3, L + T + 4

    def ext(name, shape, dt=F32):
        return P.dram(name, shape, dt, kind="ExternalInput")

    K.x_d = ext("x", [T, D])
    K.ctx_d = ext("ctx", [L, D])
    K.cvec_d = ext("cvec", [128, 16])
    K.pvec_d = ext("pvec", [128, PV_N])
    K.cst_d = ext("cst", [128, CST_N])
    K.rows_d = ext("rows", [3, D])
    K.mod_w_d = ext("mod_w", [2, D, 3 * D])
    K.mod_b_d = ext("mod_b", [2, 3 * D])
    K.w_in_d = ext("rwkv_w_in", [4, D, D])
    K.w1_d = ext("rwkv_w1", [2, D, 64])
    K.w2_d = ext("rwkv_w2", [128, D])
    K.a1_d = ext("rwkv_a1", [2, D, 64])
    K.a2_d = ext("rwkv_a2", [128, D])
    K.wout1_d = ext("rwkv_w_out", [D, D])
    K.mw_in_d = ext("mla_w_in", [D, 1696 + 32])
    K.wqb_d = ext("mla_w_qb", [384, 16 * 128])
    K.wkvb_d = ext("mla_w_kvb", [256, 2048])
    K.wout2_d = ext("mla_w_out", [D, D])
    K.rope_d = ext("rope", [32, 2, T])
    K.out_d = P.dram("out", [T, D], F32, kind="ExternalOutput")

    okind = "ExternalOutput" if dbg else "Internal"

    def scr(name, shape, dt=F32):
        return P.dram(name, shape, dt, kind=("ExternalOutput" if name in dbg else "Internal"))

    K.modrow_d = scr("modrow", [2, 2, 3 * D])
    K.hT_d = scr("hT", [128, 8, K.NTP], BF16)
    K.fm_d = [scr("fm%d" % d, [128, 8, NS, 4, 128], BF16) for d in range(2)]
    K.tm_d = [scr("tm%d" % d, [NT, 3, D], F32R) for d in range(2)]
    K.v_d = scr("vtm", [NT, D], F32R)
    K.sg_d = scr("sg", [NT, D], BF16)
    K.bonus_d = scr("bonus", [NT, 16])
    K.y_d = [scr("y%d" % d, [NT, D]) for d in range(2)]
    K.x1_d = scr("x1", [T, D])
    K.ctx1_d = scr("ctx1", [L, D])
    K.sgT_d = scr("sgT", [128, 8, T], BF16)
    K.oT_d = scr("oT", [128, 8, T], BF16)

    A = Arena(P, 190 * 1024)
    K.A = A
    K.pb = [P.psum("pb%d" % i, [128, 512], F32) for i in range(8)]
    K.pbh = Buf("pbh", K.pb[7].ap.bitcast(BF16))
    K.pbh2 = Buf("pbh2", K.pb[6].ap.bitcast(BF16))

    K.cst = A.alloc("cst", [128, CST_N])
    K.pv = A.alloc("pv", [128, PV_N])
    K.identb = A.alloc("identb", [128, 128], BF16)
    K.cst_r = A.alloc("cstr", [128, CST_N], F32R)
    K.tiny = [A.alloc("tiny%d" % i, [128, 8]) for i in range(4)]
    K.modT = [A.alloc("modT%d" % i, [128, 24, 2]) for i in range(2)]
    K.modA = [[A.alloc("modA", [128, 8]) for w in range(2)] for i in range(2)]
    K.modB = [[A.alloc("modB", [128, 8]) for w in range(2)] for i in range(2)]
    K.negw0 = A.alloc("negw0", [128, 16])
    K.nega0 = A.alloc("nega0", [128, 16])
    K.omka = A.alloc("omka", [128, 8])
    K.gam = A.alloc("gam", [128, 2, 8, NS])
    K.zero = A.alloc("zero", [128, 8, 1], BF16)
    A.persist()
    P.dma("sync", K.cst, K.cst_d)
    P.dma("sync", K.pv, K.pvec_d)
    P.copy("vector", K.identb, K.cst[:, CST_IDENT:CST_IDENT + 128])
    P.copy("vector", K.cst_r, K.cst)
    for i in range(4):
        P.memset("vector", K.tiny[i], 0.0)
    P.memset("vector", K.zero, 0.0)
    P.ts("vector", K.negw0, K.pv[:, PV_W0:PV_W0 + 16], -1.0, ALU.mult)
    P.ts("vector", K.nega0, K.pv[:, PV_A0:PV_A0 + 16], -1.0, ALU.mult)
    P.ts("vector", K.omka, K.pv[:, PV_KA:PV_KA + 8], -1.0, ALU.mult, 1.0, ALU.add)

    def stage_end():
        barrier(P, K.tiny)
        A.reset()

    if "mod" in stages:
        stage_mod(K)
        stage_end()
    if "r1a" in stages:
        stage_hT(K, 0, K.x_d, K.ctx_d)
        stage_end()
    if "r1b" in stages:
        stage_r1b(K)
        stage_end()
    if "r2" in stages:
        stage_r2(K)
        stage_end()
    if "r3" in stages:
        stage_r3(K)
        stage_end()
    if "m1a" in stages:
        stage_hT(K, 1, K.x1_d, K.ctx1_d)
        stage_end()
    if "m1b" in stages:
        stage_mla(K, "m2" in stages, "m3" in stages)
    P.finish()
    return nc, P


def stage_mod(K):
    P, A = K.P, K.A
    cT = A.alloc("cT", [128, 16])
    sc = A.alloc("scT", [128, 8, 2])
    P.dma("sync", cT, K.cvec_d)
    sg = A.alloc("sgc", [128, 16])
    P.act(sg, cT, AF.Sigmoid)
    P.tt("vector", sc.re("p f w -> p w f"), cT.re("p (w f) -> p w f", w=2), sg.re("p (w f) -> p w f", w=2), ALU.mult)
    mwr = A.ring("mw", 3, [128, 512])
    mb = A.alloc("mb", [2, 3 * D])
    mrow = A.alloc("mrow", [2, 3 * D])
    for layer in range(2):
        P.dma("gpsimd", mb, V(K.mod_b_d, K.mod_b_d.ap[layer:layer + 1, :].partition_broadcast(2)))
        for cb in range(6):
            ps = K.pb[cb % 2]
            for f in range(8):
                mw = mwr.next()
                P.dma("sync", mw, K.mod_w_d[layer, f * 128:(f + 1) * 128, cb * 512:(cb + 1) * 512])
                P.mm(ps[0:2, :], sc[:, f, :], mw, start=(f == 0), stop=(f == 7))
            P.tt("vector", mrow[:, cb * 512:(cb + 1) * 512], ps[0:2, :], mb[:, cb * 512:(cb + 1) * 512], ALU.add)
        P.dma("sync", K.modrow_d[layer], mrow)
        pt = K.pb[2]
        for c in range(24):
            P.tr(pt[:, 2 * c:2 * c + 2], mrow[:, c * 128:(c + 1) * 128], K.cst[0:2, CST_IDENT:CST_IDENT + 2])
        P.copy("vector", K.modT[layer].re("p c w -> p (c w)"), pt[:, 0:48])
        for w in range(2):
            ng = K.pv[:, PV_NG0 + 8 * layer:PV_NG0 + 8 * layer + 8]
            P.stt("vector", K.modA[layer][w], K.modT[layer][:, 8:16, w], 1.0, ng, ALU.add, ALU.mult)
            P.copy("vector", K.modB[layer][w], K.modT[layer][:, 0:8, w])


def seq_tiles(K):
    out = []
    for i in range(K.L // 128):
        out.append((1, i, i * 128, i * 128, K.CO + i * 128))
    for i in range(K.T // 128):
        out.append((0, i, i * 128, K.L + i * 128, K.XO + i * 128))
    return out


def stage_hT(K, layer, x_d, ctx_d):
    P, A = K.P, K.A
    xr = A.ring("xt", 3, [128, D])
    jr = A.ring("junk", 2, [128, D], BF16)
    xnr = A.ring("xn", 2, [128, D], BF16)
    tmpr = A.ring("tmp", 2, [128, 8, 128])
    hr = A.ring("ht", 2, [128, 8, 128], BF16)
    ssr = A.ring("ss", 4, [128, 2])
    for col in (0, K.L + 1, K.L + 2, K.L + K.T + 3):
        P.dma("gpsimd", K.hT_d[:, :, col:col + 1], K.zero, allow_slow_non_contiguous=True)
    for (w, i, r0, tok, col) in seq_tiles(K):
        src = ctx_d if w else x_d
        xt = xr.next()
        P.dma("sync", xt, src[r0:r0 + 128, :])
        ss = ssr.next()
        P.act(jr.next(), xt, AF.Square, accum_out=ss[:, 0:1])
        P.act(ss[:, 1:2], ss[:, 0:1], AF.Ln, bias=1e-6, scale=1.0 / D)
        P.act(ss[:, 1:2], ss[:, 1:2], AF.Exp, scale=-0.5)
        xn = xnr.next()
        P.ts("gpsimd", xn, xt, ss[:, 1:2], ALU.mult)
        ph = K.pbh if (tok // 128) % 2 == 0 else K.pbh2
        for f in range(8):
            P.tr(ph[:, f * 128:(f + 1) * 128], xn[:, f * 128:(f + 1) * 128], K.identb)
        tmp = tmpr.next()
        ht = hr.next()
        P.tt("vector", tmp, ph.re("p (f t) -> p f t", t=128), K.modA[layer][w].v.bc(2, [128, 8, 128]), ALU.mult)
        P.tt("gpsimd", ht, tmp, K.modB[layer][w].v.bc(2, [128, 8, 128]), ALU.add)
        P.dma("sync", K.hT_d[:, :, col:col + 128], ht)


def load_w_bf16(K, name, src_ap_v, shape):
    t = K.A.alloc(name, shape, BF16)
    K.P.dma("gpsimd", t, src_ap_v)
    return t


def stage_r1b(K):
    P, A, L, T = K.P, K.A, K.L, K.T
    NB = 256
    pv, cst = K.pv, K.cst
    W = [A.alloc("win%d" % j, [128, 8, D], BF16) for j in range(4)]
    for j in range(4):
        for f in range(8):
            P.dma("gpsimd", W[j][:, f, :], K.w_in_d[j, f * 128:(f + 1) * 128, :])
    W1c = A.alloc("w1c", [128, 8, 128], BF16)
    A1c = A.alloc("a1c", [128, 8, 128], BF16)
    for d in range(2):
        P.dma("gpsimd", W1c[:, :, 64 * d:64 * d + 64], K.w1_d.re("d (f p) r -> d p f r", p=128)[d])
        P.dma("gpsimd", A1c[:, :, 64 * d:64 * d + 64], K.a1_d.re("d (f p) r -> d p f r", p=128)[d])
    W2c = load_w_bf16(K, "w2c", K.w2_d, [128, D])
    A2c = load_w_bf16(K, "a2c", K.a2_d, [128, D])
    bones_r = K.cst_r[:, CST_BONES:CST_BONES + 128]
    scanm = cst[:, CST_SCAN:CST_SCAN + 256]
    hbr = A.ring("hb", 1, [128, 8, NB + 2], BF16)
    xx = A.alloc("xx", [128, 8, NB])
    tmpx = A.ring("tmpx", 2, [128, NB])
    lerp = [A.alloc("lerp%d" % j, [128, 8, NB], BF16) for j in range(6)]
    twb = A.alloc("twb", [128, NB], BF16)
    tab = A.alloc("tab", [128, NB], BF16)
    prod = A.alloc("prod", [128, 2, 8, NB], BF16)
    hselb = A.alloc("hselb", [128, 128], BF16)
    P.copy("vector", hselb, cst[:, CST_HSEL:CST_HSEL + 128])
    vtm = A.ring("vtm", 1, [128, D], F32R)
    sgt = A.ring("sgt", 1, [128, D], BF16)
    bon = A.ring("bon", 2, [128, 16])
    w = lambda nm, n=2, dt=F32: A.ring(nm, n, [128, NB], dt)
    r_sb, k_sb, kkf, sq, nmx, rn, kk = w("r_sb"), w("k_sb"), w("kkf", 1), w("sq", 1, F32R), w("nmx", 1), w("rn", 1), w("kk")
    ew, sw, ea, av, cum, epos, eneg, dprev, eprev = (w("ew", 1), w("sw"), w("ea", 1), w("av"), w("cum"), w("epos"),
                                                     w("eneg"), w("dprev", 1), w("eprev"))
    mfac, kd, bv, ktil, btil, atil = w("mfac", 1), w("kd"), w("bv"), w("ktil"), w("btil"), w("atil")
    fmo = A.ring("fmo", 2, [128, 2, 4, 128], BF16)
    tmo = A.ring("tmo", 2, [128, 2, 3, 128], F32R)

    blocks = [(1, 0, 0, K.CO)]
    for i in range(T // NB):
        blocks.append((0, L + i * NB, L + i * NB, K.XO + i * NB))
    blocks[0] = (1, 0, 0, K.CO)
    assert L == NB

    for (wh, t0, _, c0) in blocks:
        s0 = t0 // 128
        hb = hbr.next()
        P.dma("sync", hb, K.hT_d[:, :, c0 - 1:c0 + NB + 1])
        for f in range(8):
            tx = tmpx.next()
            P.tt("gpsimd", tx, hb[:, f, 0:NB], hb[:, f, 2:NB + 2], ALU.add)
            P.ts("gpsimd", tx, tx, 0.5, ALU.mult)
            P.tt("gpsimd", xx[:, f, :], tx, hb[:, f, 1:NB + 1], ALU.subtract)
            for j in range(6):
                if P.rr("lerp", ["vector", "gpsimd", "vector"]) == "vector":
                    P.stt("vector", lerp[j][:, f, :], xx[:, f, :],
                          pv[:, PV_MU + 8 * j + f:PV_MU + 8 * j + f + 1], hb[:, f, 1:NB + 1], ALU.mult, ALU.add)
                else:
                    tl = tmpx.next()
                    P.ts("gpsimd", tl, xx[:, f, :], pv[:, PV_MU + 8 * j + f:PV_MU + 8 * j + f + 1], ALU.mult)
                    P.tt("gpsimd", lerp[j][:, f, :], tl, hb[:, f, 1:NB + 1], ALU.add)
        for tt in range(NB // 128):
            for (j, kind) in ((2, "v"), (3, "g")):
                dst = vtm.next() if kind == "v" else sgt.next()
                for ch in range(2):
                    ps = K.pb[ch]
                    for f in range(8):
                        P.mm(ps, lerp[j][:, f, tt * 128:(tt + 1) * 128], W[j][:, f, ch * 512:(ch + 1) * 512],
                             start=(f == 0), stop=(f == 7))
                    if kind == "v":
                        P.copy("scalar", dst[:, ch * 512:(ch + 1) * 512], ps)
                    else:
                        P.act(dst[:, ch * 512:(ch + 1) * 512], ps, AF.Silu)
                if kind == "v":
                    P.dma("sync", K.v_d[t0 + tt * 128:t0 + (tt + 1) * 128, :], dst)
                else:
                    P.dma("sync", K.sg_d[t0 + tt * 128:t0 + (tt + 1) * 128, :], dst)
        ps = K.pb[2]
        for f in range(8):
            P.mm(ps[:, 0:NB], W1c[:, f, :], lerp[4][:, f, :], start=(f == 0), stop=(f == 7))
        P.act(twb, ps[:, 0:NB], AF.Tanh)
        for f in range(8):
            P.mm(ps[:, NB:2 * NB], A1c[:, f, :], lerp[5][:, f, :], start=(f == 0), stop=(f == 7))
        P.copy("vector", tab, ps[:, NB:2 * NB])
        for g in range(8):
            gs = slice(g * 128, (g + 1) * 128)
            pr = K.pb[3]
            for f in range(8):
                P.mm(pr[:, 0:NB], W[0][:, f, gs], lerp[0][:, f, :], start=(f == 0), stop=(f == 7))
            for f in range(8):
                P.mm(pr[:, NB:2 * NB], W[1][:, f, gs], lerp[1][:, f, :], start=(f == 0), stop=(f == 7))
            r_, k_ = r_sb.next(), k_sb.next()
            P.copy("scalar", r_, pr[:, 0:NB])
            P.copy("scalar", k_, pr[:, NB:2 * NB])
            kf, sq_, nm, rn_, kk_ = kkf.next(), sq.next(), nmx.next(), rn.next(), kk.next()
            P.ts("gpsimd", kf, k_, pv[:, PV_KK + g:PV_KK + g + 1], ALU.mult)
            P.tt("gpsimd", sq_, kf, kf, ALU.mult)
            pn = K.pb[4]
            P.mm(pn[:, 0:NB], bones_r, sq_)
            P.ts("vector", nm, pn[:, 0:NB], 1e-24, ALU.max)
            P.act(rn_, nm, AF.Ln)
            P.act(rn_, rn_, AF.Exp, scale=-0.5)
            P.tt("gpsimd", kk_, kf, rn_, ALU.mult)
            for d in range(2):
                hs = slice(64 * d, 64 * d + 64)
                pl = K.pb[5 + (d % 2)]
                P.mm(pl[:, 0:NB], W2c[hs, gs], twb[hs, :])
                P.mm(pl[:, NB:2 * NB], A2c[hs, gs], tab[hs, :])
                ew_, sw_, ea_, a_ = ew.next(), sw.next(), ea.next(), av.next()
                P.act(ew_, pl[:, 0:NB], AF.Exp, bias=K.negw0[:, 8 * d + g:8 * d + g + 1], scale=-1.0)
                P.act(ea_, pl[:, NB:2 * NB], AF.Exp, bias=K.nega0[:, 8 * d + g:8 * d + g + 1], scale=-1.0)
                P.act(ew_, ew_, AF.Identity, bias=1.0)
                P.recip(sw_, ew_)
                P.act(ea_, ea_, AF.Identity, bias=1.0)
                P.recip(a_, ea_)
                cm = cum.next()
                if d == 0:
                    P._c("vector", "tensor_tensor_scan", ["out"],
                         dict(out=cm, data0=scanm, data1=sw_, initial=0.0, op0=ALU.mult, op1=ALU.add))
                else:
                    P._c("vector", "tensor_tensor_scan", ["out"],
                         dict(out=cm[:, NB - 1::-1], data0=scanm, data1=sw_[:, NB - 1::-1], initial=0.0,
                              op0=ALU.mult, op1=ALU.add))
                ep, en, dp, epv = epos.next(), eneg.next(), dprev.next(), eprev.next()
                P.act(ep, cm, AF.Exp, scale=-C0)
                P.act(en, cm, AF.Exp, scale=C0)
                P.tt("gpsimd", dp, cm, sw_, ALU.subtract)
                P.act(epv, dp, AF.Exp, scale=-C0)
                for s in range(NB // 128):
                    cc = s * 128 + (127 if d == 0 else 0)
                    P.copy("gpsimd", K.gam[:, d, g, s0 + s:s0 + s + 1], ep[:, cc:cc + 1])
                mf, kd_, b_ = mfac.next(), kd.next(), bv.next()
                P.ts("gpsimd", mf, a_, pv[:, PV_KA + g:PV_KA + g + 1], ALU.mult, K.omka[:, g:g + 1], ALU.add)
                P.tt("gpsimd", kd_, k_, mf, ALU.mult)
                P.tt("gpsimd", b_, kk_, a_, ALU.mult)
                kt_, bt_, at_ = ktil.next(), btil.next(), atil.next()
                P.tt("vector", kt_, kd_, en, ALU.mult)
                P.tt("vector", bt_, b_, en, ALU.mult)
                P.stt("vector", at_, kk_, -1.0, epv, ALU.mult, ALU.mult)
                tp = tmpx.next()
                P.ts("gpsimd", tp, r_, pv[:, PV_RK + g:PV_RK + g + 1], ALU.mult)
                P.tt("gpsimd", prod[:, d, g, :], tp, kd_, ALU.mult)
                fo = fmo.next()
                fo3 = lambda arr: fo[:, :, arr, :]
                P.copy("scalar", fo3(0), bt_.re("p (s t) -> p s t", t=128))
                P.copy("scalar", fo3(1), kt_.re("p (s t) -> p s t", t=128))
                P.copy("gpsimd", fo3(2), at_.re("p (s t) -> p s t", t=128))
                P.tt("vector", fo3(3), r_.re("p (s t) -> p s t", t=128), ep.re("p (s t) -> p s t", t=128), ALU.mult)
                P.dma("sync", K.fm_d[d][:, g, s0:s0 + NB // 128, :, :], fo)
                to = tmo.next()
                pt, pq = K.pb[d % 2], K.pb[2 + (d % 2)]
                for tt in range(NB // 128):
                    for ai, src in enumerate((at_, bt_, kt_)):
                        idx = tt * 3 + ai
                        dst = pt[:, idx * 128:(idx + 1) * 128] if idx < 4 else pq[:, (idx - 4) * 128:(idx - 3) * 128]
                        P.tr(dst, src[:, tt * 128:(tt + 1) * 128], cst[:, CST_IDENT:CST_IDENT + 128])
                tof = to.re("p t a f -> p (t a f)")
                P.copy("scalar", tof[:, 0:512], pt)
                P.copy("scalar", tof[:, 512:768], pq[:, 0:256])
                for tt in range(NB // 128):
                    P.dma("sync", K.tm_d[d][t0 + tt * 128:t0 + (tt + 1) * 128, :, gs], to[:, tt, :, :])
        for tt in range(NB // 128):
            pbn = K.pb[4]
            n = 0
            for d in range(2):
                for g in range(8):
                    P.mm(pbn[:, 256 + 16 * tt:256 + 16 * tt + 16], prod[:, d, g, tt * 128:(tt + 1) * 128],
                         hselb[:, 16 * g:16 * g + 16], start=(n == 0), stop=(n == 15))
                    n += 1
            b_t = bon.next()
            P.copy("vector", b_t, pbn[:, 256 + 16 * tt:256 + 16 * tt + 16])
            P.dma("sync", K.bonus_d[t0 + tt * 128:t0 + (tt + 1) * 128, :], b_t)


def stage_r2(K):
    P, A, L, T, NS = K.P, K.A, K.L, K.T, K.NS
    cst = K.cst
    m512 = [A.alloc("m512", [128, 2, 256]) for d in range(2)]
    for d in range(2):
        for r in range(2):
            P.copy("gpsimd", m512[d][:, r, :], cst[:, CST_M01 + 256 * d:CST_M01 + 256 * d + 256])
    i64 = A.alloc("i64", [128, 64])
    for p in range(2):
        P.copy("gpsimd", i64[64 * p:64 * p + 64, :], cst[64 * p:64 * p + 64, CST_IDENT + 64 * p:CST_IDENT + 64 * p + 64])
    tmbr = A.ring("tmb", 2, [128, 3, D], F32R)
    vbr = A.ring("vb", 2, [128, D], F32R)
    fmbr = A.ring("fmb", 3, [128, 4, 128], BF16)
    sabr = A.ring("sab", 2, [128, 2, 512], F32)
    p0r = A.ring("p0", 2, [128, 2, 128], F32)
    xr = A.ring("xp", 4, [128, 2, 128], F32)
    rpr = A.ring("rp", 3, [128, 2, 256], F32)
    igr = A.ring("ig", 2, [128, 128], F32R)
    ahr = A.ring("ah", 2, [128, 128], F32)
    rhr = A.ring("rh", 2, [128, 256], F32R)
    ysr = A.ring("ys", 2, [128, D])
    Mst = [A.alloc("Mst", [128, 8, 2, 64], F32R) for _ in range(2)]
    for b in igr.bufs + rhr.bufs + Mst:
        P.memset("gpsimd", b, 0.0)
    pb = K.pb
    PAB = [pb[p].v for p in range(2)]
    PC = [pb[2][:, p * 128:(p + 1) * 128] for p in range(2)]
    PXi = [pb[2][:, 256 + p * 64:256 + (p + 1) * 64] for p in range(2)]
    PG = pb[2][:, 384:512]
    if R2_SPLIT:
        PX = [pb[3 + p][:, 0:128] for p in range(2)]
        PRP = [pb[3 + p][:, 128:384] for p in range(2)]
        PRh = [pb[3 + p][:, 384:512] for p in range(2)]
    else:
        PX = [pb[3][:, p * 128:(p + 1) * 128] for p in range(2)]
        PRh = [pb[3][:, 256 + 128 * p:384 + 128 * p] for p in range(2)]
        PRP = [pb[4][:, p * 256:(p + 1) * 256] for p in range(2)]
    PY = [pb[5], pb[6]]
    PM = pb[7][:, 0:128]
    nctx = L // 128
    for d in range(2):
        order = list(range(NS)) if d == 0 else (list(range(nctx - 1, -1, -1)) + list(range(NS - 1, nctx - 1, -1)))
        mi = 0
        P.memset("gpsimd", Mst[0], 0.0)
        P.memset("gpsimd", Mst[1], 0.0)
        for s in order:
            Mcur, Mnew = Mst[mi % 2], Mst[(mi + 1) % 2]
            mi += 1
            tmb, vb = tmbr.next(), vbr.next()
            P.dma("sync", tmb, K.tm_d[d][s * 128:(s + 1) * 128, :, :])
            P.dma("sync", vb, K.v_d[s * 128:(s + 1) * 128, :])
            for g in range(8):
                fmb = fmbr.next()
                P.dma("sync", fmb, K.fm_d[d][:, g, s, :, :])
                sab, p0, x = sabr.next(), p0r.next(), xr.next()
                if R2CUT < 1:
                    continue
                for p in range(2):
                    bs = slice(64 * p, 64 * p + 64)
                    ar = fmb.re("p a t -> p (a t)")[bs, 256:512]
                    P.mm(PAB[p][:, 0:256], fmb[bs, 0, :], ar)
                    P.mm(PAB[p][:, 256:512], fmb[bs, 1, :], ar)

                if R2CUT < 1.2:
                    continue
                for p in range(2):
                    hc = slice(64 * (2 * g + p), 64 * (2 * g + p) + 64)
                    P.tt("vector", sab[:, p, :], PAB[p], m512[d].re("p r c -> p (r c)"), ALU.mult)
                    P.tr(PC[p], sab[:, p, 0:128], cst[:, CST_IDENT:CST_IDENT + 128])
                    P.copy("vector", p0[:, p, :], PC[p])
                    if R2CUT < 1.4:
                        continue
                    P.copy("gpsimd", x[:, p, 0:64], tmb[:, 0, hc])
                    if R2CUT < 1.6:
                        continue
                    P.mm(PXi[p], sab[:, p, 256:384], vb[:, hc].cast(F32))
                    P.copy("scalar", x[:, p, 64:128], PXi[p])
                Rk = [sab[:, p, 0:128] for p in range(2)]
                Pk = [p0[:, p, :] for p in range(2)]
                if R2CUT < 2:
                    continue
                for k in range(7):
                    xn = xr.next()
                    rp = rpr.next() if k < 6 else None
                    for p in range(2):
                        P.mm(PX[p], Rk[p], x[:, p, :])
                        if k < 6:
                            P.mm(PRP[p][:, 0:128], Pk[p], Rk[p])
                            if k < 5:
                                P.mm(PRP[p][:, 128:256], Rk[p], Pk[p])
                    for p in range(2):
                        P.tt("vector", xn[:, p, :], PX[p], x[:, p, :], ALU.add)
                        if k < 5:
                            P.copy("scalar", rp[:, p, :], PRP[p])
                        elif k == 5:
                            P.copy("scalar", rp[:, p, 0:128], PRP[p][:, 0:128])
                    x = xn
                    if k < 6:
                        Rk = [rp[:, p, 0:128] for p in range(2)]
                        Pk = [rp[:, p, 128:256] for p in range(2)]
                if R2CUT < 3:
                    continue
                ig, rh = igr.next(), rhr.next()
                gc = slice(g * 128, (g + 1) * 128)
                ah = ahr.next()
                for p in range(2):
                    P.copy("gpsimd", ah[:, 64 * p:64 * p + 64], x[:, p, 0:64])
                P.mm(PG, ah, tmb[:, 1, gc].cast(F32))
                for p in range(2):
                    P.mm(PRh[p], ah, sab[:, p, 128:256])
                for p in range(2):
                    bs = slice(64 * p, 64 * p + 64)
                    P.tt("vector", ig[bs, 64 * p:64 * p + 64], PG[bs, 64 * p:64 * p + 64], i64[bs, :], ALU.add)
                    P.tt("vector", rh[bs, 128 * p:128 * p + 128], PRh[p][bs, :], fmb[bs, 3, :], ALU.add)
                if R2CUT < 4:
                    continue
                for p in range(2):
                    h = 2 * g + p
                    hc = slice(64 * h, 64 * h + 64)
                    py = PY[h // 8][:, (h % 8) * 64:(h % 8) * 64 + 64]
                    P.mm(py, sab[:, p, 128:256], x[:, p, 64:128], start=True, stop=False)
                    P.mm(py, sab[:, p, 384:512], vb[:, hc].cast(F32), start=False, stop=False)
                    P.mm(py, rh[:, 128 * p:128 * p + 128], Mcur[:, g, p, :], start=False, stop=True)
                if R2CUT < 5:
                    continue
                P.mm(PM[:, 0:64], tmb[:, 1, gc].cast(F32), x[:, 0, 64:128], start=True, stop=False, skip=True)
                P.mm(PM[:, 64:128], tmb[:, 1, gc].cast(F32), x[:, 1, 64:128], start=False, stop=False, skip=True)
                P.mm(PM, tmb[:, 2, gc], vb[:, gc], start=False, stop=False, skip=True)
                P.mm(PM, ig, Mcur.re("p g a i -> p g (a i)")[:, g, :], start=False, stop=True, skip=True)
                for p in range(2):
                    bs = slice(64 * p, 64 * p + 64)
                    P.act(Mnew[bs, g, p, :], PM[bs, 64 * p:64 * p + 64], AF.Identity, scale=K.gam[bs, d, g, s:s + 1])
            if R2CUT < 4:
                continue
            ys = ysr.next()
            P.copy("scalar", ys[:, 0:512], PY[0])
            P.copy("vector", ys[:, 512:1024], PY[1])
            P.dma("sync", K.y_d[d][s * 128:(s + 1) * 128, :], ys)


def bcast_row(K, name, src_v):
    n = src_v.ap.shape[-1]
    t = K.A.alloc(name, [128, n])
    K.P.dma("gpsimd", t, V(src_v.buf, src_v.ap.partition_broadcast(128)))
    return t


def stage_r3(K):
    P, A, L, T = K.P, K.A, K.L, K.T
    Wo = A.alloc("wo1", [128, 8, D], BF16)
    for f in range(8):
        P.dma("gpsimd", Wo[:, f, :], K.wout1_d[f * 128:(f + 1) * 128, :])
    lng = bcast_row(K, "lng", K.rows_d[0:1, :])
    lnb = bcast_row(K, "lnb", K.rows_d[1:2, :])
    gx = [bcast_row(K, "gres%d" % w, K.modrow_d[0, w:w + 1, 2 * D:3 * D]) for w in range(2)]
    y0r, y1r, vr, xr_ = (A.ring(n, 2, [128, D]) for n in ("y0", "y1", "vv", "xres"))
    sgr = A.ring("sgl", 2, [128, D], BF16)
    bnr = A.ring("bnl", 2, [128, 16])
    yfr, sqr, t1r = A.ring("yf", 2, [128, D]), A.ring("ysq", 2, [128, D]), A.ring("t1", 2, [128, D])
    obr = A.ring("ob", 2, [128, D], BF16)
    otr = A.ring("oT", 2, [128, 8, 128], BF16)
    str_ = A.ring("stat", 2, [128, 4, 16])
    outr = A.ring("xo", 2, [128, D])
    for (w, i, r0, tok, col) in seq_tiles(K):
        y0, y1, vv, xres, sg, bn = y0r.next(), y1r.next(), vr.next(), xr_.next(), sgr.next(), bnr.next()
        ts_ = slice(tok, tok + 128)
        P.dma("sync", y0, K.y_d[0][ts_, :])
        P.dma("sync", y1, K.y_d[1][ts_, :])
        P.dma("sync", vv, V(K.v_d, K.v_d.ap[ts_, :].bitcast(F32)))
        P.dma("sync", sg, K.sg_d[ts_, :])
        P.dma("sync", bn, K.bonus_d[ts_, :])
        P.dma("sync", xres, (K.ctx_d if w else K.x_d)[r0:r0 + 128, :])
        yf, sq, t1, st = yfr.next(), sqr.next(), t1r.next(), str_.next()
        P.tt("gpsimd", yf, y0, y1, ALU.add)
        P.act(sq, yf, AF.Square)
        h3 = lambda b: b.re("p (h k) -> p h k", k=64)
        P._c("vector", "tensor_reduce", ["out"], dict(out=st[:, 0, :], in_=h3(yf), axis=AX.X, op=ALU.add))
        P._c("vector", "tensor_reduce", ["out"], dict(out=st[:, 1, :], in_=h3(sq), axis=AX.X, op=ALU.add))
        P.ts("vector", st[:, 0, :], st[:, 0, :], 1.0 / 64, ALU.mult)
        P.tt("vector", st[:, 2, :], st[:, 0, :], st[:, 0, :], ALU.mult)
        P.stt("vector", st[:, 1, :], st[:, 1, :], 1.0 / 64, st[:, 2, :], ALU.mult, ALU.subtract)
        P.act(st[:, 3, :], st[:, 1, :], AF.Ln, bias=64e-5)
        P.act(st[:, 3, :], st[:, 3, :], AF.Exp, scale=-0.5)
        bc = lambda v_: v_.bc(2, [128, 16, 64])
        P.tt("vector", h3(t1), h3(yf), bc(st[:, 0, :]), ALU.subtract)
        P.tt("gpsimd", h3(t1), h3(t1), bc(st[:, 3, :]), ALU.mult)
        P.tt("vector", t1, t1, lng, ALU.mult)
        P.tt("gpsimd", t1, t1, lnb, ALU.add)
        P.tt("vector", h3(vv), h3(vv), bc(bn.v), ALU.mult)
        P.tt("gpsimd", t1, t1, vv, ALU.add)
        ob = obr.next()
        P.tt("vector", ob, t1, sg, ALU.mult)
        ph = K.pbh if (tok // 128) % 2 == 0 else K.pbh2
        for f in range(8):
            P.tr(ph[:, f * 128:(f + 1) * 128], ob[:, f * 128:(f + 1) * 128], K.identb)
        oT = otr.next()
        P.copy("scalar", oT.re("p f t -> p (f t)"), ph)
        for ch in range(2):
            ps = K.pb[ch]
            for f in range(8):
                P.mm(ps, oT[:, f, :], Wo[:, f, ch * 512:(ch + 1) * 512], start=(f == 0), stop=(f == 7))
        xo = outr.next()
        for ch in range(2):
            cs = slice(ch * 512, (ch + 1) * 512)
            P.tt("vector", xo[:, cs], K.pb[ch], gx[w][:, cs], ALU.mult)
        P.tt("gpsimd", xo, xo, xres, ALU.add)
        P.dma("sync", (K.ctx1_d if w else K.x1_d)[r0:r0 + 128, :], xo)


def stage_mla(K, do_attn=True, do_final=True):
    return stage_mla_impl(K, do_attn, do_final)


def host_inputs(inp, b, T, L):
    f32 = lambda a: np.ascontiguousarray(np.asarray(a, np.float32))
    pv = np.zeros((128, PV_N), np.float32)
    pv[:, PV_NG0:PV_NG0 + 8] = fm(inp["norm_g"][0])
    pv[:, PV_NG1:PV_NG1 + 8] = fm(inp["norm_g"][1])
    for j in range(6):
        pv[:, PV_MU + 8 * j:PV_MU + 8 * j + 8] = fm(inp["rwkv_mu"][0, j])
    for d in range(2):
        pv[:, PV_W0 + 8 * d:PV_W0 + 8 * d + 8] = fm(inp["rwkv_w0"][0, d])
        pv[:, PV_A0 + 8 * d:PV_A0 + 8 * d + 8] = fm(inp["rwkv_a0"][0, d])
    pv[:, PV_KK:PV_KK + 8] = fm(inp["rwkv_k_k"][0])
    pv[:, PV_KA:PV_KA + 8] = fm(inp["rwkv_k_a"][0])
    pv[:, PV_RK:PV_RK + 8] = fm(np.asarray(inp["rwkv_r_k"][0]).reshape(-1))
    pv[:, PV_QG:PV_QG + 3] = fm(inp["mla_q_norm_g"][0])
    pv[:, PV_KVG:PV_KVG + 2] = fm(inp["mla_kv_norm_g"][0])
    cvec = np.concatenate([fm(inp["c"][b]), fm(inp["c_ctx"])], axis=1)
    w_in2 = np.asarray(inp["mla_w_in"][0], np.float32)
    w_in2 = np.concatenate([w_in2, w_in2[:, 656:672], w_in2[:, 640:656]], axis=1)
    wqb = np.asarray(inp["mla_w_qb"][0], np.float32).reshape(384, 16, 96)
    wqb = np.concatenate([wqb, wqb[:, :, 80:96], wqb[:, :, 64:80]], axis=2).reshape(384, 16 * 128)
    return {
        "x": f32(inp["x"][b][:T]), "ctx": f32(inp["ctx"][b][:L]), "cvec": f32(cvec), "pvec": pv,
        "cst": make_consts(),
        "rows": f32(np.stack([inp["rwkv_lnx_g"][0], inp["rwkv_lnx_b"][0], inp["final_g"]])),
        "mod_w": f32(inp["mod_w"]), "mod_b": f32(inp["mod_b"]),
        "rwkv_w_in": f32(inp["rwkv_w_in"][0]), "rwkv_w1": f32(inp["rwkv_w1"][0]),
        "rwkv_w2": f32(np.asarray(inp["rwkv_w2"][0]).reshape(128, D)),
        "rwkv_a1": f32(inp["rwkv_a1"][0]), "rwkv_a2": f32(np.asarray(inp["rwkv_a2"][0]).reshape(128, D)),
        "rwkv_w_out": f32(inp["rwkv_w_out"][0]),
        "mla_w_in": f32(w_in2), "mla_w_qb": f32(wqb), "mla_w_kvb": f32(inp["mla_w_kvb"][0]),
        "mla_w_out": f32(inp["mla_w_out"][0]), "rope": rope_tables(T),
    }


_CACHE = {}


def kernel(**inputs):
    B, T, _ = inputs["x"].shape
    L = inputs["ctx"].shape[1]
    key = (T, L)
    if key not in _CACHE:
        _CACHE[key] = build(T, L)[0]
    nc = _CACHE[key]
    in_maps = [host_inputs(inputs, b, T, L) for b in range(B)]
    res = run_bass_kernel_spmd(nc, in_maps, core_ids=list(range(B)))
    return np.stack([np.asarray(r["out"], np.float32) for r in res.results], axis=0)


def stage_mla_impl(K, do_attn=True, do_final=True):
    P, A, L, T, NT, NS = K.P, K.A, K.L, K.T, K.NT, K.NS
    pv, cst, pb = K.pv, K.cst, K.pb
    SCL = 1.0 / math.sqrt(96.0)
    qnT = A.alloc("qnT", [128, 3, T], BF16)
    kvnT = A.alloc("kvnT", [128, 2, NT], BF16)
    KT = A.alloc("KT", [128, NT], BF16)
    ones_r = A.alloc("ones_r", [128, 128], F32R)
    ones_f = A.alloc("ones_f", [128, 64])
    P.memset("vector", ones_r, 1.0)
    P.memset("vector", ones_f, 1.0)
    mark = A.off
    Win = A.alloc("mwin", [128, 8, 1728], BF16)
    for f in range(8):
        P.dma("gpsimd", Win[:, f, :], K.mw_in_d[f * 128:(f + 1) * 128, :])
    hbr = A.ring("mhb", 1, [128, 8, 512], BF16)
    qc = A.alloc("qc", [128, 5, 512])
    sqr = A.ring("msq", 1, [128, 512], F32R)
    rsr = A.ring("mrs", 1, [128, 512])
    rpr = A.ring("mrp", 1, [128, 2, 512])
    t1r = A.ring("mt1", 1, [128, 512])
    t2r = A.ring("mt2", 1, [128, 512])
    sgo = A.ring("msg", 1, [128, 8, 512], BF16)
    blocks = [(1, 0, K.CO, L, 0)] + [(0, L + i * 512, K.XO + i * 512, 512, i * 512) for i in range(T // 512)]
    for (wh, t0, c0, nb, xt0) in blocks:
        hb = hbr.next()
        P.dma("sync", hb[:, :, 0:nb], K.hT_d[:, :, c0:c0 + nb])
        tiles = ([] if wh else [0, 1, 2]) + [3, 4]
        for m in tiles:
            for f in range(8):
                P.mm(pb[m][:, 0:nb], Win[:, f, m * 128:(m + 1) * 128], hb[:, f, 0:nb], start=(f == 0), stop=(f == 7))
            P.copy("scalar", qc[:, m, 0:nb], pb[m][:, 0:nb])
        for (ms, nfeat, dst, gcol) in (([] if wh else [0, 1, 2], 384.0, qnT, PV_QG), ([3, 4], 256.0, kvnT, PV_KVG)):
            if not ms:
                continue
            for i, m in enumerate(ms):
                sq = sqr.next()
                P.tt("gpsimd", sq[:, 0:nb], qc[:, m, 0:nb], qc[:, m, 0:nb], ALU.mult)
                P.mm(pb[5][:, 0:nb], ones_r, sq[:, 0:nb], start=(i == 0), stop=(i == len(ms) - 1))
            rs = rsr.next()
            P.act(rs[:, 0:nb], pb[5][:, 0:nb], AF.Ln, bias=1e-6, scale=1.0 / nfeat)
            P.act(rs[:, 0:nb], rs[:, 0:nb], AF.Exp, scale=-0.5)
            for i, m in enumerate(ms):
                o_ = dst[:, i, xt0:xt0 + nb] if dst is qnT else dst[:, i, t0:t0 + nb]
                P.stt("vector", o_, qc[:, m, 0:nb], pv[:, gcol + i:gcol + i + 1], rs[:, 0:nb], ALU.mult, ALU.mult)
        pe, sw = pb[6][64:96, 0:nb], pb[7][64:96, 0:nb]
        for f in range(8):
            P.mm(pe, Win[:, f, 640:672], hb[:, f, 0:nb], start=(f == 0), stop=(f == 7))
        if wh:
            P.copy("scalar", KT[64:96, t0:t0 + nb], pe)
        else:
            for f in range(8):
                P.mm(sw, Win[:, f, 1696:1728], hb[:, f, 0:nb], start=(f == 0), stop=(f == 7))
            rp = rpr.next()
            P.dma("sync", rp[64:96, :, :], K.rope_d[:, :, xt0:xt0 + nb])
            t1, t2 = t1r.next(), t2r.next()
            P.tt("vector", t1[64:96, :], pe, rp[64:96, 0, :], ALU.mult)
            P.tt("vector", t2[64:96, :], sw, rp[64:96, 1, :], ALU.mult)
            P.tt("gpsimd", KT[64:96, t0:t0 + nb], t1[64:96, :], t2[64:96, :], ALU.add)
            so = sgo.next()
            for g in range(8):
                ps = pb[g % 4]
                for f in range(8):
                    P.mm(ps, Win[:, f, 672 + g * 128:672 + (g + 1) * 128], hb[:, f, :], start=(f == 0), stop=(f == 7))
                P.act(so[:, g, :], ps, AF.Silu)
            P.dma("sync", K.sgT_d[:, :, xt0:xt0 + 512], so)
    barrier(P, K.tiny)
    A.off = mark
    if not do_attn:
        return
    QT = A.alloc("QT", [128, T], BF16)
    Vh = A.alloc("Vh", [128, NS, 65], BF16)
    P.memset("vector", Vh, 1.0)
    wqr = A.ring("wq", 2, [128, 3, 128], BF16)
    wkr = A.ring("wk", 2, [128, 2, 128], BF16)
    ptr = A.ring("pt", 3, [128, 512], BF16)
    rpr = A.ring("arp", 2, [128, 2, 512])
    t1r = A.ring("at1", 2, [128, 512])
    t2r = A.ring("at2", 2, [128, 512])
    recr = A.ring("rec", 2, [128, 512])
    bcr = A.ring("bcs", 2, [64, 512])
    otr = A.ring("oth", 2, [64, 512], BF16)
    kblocks = [(k0, min(512, NT - k0)) for k0 in range(0, NT, 512)]
    for h in range(NH):
        wq, wk = wqr.next(), wkr.next()
        P.dma("gpsimd", wq, K.wqb_d.re("(k p) n -> p k n", p=128)[:, :, h * 128:(h + 1) * 128])
        P.dma("gpsimd", wk, K.wkvb_d.re("(k p) n -> p k n", p=128)[:, :, h * 128:(h + 1) * 128])
        for (k0, nk) in kblocks:
            for kt in range(2):
                P.mm(pb[0][0:64, 0:nk], wk[:, kt, 0:64], kvnT[:, kt, k0:k0 + nk], start=(kt == 0), stop=(kt == 1))
            P.copy("scalar", KT[0:64, k0:k0 + nk], pb[0][0:64, 0:nk])
        for j0 in range(0, NS, 8):
            nj = min(8, NS - j0)
            for j in range(nj):
                for kt in range(2):
                    P.mm(pb[1][:, j * 64:(j + 1) * 64], kvnT[:, kt, (j0 + j) * 128:(j0 + j + 1) * 128], wk[:, kt, 64:128],
                         start=(kt == 0), stop=(kt == 1))
            P.copy("vector", Vh[:, j0:j0 + nj, 0:64], pb[1][:, 0:nj * 64].re("p (j c) -> p j c", c=64))
        for qb in range(T // 512):
            qs = slice(qb * 512, (qb + 1) * 512)
            for kt in range(3):
                P.mm(pb[2][0:96, :], wq[:, kt, 0:96], qnT[:, kt, qs], start=(kt == 0), stop=(kt == 2))
            for kt in range(3):
                P.mm(pb[3][64:96, :], wq[:, kt, 96:128], qnT[:, kt, qs], start=(kt == 0), stop=(kt == 2))
            rp = rpr.next()
            P.dma("sync", rp[64:96, :, :], K.rope_d[:, :, qs])
            t1, t2 = t1r.next(), t2r.next()
            P.copy("scalar", QT[0:64, qs], pb[2][0:64, :])
            P.tt("vector", t1[64:96, :], pb[2][64:96, :], rp[64:96, 0, :], ALU.mult)
            P.tt("vector", t2[64:96, :], pb[3][64:96, :], rp[64:96, 1, :], ALU.mult)
            P.tt("gpsimd", QT[64:96, qs], t1[64:96, :], t2[64:96, :], ALU.add)
        for qb in range(T // 512):
            qs = slice(qb * 512, (qb + 1) * 512)
            po = pb[4 + qb % 2]
            LOOK = ATT_LOOK
            for kt in range(min(LOOK, NS)):
                P.mm(pb[kt % 4], KT[0:96, kt * 128:(kt + 1) * 128], QT[0:96, qs])
            for kt in range(NS):
                if kt + LOOK < NS:
                    k2 = kt + LOOK
                    P.mm(pb[k2 % 4], KT[0:96, k2 * 128:(k2 + 1) * 128], QT[0:96, qs])

                pt = ptr.next()
                P.act(pt, pb[kt % 4], AF.Exp, scale=SCL)
                P.mm(po[0:65, :], Vh[:, kt, :], pt, start=(kt == 0), stop=(kt == NS - 1))
            rec, bcs, ot = recr.next(), bcr.next(), otr.next()
            P.recip(rec[64:65, :], po[64:65, :])
            P.mm(pb[6][0:64, :], ones_f[64:65, :], rec[64:65, :])
            P.copy("scalar", bcs, pb[6][0:64, :])
            P.tt("vector", ot, po[0:64, :], bcs, ALU.mult)
            P.dma("sync", K.oT_d[64 * (h % 2):64 * (h % 2) + 64, h // 2, qs], ot)
    barrier(P, K.tiny)
    A.off = A.base
    if not do_final:
        return
    Wo = A.alloc("wo2", [128, 8, D], BF16)
    for f in range(8):
        P.dma("gpsimd", Wo[:, f, :], K.wout2_d[f * 128:(f + 1) * 128, :])
    gx2 = bcast_row(K, "gx2", K.modrow_d[1, 0:1, 2 * D:3 * D])
    fing = bcast_row(K, "fing", K.rows_d[2:3, :])
    obr = A.ring("fo", 2, [128, 8, 512], BF16)
    sbr = A.ring("fs", 2, [128, 8, 512], BF16)
    ogr = A.ring("fg", 2, [128, 8, 512], BF16)
    x1r = A.ring("fx", 2, [128, D])
    x2r = A.ring("fx2", 2, [128, D])
    jr = A.ring("fj", 1, [128, D], BF16)
    ssr = A.ring("fss", 4, [128, 2])
    for qb in range(T // 512):
        qs = slice(qb * 512, (qb + 1) * 512)
        ob, sb, og = obr.next(), sbr.next(), ogr.next()
        P.dma("sync", ob, K.oT_d[:, :, qs])
        P.dma("sync", sb, K.sgT_d[:, :, qs])
        P.tt("gpsimd", og, ob, sb, ALU.mult)
        for tt in range(4):
            r0 = qb * 512 + tt * 128
            x1 = x1r.next()
            P.dma("sync", x1, K.x1_d[r0:r0 + 128, :])
            for ch in range(2):
                for f in range(8):
                    P.mm(pb[ch], og[:, f, tt * 128:(tt + 1) * 128], Wo[:, f, ch * 512:(ch + 1) * 512], start=(f == 0), stop=(f == 7))
            x2 = x2r.next()
            for ch in range(2):
                cs = slice(ch * 512, (ch + 1) * 512)
                P.tt("vector", x2[:, cs], pb[ch], gx2[:, cs], ALU.mult)
            P.tt("gpsimd", x2, x2, x1, ALU.add)
            ss = ssr.next()
            P.act(jr.next(), x2, AF.Square, accum_out=ss[:, 0:1])
            P.act(ss[:, 1:2], ss[:, 0:1], AF.Ln, bias=1e-6, scale=1.0 / D)
            P.act(ss[:, 1:2], ss[:, 1:2], AF.Exp, scale=-0.5)
            P.stt("vector", x2, x2, ss[:, 1:2], fing, ALU.mult, ALU.mult)
            P.dma("sync", K.out_d[r0:r0 + 128, :], x2)
```

```python
import contextlib
import math
import numpy as np
import concourse.bass as bass
import concourse.mybir as mybir
from concourse.bass_utils import run_bass_kernel_spmd

F32 = mybir.dt.float32
F32R = mybir.dt.float32
BF16 = mybir.dt.bfloat16
ALU = mybir.AluOpType
AF = mybir.ActivationFunctionType
AX = mybir.AxisListType

N_DMA_SEMS = 8
DEBUG_SRC = False
R2CUT = 99
ATT_LOOK = 3
R2_SPLIT = False
R2VAR = 0
NO_SELF_SYNC = ("tensor",)

D = 1024
NH = 16
C0 = math.exp(-0.5)


class V:
    __slots__ = ("buf", "ap")

    def __init__(self, buf, ap):
        self.buf = buf
        self.ap = ap

    def __getitem__(self, k):
        return V(self.buf, self.ap[k])

    def re(self, s, **kw):
        return V(self.buf, self.ap.rearrange(s, **kw))

    def bc(self, axis, shape):
        return V(self.buf, self.ap.unsqueeze(axis).to_broadcast(list(shape)))

    def cast(self, dt):
        return V(self.buf, self.ap.bitcast(dt))


class Buf:
    __slots__ = ("name", "w", "r", "ap")

    def __init__(self, name, ap):
        self.name = name
        self.w = None
        self.r = []
        self.ap = ap

    def __getitem__(self, k):
        return V(self, self.ap[k])

    @property
    def v(self):
        return V(self, self.ap)

    def re(self, s, **kw):
        return V(self, self.ap.rearrange(s, **kw))


def _v(x):
    return x.v if isinstance(x, Buf) else x


class Op:
    __slots__ = ("eng", "fn", "deps", "sig", "cnt", "dma_sem", "dma_val", "dma_prev", "src")

    def __init__(self, eng, fn):
        self.src = None
        self.eng = eng
        self.fn = fn
        self.deps = []
        self.sig = False
        self.cnt = 0
        self.dma_sem = None
        self.dma_val = 0
        self.dma_prev = None


class Eng:
    def __init__(self, name):
        self.name = name
        self.ops = []
        self.sem = None
        self.dma_sems = []
        self.dma_uses = [0] * N_DMA_SEMS
        self.dma_last = [None] * N_DMA_SEMS
        self.n_dma = 0


class Prog:
    def __init__(self, nc):
        self.nc = nc
        self.engs = {n: Eng(n) for n in ("tensor", "vector", "scalar", "gpsimd", "sync")}
        self.stack = contextlib.ExitStack()
        self.n_ops = 0
        self._rr = {}

    def sbuf(self, name, shape, dtype=F32):
        t = self.stack.enter_context(self.nc.sbuf_tensor(name, list(shape), dtype))
        return Buf(name, t[:])

    def psum(self, name, shape, dtype=F32):
        t = self.stack.enter_context(self.nc.psum_tensor(name, list(shape), dtype))
        return Buf(name, t[:])

    def dram(self, name, shape, dtype=F32, kind="Internal"):
        t = self.nc.dram_tensor(name, list(shape), dtype, kind=kind)
        return Buf(name, t.ap())

    def sub(self, buf, name, key):
        return Buf(name, buf.ap[key])

    def ring(self, name, n, shape, dtype=F32, space="sbuf"):
        mk = self.sbuf if space == "sbuf" else self.psum
        return Ring([mk("%s%d" % (name, i), shape, dtype) for i in range(n)])

    def op(self, eng, fn, reads=(), writes=()):
        e = self.engs[eng]
        o = Op(e, fn)
        if DEBUG_SRC:
            import sys as _s
            fr = _s._getframe(1)
            while fr.f_code.co_name in ("op", "_c", "mm", "tr", "act", "tt", "ts", "stt", "copy", "recip", "memset", "dma"):
                fr = fr.f_back
            o.src = fr.f_lineno
        deps = {}
        for b in reads:
            if b.w is not None:
                deps[id(b.w)] = b.w
        for b in writes:
            if b.w is not None:
                deps[id(b.w)] = b.w
            for r in b.r:
                deps[id(r)] = r
        for d in deps.values():
            if d.dma_sem is None and d.eng is e and e.name in NO_SELF_SYNC:
                continue
            o.deps.append(d)
            if d.dma_sem is None:
                d.sig = True
        for b in reads:
            b.r.append(o)
        for b in writes:
            b.w = o
            b.r = []
        e.ops.append(o)
        self.n_ops += 1
        return o

    def dma(self, eng, out, in_, **kw):
        out, in_ = _v(out), _v(in_)
        e = self.engs[eng]
        s = e.n_dma % N_DMA_SEMS
        e.n_dma += 1
        o = self.op(eng, ("dma", out.ap, in_.ap, kw), [in_.buf], [out.buf])
        o.dma_sem = s
        e.dma_uses[s] += 1
        o.dma_val = 16 * e.dma_uses[s]
        o.dma_prev = e.dma_last[s]
        e.dma_last[s] = o
        return o

    def _c(self, eng, method, outs, kw, extra_reads=()):
        reads, writes, args = list(extra_reads), [], {}
        for k, a in kw.items():
            if isinstance(a, (V, Buf)):
                a = _v(a)
                (writes if k in outs else reads).append(a.buf)
                args[k] = a.ap
            else:
                args[k] = a
        return self.op(eng, lambda q: getattr(q, method)(**args), reads, writes)

    def mm(self, out, lhsT, rhs, start=True, stop=True, skip=False):
        out, lhsT, rhs = _v(out), _v(lhsT), _v(rhs)
        oa, la, ra = out.ap, lhsT.ap, rhs.ap
        assert len(ra.shape) == 2 and len(la.shape) == 2 and len(oa.shape) == 2, (oa.shape, la.shape, ra.shape)
        return self.op("tensor", lambda q: q.matmul(oa, lhsT=la, rhs=ra, start=start, stop=stop,
                                                    skip_group_check=skip),
                       [lhsT.buf, rhs.buf], [out.buf])

    def tr(self, out, in_, ident):
        out, in_, ident = _v(out), _v(in_), _v(ident)
        oa, ia, da = out.ap, in_.ap, ident.ap
        return self.op("tensor", lambda q: q.transpose(out=oa, in_=ia, identity=da),
                       [in_.buf, ident.buf], [out.buf])

    def act(self, out, in_, func, bias=0.0, scale=1.0, accum_out=None, eng="scalar"):
        kw = dict(out=out, in_=in_, func=func, bias=bias, scale=scale)
        outs = ["out"]
        if accum_out is not None:
            kw["accum_out"] = accum_out
            outs.append("accum_out")
        return self._c("scalar", "activation", outs, kw)

    def tt(self, eng, out, in0, in1, op):
        return self._c(eng, "tensor_tensor", ["out"], dict(out=out, in0=in0, in1=in1, op=op))

    def ts(self, eng, out, in0, s1, op0, s2=None, op1=None):
        kw = dict(out=out, in0=in0, scalar1=s1, scalar2=s2, op0=op0)
        if op1 is not None:
            kw["op1"] = op1
        return self._c(eng, "tensor_scalar", ["out"], kw)

    def stt(self, eng, out, in0, scalar, in1, op0, op1):
        return self._c(eng, "scalar_tensor_tensor", ["out"],
                       dict(out=out, in0=in0, scalar=scalar, in1=in1, op0=op0, op1=op1))

    def copy(self, eng, out, in_):
        if eng == "scalar":
            return self._c("scalar", "copy", ["out"], dict(out=out, in_=in_))
        return self._c(eng, "tensor_copy", ["out"], dict(out=out, in_=in_))

    def recip(self, out, in_):
        return self._c("vector", "reciprocal", ["out"], dict(out=out, in_=in_))

    def memset(self, eng, out, val):
        out = _v(out)
        oa = out.ap
        return self.op(eng, lambda q: q.memset(oa, val), [], [out.buf])

    def rr(self, key, engs):
        i = self._rr.get(key, 0)
        self._rr[key] = i + 1
        return engs[i % len(engs)]

    def finish(self):
        nc = self.nc
        st = self.stack
        for e in self.engs.values():
            e.sem = st.enter_context(nc.semaphore("s_" + e.name))
            if e.n_dma:
                e.dma_sems = [st.enter_context(nc.semaphore("d_%s%d" % (e.name, i)))
                              for i in range(N_DMA_SEMS)]
        for e in self.engs.values():
            c = 0
            for o in e.ops:
                if o.dma_sem is None and o.sig:
                    c += 1
                    o.cnt = c
        block = st.enter_context(nc.Block())
        prog = self

        def replay(e, q):
            waited = {}

            def wait(sem, key, val):
                if waited.get(key, 0) < val:
                    q.wait_ge(sem, val)
                    waited[key] = val

            for o in e.ops:
                for d in o.deps:
                    if d.dma_sem is not None:
                        wait(d.eng.dma_sems[d.dma_sem], (d.eng.name, d.dma_sem), d.dma_val)
                    else:
                        wait(d.eng.sem, d.eng.name, d.cnt)
                if o.dma_sem is not None:
                    p = o.dma_prev
                    if p is not None:
                        wait(e.dma_sems[p.dma_sem], (e.name, p.dma_sem), p.dma_val)
                    _, oa, ia, kw = o.fn
                    q.dma_start(out=oa, in_=ia, **kw).then_inc(e.dma_sems[o.dma_sem], 16)
                elif o.fn is not None:
                    ins = o.fn(q)
                    if o.sig:
                        ins.then_inc(e.sem, 1)
            if e.name == "sync":
                for e2 in prog.engs.values():
                    if e2.n_dma:
                        for s in range(N_DMA_SEMS):
                            lo = e2.dma_last[s]
                            if lo is not None:
                                wait(e2.dma_sems[s], (e2.name, s), lo.dma_val)

        engs = self.engs

        @block.tensor
        def _(q):
            replay(engs["tensor"], q)

        @block.vector
        def _(q):
            replay(engs["vector"], q)

        @block.scalar
        def _(q):
            replay(engs["scalar"], q)

        @block.gpsimd
        def _(q):
            replay(engs["gpsimd"], q)

        @block.sync
        def _(q):
            replay(engs["sync"], q)

        st.close()


class Ring:
    def __init__(self, bufs):
        self.bufs = bufs
        self.i = 0

    def next(self):
        b = self.bufs[self.i % len(self.bufs)]
        self.i += 1
        return b


DT_SIZE = {F32: 4, F32R: 4, BF16: 2}


class Arena:
    def __init__(self, P, nbytes):
        self.P = P
        self.n4 = nbytes // 4
        t = P.stack.enter_context(P.nc.sbuf_tensor("arena", [128, self.n4], F32))
        self.ap = t[:]
        self.base = 0
        self.off = 0
        self.k = 0

    def persist(self):
        self.base = self.off

    def reset(self):
        self.off = self.base

    def alloc(self, name, shape, dtype=F32, parts=128):
        free = int(np.prod(shape[1:]))
        nb = free * DT_SIZE[dtype]
        n4 = (nb + 31) // 32 * 8
        assert self.off + n4 <= self.n4, "arena overflow at %s (%d KB)" % (name, (self.off + n4) * 4 // 1024)
        ap = self.ap[0:shape[0], self.off:self.off + n4]
        self.off += n4
        if dtype != F32:
            ap = ap.bitcast(dtype)
        ap = ap[:, 0:free]
        if len(shape) == 3:
            ap = ap.rearrange("p (a b) -> p a b", b=shape[2])
        elif len(shape) == 4:
            ap = ap.rearrange("p (a b c) -> p a b c", b=shape[2], c=shape[3])
        self.k += 1
        return Buf("%s_%d" % (name, self.k), ap)

    def ring(self, name, n, shape, dtype=F32):
        return Ring([self.alloc(name, shape, dtype) for _ in range(n)])


def barrier(P, tiny):
    firsts = [P.memset("vector", tiny[0], 0.0), P.memset("gpsimd", tiny[1], 0.0), P.act(tiny[2], tiny[3], AF.Copy)]
    dmas = []
    for e in P.engs.values():
        for s in range(N_DMA_SEMS):
            if e.dma_last[s] is not None:
                dmas.append(e.dma_last[s])
    for f in firsts:
        f.sig = True
    for name, e in P.engs.items():
        o = Op(e, None)
        o.deps = [f for f in firsts if f.eng is not e] + dmas
        e.ops.append(o)


CST_IDENT, CST_BONES, CST_M01, CST_MT, CST_HSEL, CST_SCAN, CST_N = 0, 128, 256, 768, 1024, 1152, 1408


def make_consts():
    c = np.zeros((128, CST_N), np.float32)
    p = np.arange(128)
    c[:, CST_IDENT:CST_IDENT + 128] = np.eye(128)
    c[:, CST_BONES:CST_BONES + 128] = (p[:, None] // 64 == p[None, :] // 64)
    s, t = p[:, None], p[None, :]
    c[:, CST_M01 + 0:CST_M01 + 128] = (t > s)
    c[:, CST_M01 + 128:CST_M01 + 256] = (t >= s)
    c[:, CST_M01 + 256:CST_M01 + 384] = (t < s)
    c[:, CST_M01 + 384:CST_M01 + 512] = (t <= s)
    c[:, CST_MT:CST_MT + 128] = (p[None, :] < p[:, None])
    c[:, CST_MT + 128:CST_MT + 256] = (p[None, :] > p[:, None])
    for g in range(8):
        for h in range(16):
            c[:, CST_HSEL + g * 16 + h] = (h == 2 * g + p // 64)
    sm = np.ones((128, 256), np.float32)
    sm[:, 0] = 0
    sm[:, 128] = 0
    c[:, CST_SCAN:CST_SCAN + 256] = sm
    return c


def fm(vec):
    v = np.asarray(vec, np.float32).reshape(-1, 128)
    return np.ascontiguousarray(v.T)


PV_NG0, PV_NG1, PV_MU, PV_W0, PV_A0, PV_KK, PV_KA, PV_RK, PV_QG, PV_KVG, PV_N = 0, 8, 16, 64, 80, 96, 104, 112, 120, 123, 128


def rope_tables(T):
    rows = T // 64
    row = np.repeat(np.arange(rows), 64).astype(np.float32)
    col = np.tile(np.arange(64), rows).astype(np.float32)
    inv = (1.0 / (10000.0 ** (np.arange(0, 16, 2, dtype=np.float32) / 16))).astype(np.float32)
    ang = np.concatenate([row[:, None] * inv, col[:, None] * inv], axis=-1).astype(np.float32)
    cos, sin = np.cos(ang).T.astype(np.float32), np.sin(ang).T.astype(np.float32)
    out = np.zeros((32, 2, T), np.float32)
    out[0:16, 0], out[16:32, 0] = cos, cos
    out[0:16, 1], out[16:32, 1] = -sin, sin
    return out


class Ctx:
    pass


def build(T, L, stages=("mod", "r1a", "r1b", "r2", "r3", "m1a", "m1b", "m2", "m3"), dbg=()):
    nc = bass.Bass("TRN2", target_bir_lowering=False)
    nc.dge_precook = False
    P = Prog(nc)
    K = Ctx()
    K.P, K.T, K.L = P, T, L
    NT = L + T
    NS = NT // 128
    K.NT, K.NS = NT, NS
    K.CO, K.XO, K.NTP = 1, L + 3, L + T + 4

    def ext(name, shape, dt=F32):
        return P.dram(name, shape, dt, kind="ExternalInput")

    K.x_d = ext("x", [T, D])
    K.ctx_d = ext("ctx", [L, D])
    K.cvec_d = ext("cvec", [128, 16])
    K.pvec_d = ext("pvec", [128, PV_N])
    K.cst_d = ext("cst", [128, CST_N])
    K.rows_d = ext("rows", [3, D])
    K.mod_w_d = ext("mod_w", [2, D, 3 * D])
    K.mod_b_d = ext("mod_b", [2, 3 * D])
    K.w_in_d = ext("rwkv_w_in", [4, D, D])
    K.w1_d = ext("rwkv_w1", [2, D, 64])
    K.w2_d = ext("rwkv_w2", [128, D])
    K.a1_d = ext("rwkv_a1", [2, D, 64])
    K.a2_d = ext("rwkv_a2", [128, D])
    K.wout1_d = ext("rwkv_w_out", [D, D])
    K.mw_in_d = ext("mla_w_in", [D, 1696 + 32])
    K.wqb_d = ext("mla_w_qb", [384, 16 * 128])
    K.wkvb_d = ext("mla_w_kvb", [256, 2048])
    K.wout2_d = ext("mla_w_out", [D, D])
    K.rope_d = ext("rope", [32, 2, T])
    K.out_d = P.dram("out", [T, D], F32, kind="ExternalOutput")

    okind = "ExternalOutput" if dbg else "Internal"

    def scr(name, shape, dt=F32):
        return P.dram(name, shape, dt, kind=("ExternalOutput" if name in dbg else "Internal"))

    K.modrow_d = scr("modrow", [2, 2, 3 * D])
    K.hT_d = scr("hT", [128, 8, K.NTP], BF16)
    K.fm_d = [scr("fm%d" % d, [128, 8, NS, 4, 128], BF16) for d in range(2)]
    K.tm_d = [scr("tm%d" % d, [NT, 3, D], F32R) for d in range(2)]
    K.v_d = scr("vtm", [NT, D], F32R)
    K.sg_d = scr("sg", [NT, D], BF16)
    K.bonus_d = scr("bonus", [NT, 16])
    K.y_d = [scr("y%d" % d, [NT, D]) for d in range(2)]
    K.x1_d = scr("x1", [T, D])
    K.ctx1_d = scr("ctx1", [L, D])
    K.sgT_d = scr("sgT", [128, 8, T], BF16)
    K.oT_d = scr("oT", [128, 8, T], BF16)

    A = Arena(P, 190 * 1024)
    K.A = A
    K.pb = [P.psum("pb%d" % i, [128, 512], F32) for i in range(8)]
    K.pbh = Buf("pbh", K.pb[7].ap.bitcast(BF16))
    K.pbh2 = Buf("pbh2", K.pb[6].ap.bitcast(BF16))

    K.cst = A.alloc("cst", [128, CST_N])
    K.pv = A.alloc("pv", [128, PV_N])
    K.identb = A.alloc("identb", [128, 128], BF16)
    K.cst_r = A.alloc("cstr", [128, CST_N], F32R)
    K.tiny = [A.alloc("tiny%d" % i, [128, 8]) for i in range(4)]
    K.modT = [A.alloc("modT%d" % i, [128, 24, 2]) for i in range(2)]
    K.modA = [[A.alloc("modA", [128, 8]) for w in range(2)] for i in range(2)]
    K.modB = [[A.alloc("modB", [128, 8]) for w in range(2)] for i in range(2)]
    K.negw0 = A.alloc("negw0", [128, 16])
    K.nega0 = A.alloc("nega0", [128, 16])
    K.omka = A.alloc("omka", [128, 8])
    K.gam = A.alloc("gam", [128, 2, 8, NS])
    K.zero = A.alloc("zero", [128, 8, 1], BF16)
    A.persist()
    P.dma("sync", K.cst, K.cst_d)
    P.dma("sync", K.pv, K.pvec_d)
    P.copy("vector", K.identb, K.cst[:, CST_IDENT:CST_IDENT + 128])
    P.copy("vector", K.cst_r, K.cst)
    for i in range(4):
        P.memset("vector", K.tiny[i], 0.0)
    P.memset("vector", K.zero, 0.0)
    P.ts("vector", K.negw0, K.pv[:, PV_W0:PV_W0 + 16], -1.0, ALU.mult)
    P.ts("vector", K.nega0, K.pv[:, PV_A0:PV_A0 + 16], -1.0, ALU.mult)
    P.ts("vector", K.omka, K.pv[:, PV_KA:PV_KA + 8], -1.0, ALU.mult, 1.0, ALU.add)

    def stage_end():
        barrier(P, K.tiny)
        A.reset()

    if "mod" in stages:
        stage_mod(K)
        stage_end()
    if "r1a" in stages:
        stage_hT(K, 0, K.x_d, K.ctx_d)
        stage_end()
    if "r1b" in stages:
        stage_r1b(K)
        stage_end()
    if "r2" in stages:
        stage_r2(K)
        stage_end()
    if "r3" in stages:
        stage_r3(K)
        stage_end()
    if "m1a" in stages:
        stage_hT(K, 1, K.x1_d, K.ctx1_d)
        stage_end()
    if "m1b" in stages:
        stage_mla(K, "m2" in stages, "m3" in stages)
    P.finish()
    return nc, P


def stage_mod(K):
    P, A = K.P, K.A
    cT = A.alloc("cT", [128, 16])
    sc = A.alloc("scT", [128, 8, 2])
    P.dma("sync", cT, K.cvec_d)
    sg = A.alloc("sgc", [128, 16])
    P.act(sg, cT, AF.Sigmoid)
    P.tt("vector", sc.re("p f w -> p w f"), cT.re("p (w f) -> p w f", w=2), sg.re("p (w f) -> p w f", w=2), ALU.mult)
    mwr = A.ring("mw", 3, [128, 512])
    mb = A.alloc("mb", [2, 3 * D])
    mrow = A.alloc("mrow", [2, 3 * D])
    for layer in range(2):
        P.dma("gpsimd", mb, V(K.mod_b_d, K.mod_b_d.ap[layer:layer + 1, :].partition_broadcast(2)))
        for cb in range(6):
            ps = K.pb[cb % 2]
            for f in range(8):
                mw = mwr.next()
                P.dma("sync", mw, K.mod_w_d[layer, f * 128:(f + 1) * 128, cb * 512:(cb + 1) * 512])
                P.mm(ps[0:2, :], sc[:, f, :], mw, start=(f == 0), stop=(f == 7))
            P.tt("vector", mrow[:, cb * 512:(cb + 1) * 512], ps[0:2, :], mb[:, cb * 512:(cb + 1) * 512], ALU.add)
        P.dma("sync", K.modrow_d[layer], mrow)
        pt = K.pb[2]
        for c in range(24):
            P.tr(pt[:, 2 * c:2 * c + 2], mrow[:, c * 128:(c + 1) * 128], K.cst[0:2, CST_IDENT:CST_IDENT + 2])
        P.copy("vector", K.modT[layer].re("p c w -> p (c w)"), pt[:, 0:48])
        for w in range(2):
            ng = K.pv[:, PV_NG0 + 8 * layer:PV_NG0 + 8 * layer + 8]
            P.stt("vector", K.modA[layer][w], K.modT[layer][:, 8:16, w], 1.0, ng, ALU.add, ALU.mult)
            P.copy("vector", K.modB[layer][w], K.modT[layer][:, 0:8, w])


def seq_tiles(K):
    out = []
    for i in range(K.L // 128):
        out.append((1, i, i * 128, i * 128, K.CO + i * 128))
    for i in range(K.T // 128):
        out.append((0, i, i * 128, K.L + i * 128, K.XO + i * 128))
    return out


def stage_hT(K, layer, x_d, ctx_d):
    P, A = K.P, K.A
    xr = A.ring("xt", 3, [128, D])
    jr = A.ring("junk", 2, [128, D], BF16)
    xnr = A.ring("xn", 2, [128, D], BF16)
    tmpr = A.ring("tmp", 2, [128, 8, 128])
    hr = A.ring("ht", 2, [128, 8, 128], BF16)
    ssr = A.ring("ss", 4, [128, 2])
    for col in (0, K.L + 1, K.L + 2, K.L + K.T + 3):
        P.dma("gpsimd", K.hT_d[:, :, col:col + 1], K.zero, allow_slow_non_contiguous=True)
    for (w, i, r0, tok, col) in seq_tiles(K):
        src = ctx_d if w else x_d
        xt = xr.next()
        P.dma("sync", xt, src[r0:r0 + 128, :])
        ss = ssr.next()
        P.act(jr.next(), xt, AF.Square, accum_out=ss[:, 0:1])
        P.act(ss[:, 1:2], ss[:, 0:1], AF.Ln, bias=1e-6, scale=1.0 / D)
        P.act(ss[:, 1:2], ss[:, 1:2], AF.Exp, scale=-0.5)
        xn = xnr.next()
        P.ts("gpsimd", xn, xt, ss[:, 1:2], ALU.mult)
        ph = K.pbh if (tok // 128) % 2 == 0 else K.pbh2
        for f in range(8):
            P.tr(ph[:, f * 128:(f + 1) * 128], xn[:, f * 128:(f + 1) * 128], K.identb)
        tmp = tmpr.next()
        ht = hr.next()
        P.tt("vector", tmp, ph.re("p (f t) -> p f t", t=128), K.modA[layer][w].v.bc(2, [128, 8, 128]), ALU.mult)
        P.tt("gpsimd", ht, tmp, K.modB[layer][w].v.bc(2, [128, 8, 128]), ALU.add)
        P.dma("sync", K.hT_d[:, :, col:col + 128], ht)


def load_w_bf16(K, name, src_ap_v, shape):
    t = K.A.alloc(name, shape, BF16)
    K.P.dma("gpsimd", t, src_ap_v)
    return t


def stage_r1b(K):
    P, A, L, T = K.P, K.A, K.L, K.T
    NB = 256
    pv, cst = K.pv, K.cst
    W = [A.alloc("win%d" % j, [128, 8, D], BF16) for j in range(4)]
    for j in range(4):
        for f in range(8):
            P.dma("gpsimd", W[j][:, f, :], K.w_in_d[j, f * 128:(f + 1) * 128, :])
    W1c = A.alloc("w1c", [128, 8, 128], BF16)
    A1c = A.alloc("a1c", [128, 8, 128], BF16)
    for d in range(2):
        P.dma("gpsimd", W1c[:, :, 64 * d:64 * d + 64], K.w1_d.re("d (f p) r -> d p f r", p=128)[d])
        P.dma("gpsimd", A1c[:, :, 64 * d:64 * d + 64], K.a1_d.re("d (f p) r -> d p f r", p=128)[d])
    W2c = load_w_bf16(K, "w2c", K.w2_d, [128, D])
    A2c = load_w_bf16(K, "a2c", K.a2_d, [128, D])
    bones_r = K.cst_r[:, CST_BONES:CST_BONES + 128]
    scanm = cst[:, CST_SCAN:CST_SCAN + 256]
    hbr = A.ring("hb", 1, [128, 8, NB + 2], BF16)
    xx = A.alloc("xx", [128, 8, NB])
    tmpx = A.ring("tmpx", 2, [128, NB])
    lerp = [A.alloc("lerp%d" % j, [128, 8, NB], BF16) for j in range(6)]
    twb = A.alloc("twb", [128, NB], BF16)
    tab = A.alloc("tab", [128, NB], BF16)
    prod = A.alloc("prod", [128, 2, 8, NB], BF16)
    hselb = A.alloc("hselb", [128, 128], BF16)
    P.copy("vector", hselb, cst[:, CST_HSEL:CST_HSEL + 128])
    vtm = A.ring("vtm", 1, [128, D], F32R)
    sgt = A.ring("sgt", 1, [128, D], BF16)
    bon = A.ring("bon", 2, [128, 16])
    w = lambda nm, n=2, dt=F32: A.ring(nm, n, [128, NB], dt)
    r_sb, k_sb, kkf, sq, nmx, rn, kk = w("r_sb"), w("k_sb"), w("kkf", 1), w("sq", 1, F32R), w("nmx", 1), w("rn", 1), w("kk")
    ew, sw, ea, av, cum, epos, eneg, dprev, eprev = (w("ew", 1), w("sw"), w("ea", 1), w("av"), w("cum"), w("epos"),
                                                     w("eneg"), w("dprev", 1), w("eprev"))
    mfac, kd, bv, ktil, btil, atil = w("mfac", 1), w("kd"), w("bv"), w("ktil"), w("btil"), w("atil")
    fmo = A.ring("fmo", 2, [128, 2, 4, 128], BF16)
    tmo = A.ring("tmo", 2, [128, 2, 3, 128], F32R)

    blocks = [(1, 0, 0, K.CO)]
    for i in range(T // NB):
        blocks.append((0, L + i * NB, L + i * NB, K.XO + i * NB))
    blocks[0] = (1, 0, 0, K.CO)
    assert L == NB

    for (wh, t0, _, c0) in blocks:
        s0 = t0 // 128
        hb = hbr.next()
        P.dma("sync", hb, K.hT_d[:, :, c0 - 1:c0 + NB + 1])
        for f in range(8):
            tx = tmpx.next()
            P.tt("gpsimd", tx, hb[:, f, 0:NB], hb[:, f, 2:NB + 2], ALU.add)
            P.ts("gpsimd", tx, tx, 0.5, ALU.mult)
            P.tt("gpsimd", xx[:, f, :], tx, hb[:, f, 1:NB + 1], ALU.subtract)
            for j in range(6):
                if P.rr("lerp", ["vector", "gpsimd", "vector"]) == "vector":
                    P.stt("vector", lerp[j][:, f, :], xx[:, f, :],
                          pv[:, PV_MU + 8 * j + f:PV_MU + 8 * j + f + 1], hb[:, f, 1:NB + 1], ALU.mult, ALU.add)
                else:
                    tl = tmpx.next()
                    P.ts("gpsimd", tl, xx[:, f, :], pv[:, PV_MU + 8 * j + f:PV_MU + 8 * j + f + 1], ALU.mult)
                    P.tt("gpsimd", lerp[j][:, f, :], tl, hb[:, f, 1:NB + 1], ALU.add)
        for tt in range(NB // 128):
            for (j, kind) in ((2, "v"), (3, "g")):
                dst = vtm.next() if kind == "v" else sgt.next()
                for ch in range(2):
                    ps = K.pb[ch]
                    for f in range(8):
                        P.mm(ps, lerp[j][:, f, tt * 128:(tt + 1) * 128], W[j][:, f, ch * 512:(ch + 1) * 512],
                             start=(f == 0), stop=(f == 7))
                    if kind == "v":
                        P.copy("scalar", dst[:, ch * 512:(ch + 1) * 512], ps)
                    else:
                        P.act(dst[:, ch * 512:(ch + 1) * 512], ps, AF.Silu)
                if kind == "v":
                    P.dma("sync", K.v_d[t0 + tt * 128:t0 + (tt + 1) * 128, :], dst)
                else:
                    P.dma("sync", K.sg_d[t0 + tt * 128:t0 + (tt + 1) * 128, :], dst)
        ps = K.pb[2]
        for f in range(8):
            P.mm(ps[:, 0:NB], W1c[:, f, :], lerp[4][:, f, :], start=(f == 0), stop=(f == 7))
        P.act(twb, ps[:, 0:NB], AF.Tanh)
        for f in range(8):
            P.mm(ps[:, NB:2 * NB], A1c[:, f, :], lerp[5][:, f, :], start=(f == 0), stop=(f == 7))
        P.copy("vector", tab, ps[:, NB:2 * NB])
        for g in range(8):
            gs = slice(g * 128, (g + 1) * 128)
            pr = K.pb[3]
            for f in range(8):
                P.mm(pr[:, 0:NB], W[0][:, f, gs], lerp[0][:, f, :], start=(f == 0), stop=(f == 7))
            for f in range(8):
                P.mm(pr[:, NB:2 * NB], W[1][:, f, gs], lerp[1][:, f, :], start=(f == 0), stop=(f == 7))
            r_, k_ = r_sb.next(), k_sb.next()
            P.copy("scalar", r_, pr[:, 0:NB])
            P.copy("scalar", k_, pr[:, NB:2 * NB])
            kf, sq_, nm, rn_, kk_ = kkf.next(), sq.next(), nmx.next(), rn.next(), kk.next()
            P.ts("gpsimd", kf, k_, pv[:, PV_KK + g:PV_KK + g + 1], ALU.mult)
            P.tt("gpsimd", sq_, kf, kf, ALU.mult)
            pn = K.pb[4]
            P.mm(pn[:, 0:NB], bones_r, sq_)
            P.ts("vector", nm, pn[:, 0:NB], 1e-24, ALU.max)
            P.act(rn_, nm, AF.Ln)
            P.act(rn_, rn_, AF.Exp, scale=-0.5)
            P.tt("gpsimd", kk_, kf, rn_, ALU.mult)
            for d in range(2):
                hs = slice(64 * d, 64 * d + 64)
                pl = K.pb[5 + (d % 2)]
                P.mm(pl[:, 0:NB], W2c[hs, gs], twb[hs, :])
                P.mm(pl[:, NB:2 * NB], A2c[hs, gs], tab[hs, :])
                ew_, sw_, ea_, a_ = ew.next(), sw.next(), ea.next(), av.next()
                P.act(ew_, pl[:, 0:NB], AF.Exp, bias=K.negw0[:, 8 * d + g:8 * d + g + 1], scale=-1.0)
                P.act(ea_, pl[:, NB:2 * NB], AF.Exp, bias=K.nega0[:, 8 * d + g:8 * d + g + 1], scale=-1.0)
                P.act(ew_, ew_, AF.Identity, bias=1.0)
                P.recip(sw_, ew_)
                P.act(ea_, ea_, AF.Identity, bias=1.0)
                P.recip(a_, ea_)
                cm = cum.next()
                if d == 0:
                    P._c("vector", "tensor_tensor_scan", ["out"],
                         dict(out=cm, data0=scanm, data1=sw_, initial=0.0, op0=ALU.mult, op1=ALU.add))
                else:
                    P._c("vector", "tensor_tensor_scan", ["out"],
                         dict(out=cm[:, NB - 1::-1], data0=scanm, data1=sw_[:, NB - 1::-1], initial=0.0,
                              op0=ALU.mult, op1=ALU.add))
                ep, en, dp, epv = epos.next(), eneg.next(), dprev.next(), eprev.next()
                P.act(ep, cm, AF.Exp, scale=-C0)
                P.act(en, cm, AF.Exp, scale=C0)
                P.tt("gpsimd", dp, cm, sw_, ALU.subtract)
                P.act(epv, dp, AF.Exp, scale=-C0)
                for s in range(NB // 128):
                    cc = s * 128 + (127 if d == 0 else 0)
                    P.copy("gpsimd", K.gam[:, d, g, s0 + s:s0 + s + 1], ep[:, cc:cc + 1])
                mf, kd_, b_ = mfac.next(), kd.next(), bv.next()
                P.ts("gpsimd", mf, a_, pv[:, PV_KA + g:PV_KA + g + 1], ALU.mult, K.omka[:, g:g + 1], ALU.add)
                P.tt("gpsimd", kd_, k_, mf, ALU.mult)
                P.tt("gpsimd", b_, kk_, a_, ALU.mult)
                kt_, bt_, at_ = ktil.next(), btil.next(), atil.next()
                P.tt("vector", kt_, kd_, en, ALU.mult)
                P.tt("vector", bt_, b_, en, ALU.mult)
                P.stt("vector", at_, kk_, -1.0, epv, ALU.mult, ALU.mult)
                tp = tmpx.next()
                P.ts("gpsimd", tp, r_, pv[:, PV_RK + g:PV_RK + g + 1], ALU.mult)
                P.tt("gpsimd", prod[:, d, g, :], tp, kd_, ALU.mult)
                fo = fmo.next()
                fo3 = lambda arr: fo[:, :, arr, :]
                P.copy("scalar", fo3(0), bt_.re("p (s t) -> p s t", t=128))
                P.copy("scalar", fo3(1), kt_.re("p (s t) -> p s t", t=128))
                P.copy("gpsimd", fo3(2), at_.re("p (s t) -> p s t", t=128))
                P.tt("vector", fo3(3), r_.re("p (s t) -> p s t", t=128), ep.re("p (s t) -> p s t", t=128), ALU.mult)
                P.dma("sync", K.fm_d[d][:, g, s0:s0 + NB // 128, :, :], fo)
                to = tmo.next()
                pt, pq = K.pb[d % 2], K.pb[2 + (d % 2)]
                for tt in range(NB // 128):
                    for ai, src in enumerate((at_, bt_, kt_)):
                        idx = tt * 3 + ai
                        dst = pt[:, idx * 128:(idx + 1) * 128] if idx < 4 else pq[:, (idx - 4) * 128:(idx - 3) * 128]
                        P.tr(dst, src[:, tt * 128:(tt + 1) * 128], cst[:, CST_IDENT:CST_IDENT + 128])
                tof = to.re("p t a f -> p (t a f)")
                P.copy("scalar", tof[:, 0:512], pt)
                P.copy("scalar", tof[:, 512:768], pq[:, 0:256])
                for tt in range(NB // 128):
                    P.dma("sync", K.tm_d[d][t0 + tt * 128:t0 + (tt + 1) * 128, :, gs], to[:, tt, :, :])
        for tt in range(NB // 128):
            pbn = K.pb[4]
            n = 0
            for d in range(2):
                for g in range(8):
                    P.mm(pbn[:, 256 + 16 * tt:256 + 16 * tt + 16], prod[:, d, g, tt * 128:(tt + 1) * 128],
                         hselb[:, 16 * g:16 * g + 16], start=(n == 0), stop=(n == 15))
                    n += 1
            b_t = bon.next()
            P.copy("vector", b_t, pbn[:, 256 + 16 * tt:256 + 16 * tt + 16])
            P.dma("sync", K.bonus_d[t0 + tt * 128:t0 + (tt + 1) * 128, :], b_t)


def lockstep(gens):
    gens = list(gens)
    while gens:
        for g_ in list(gens):
            try:
                next(g_)
            except StopIteration:
                gens.remove(g_)


def stage_r2(K):
    P, A, L, T, NS = K.P, K.A, K.L, K.T, K.NS
    cst = K.cst
    m512 = [A.alloc("m512", [128, 2, 256]) for d in range(2)]
    for d in range(2):
        for r in range(2):
            P.copy("gpsimd", m512[d][:, r, :], cst[:, CST_M01 + 256 * d:CST_M01 + 256 * d + 256])
    i64 = A.alloc("i64", [128, 64])
    for p in range(2):
        P.copy("gpsimd", i64[64 * p:64 * p + 64, :], cst[64 * p:64 * p + 64, CST_IDENT + 64 * p:CST_IDENT + 64 * p + 64])
    ident = cst[:, CST_IDENT:CST_IDENT + 128]
    tmbr = A.ring("tmb", 2, [128, 3, D], F32)
    vbr = A.ring("vb", 2, [128, D], F32)
    fmbr = A.ring("fmb", 4, [128, 4, 128], BF16)
    ysr = A.ring("ys", 2, [128, D])
    Mst = [A.alloc("Mst", [128, 8, 2, 64], F32) for _ in range(2)]
    pb = K.pb
    PAB = [pb[p].v for p in range(2)]
    PC = [pb[2][:, p * 128:(p + 1) * 128] for p in range(2)]
    PXi = [pb[2][:, 256 + p * 64:256 + (p + 1) * 64] for p in range(2)]
    PG = pb[2][:, 384:512]
    PY = [pb[5], pb[6]]
    PM = pb[7][:, 0:128]
    slots = []
    for q in range(2):
        S = Ctx()
        S.sabr = A.ring("sab", 2, [128, 2, 512], F32)
        S.p0r = A.ring("p0", 2, [128, 2, 128], F32)
        S.xr = A.ring("xp", 4, [128, 2, 128], F32)
        S.rpr = A.ring("rp", 3, [128, 2, 256], F32)
        S.igr = A.ring("ig", 2, [128, 128], F32)
        S.rhr = A.ring("rh", 2, [128, 256], F32)
        S.ahr = A.ring("ah", 2, [128, 128], F32)
        for b in S.igr.bufs + S.rhr.bufs:
            P.memset("gpsimd", b, 0.0)
        ba, bb = (pb[3], pb[4]) if q == 0 else (pb[0], pb[1])
        S.PX = [ba[:, p * 128:(p + 1) * 128] for p in range(2)]
        S.PRh = [ba[:, 256 + 128 * p:384 + 128 * p] for p in range(2)]
        S.PRP = [bb[:, p * 256:(p + 1) * 256] for p in range(2)]
        slots.append(S)

    def pair_body(S, d, s, g, tmb, vb, Mcur, Mnew):
        fmb = fmbr.next()
        P.dma("sync", fmb, K.fm_d[d][:, g, s, :, :])
        sab, p0, x = S.sabr.next(), S.p0r.next(), S.xr.next()
        gc = slice(g * 128, (g + 1) * 128)
        for p in range(2):
            bs = slice(64 * p, 64 * p + 64)
            ar = fmb.re("p a t -> p (a t)")[bs, 256:512]
            P.mm(PAB[p][:, 0:256], fmb[bs, 0, :], ar)
            P.mm(PAB[p][:, 256:512], fmb[bs, 1, :], ar)
        for p in range(2):
            hc = slice(64 * (2 * g + p), 64 * (2 * g + p) + 64)
            P.tt("vector", sab[:, p, :], PAB[p], m512[d].re("p r c -> p (r c)"), ALU.mult)
            P.tr(PC[p], sab[:, p, 0:128], ident)
            P.copy("vector", p0[:, p, :], PC[p])
            P.copy("gpsimd", x[:, p, 0:64], tmb[:, 0, hc])
            P.mm(PXi[p], sab[:, p, 256:384], vb[:, hc])
            P.copy("scalar", x[:, p, 64:128], PXi[p])
        yield
        Rk = [sab[:, p, 0:128] for p in range(2)]
        Pk = [p0[:, p, :] for p in range(2)]
        for k in range(7):
            xn = S.xr.next()
            rp = S.rpr.next() if k < 6 else None
            for p in range(2):
                P.mm(S.PX[p], Rk[p], x[:, p, :])
                if k < 6:
                    P.mm(S.PRP[p][:, 0:128], Pk[p], Rk[p])
                    if k < 5:
                        P.mm(S.PRP[p][:, 128:256], Rk[p], Pk[p])
            yield
            for p in range(2):
                P.tt("vector", xn[:, p, :], S.PX[p], x[:, p, :], ALU.add)
                if k < 5:
                    P.copy("scalar", rp[:, p, :], S.PRP[p])
                elif k == 5:
                    P.copy("scalar", rp[:, p, 0:128], S.PRP[p][:, 0:128])
            yield
            x = xn
            if k < 6:
                Rk = [rp[:, p, 0:128] for p in range(2)]
                Pk = [rp[:, p, 128:256] for p in range(2)]
        ig, rh, ah = S.igr.next(), S.rhr.next(), S.ahr.next()
        for p in range(2):
            P.copy("gpsimd", ah[:, 64 * p:64 * p + 64], x[:, p, 0:64])
        P.mm(PG, ah, tmb[:, 1, gc])
        for p in range(2):
            P.mm(S.PRh[p], ah, sab[:, p, 128:256])
        for p in range(2):
            bs = slice(64 * p, 64 * p + 64)
            P.tt("vector", ig[bs, 64 * p:64 * p + 64], PG[bs, 64 * p:64 * p + 64], i64[bs, :], ALU.add)
            P.tt("vector", rh[bs, 128 * p:128 * p + 128], S.PRh[p][bs, :], fmb[bs, 3, :], ALU.add)
        yield
        for p in range(2):
            h = 2 * g + p
            hc = slice(64 * h, 64 * h + 64)
            py = PY[h // 8][:, (h % 8) * 64:(h % 8) * 64 + 64]
            P.mm(py, sab[:, p, 128:256], x[:, p, 64:128], start=True, stop=False)
            P.mm(py, sab[:, p, 384:512], vb[:, hc], start=False, stop=False)
            P.mm(py, rh[:, 128 * p:128 * p + 128], Mcur[:, g, p, :], start=False, stop=True)
        P.mm(PM[:, 0:64], tmb[:, 1, gc], x[:, 0, 64:128], start=True, stop=False, skip=True)
        P.mm(PM[:, 64:128], tmb[:, 1, gc], x[:, 1, 64:128], start=False, stop=False, skip=True)
        P.mm(PM, tmb[:, 2, gc], vb[:, gc], start=False, stop=False, skip=True)
        P.mm(PM, ig, Mcur.re("p g a i -> p g (a i)")[:, g, :], start=False, stop=True, skip=True)
        for p in range(2):
            bs = slice(64 * p, 64 * p + 64)
            P.act(Mnew[bs, g, p, :], PM[bs, 64 * p:64 * p + 64], AF.Identity, scale=K.gam[bs, d, g, s:s + 1])

    nctx = L // 128
    for d in range(2):
        order = list(range(NS)) if d == 0 else (list(range(nctx - 1, -1, -1)) + list(range(NS - 1, nctx - 1, -1)))
        mi = 0
        P.memset("gpsimd", Mst[0], 0.0)
        P.memset("gpsimd", Mst[1], 0.0)
        for s in order:
            Mcur, Mnew = Mst[mi % 2], Mst[(mi + 1) % 2]
            mi += 1
            tmb, vb, ys = tmbr.next(), vbr.next(), ysr.next()
            P.dma("sync", tmb, K.tm_d[d][s * 128:(s + 1) * 128, :, :])
            P.dma("sync", vb, K.v_d[s * 128:(s + 1) * 128, :])
            for gg in range(4):
                lockstep([pair_body(slots[q], d, s, 2 * gg + q, tmb, vb, Mcur, Mnew) for q in range(2)])
            P.copy("scalar", ys[:, 0:512], PY[0])
            P.copy("vector", ys[:, 512:1024], PY[1])
            P.dma("sync", K.y_d[d][s * 128:(s + 1) * 128, :], ys)


def bcast_row(K, name, src_v):
    n = src_v.ap.shape[-1]
    t = K.A.alloc(name, [128, n])
    K.P.dma("gpsimd", t, V(src_v.buf, src_v.ap.partition_broadcast(128)))
    return t


def stage_r3(K):
    P, A, L, T = K.P, K.A, K.L, K.T
    Wo = A.alloc("wo1", [128, 8, D], BF16)
    for f in range(8):
        P.dma("gpsimd", Wo[:, f, :], K.wout1_d[f * 128:(f + 1) * 128, :])
    lng = bcast_row(K, "lng", K.rows_d[0:1, :])
    lnb = bcast_row(K, "lnb", K.rows_d[1:2, :])
    gx = [bcast_row(K, "gres%d" % w, K.modrow_d[0, w:w + 1, 2 * D:3 * D]) for w in range(2)]
    y0r, y1r, vr, xr_ = (A.ring(n, 2, [128, D]) for n in ("y0", "y1", "vv", "xres"))
    sgr = A.ring("sgl", 2, [128, D], BF16)
    bnr = A.ring("bnl", 2, [128, 16])
    yfr, sqr, t1r = A.ring("yf", 2, [128, D]), A.ring("ysq", 2, [128, D]), A.ring("t1", 2, [128, D])
    obr = A.ring("ob", 2, [128, D], BF16)
    otr = A.ring("oT", 2, [128, 8, 128], BF16)
    str_ = A.ring("stat", 2, [128, 4, 16])
    outr = A.ring("xo", 2, [128, D])
    for (w, i, r0, tok, col) in seq_tiles(K):
        y0, y1, vv, xres, sg, bn = y0r.next(), y1r.next(), vr.next(), xr_.next(), sgr.next(), bnr.next()
        ts_ = slice(tok, tok + 128)
        P.dma("sync", y0, K.y_d[0][ts_, :])
        P.dma("sync", y1, K.y_d[1][ts_, :])
        P.dma("sync", vv, V(K.v_d, K.v_d.ap[ts_, :].bitcast(F32)))
        P.dma("sync", sg, K.sg_d[ts_, :])
        P.dma("sync", bn, K.bonus_d[ts_, :])
        P.dma("sync", xres, (K.ctx_d if w else K.x_d)[r0:r0 + 128, :])
        yf, sq, t1, st = yfr.next(), sqr.next(), t1r.next(), str_.next()
        P.tt("gpsimd", yf, y0, y1, ALU.add)
        P.act(sq, yf, AF.Square)
        h3 = lambda b: b.re("p (h k) -> p h k", k=64)
        P._c("vector", "tensor_reduce", ["out"], dict(out=st[:, 0, :], in_=h3(yf), axis=AX.X, op=ALU.add))
        P._c("vector", "tensor_reduce", ["out"], dict(out=st[:, 1, :], in_=h3(sq), axis=AX.X, op=ALU.add))
        P.ts("vector", st[:, 0, :], st[:, 0, :], 1.0 / 64, ALU.mult)
        P.tt("vector", st[:, 2, :], st[:, 0, :], st[:, 0, :], ALU.mult)
        P.stt("vector", st[:, 1, :], st[:, 1, :], 1.0 / 64, st[:, 2, :], ALU.mult, ALU.subtract)
        P.act(st[:, 3, :], st[:, 1, :], AF.Ln, bias=64e-5)
        P.act(st[:, 3, :], st[:, 3, :], AF.Exp, scale=-0.5)
        bc = lambda v_: v_.bc(2, [128, 16, 64])
        P.tt("vector", h3(t1), h3(yf), bc(st[:, 0, :]), ALU.subtract)
        P.tt("gpsimd", h3(t1), h3(t1), bc(st[:, 3, :]), ALU.mult)
        P.tt("vector", t1, t1, lng, ALU.mult)
        P.tt("gpsimd", t1, t1, lnb, ALU.add)
        P.tt("vector", h3(vv), h3(vv), bc(bn.v), ALU.mult)
        P.tt("gpsimd", t1, t1, vv, ALU.add)
        ob = obr.next()
        P.tt("vector", ob, t1, sg, ALU.mult)
        ph = K.pbh if (tok // 128) % 2 == 0 else K.pbh2
        for f in range(8):
            P.tr(ph[:, f * 128:(f + 1) * 128], ob[:, f * 128:(f + 1) * 128], K.identb)
        oT = otr.next()
        P.copy("scalar", oT.re("p f t -> p (f t)"), ph)
        for ch in range(2):
            ps = K.pb[ch]
            for f in range(8):
                P.mm(ps, oT[:, f, :], Wo[:, f, ch * 512:(ch + 1) * 512], start=(f == 0), stop=(f == 7))
        xo = outr.next()
        for ch in range(2):
            cs = slice(ch * 512, (ch + 1) * 512)
            P.tt("vector", xo[:, cs], K.pb[ch], gx[w][:, cs], ALU.mult)
        P.tt("gpsimd", xo, xo, xres, ALU.add)
        P.dma("sync", (K.ctx1_d if w else K.x1_d)[r0:r0 + 128, :], xo)


def stage_mla(K, do_attn=True, do_final=True):
    return stage_mla_impl(K, do_attn, do_final)


def host_inputs(inp, b, T, L):
    f32 = lambda a: np.ascontiguousarray(np.asarray(a, np.float32))
    pv = np.zeros((128, PV_N), np.float32)
    pv[:, PV_NG0:PV_NG0 + 8] = fm(inp["norm_g"][0])
    pv[:, PV_NG1:PV_NG1 + 8] = fm(inp["norm_g"][1])
    for j in range(6):
        pv[:, PV_MU + 8 * j:PV_MU + 8 * j + 8] = fm(inp["rwkv_mu"][0, j])
    for d in range(2):
        pv[:, PV_W0 + 8 * d:PV_W0 + 8 * d + 8] = fm(inp["rwkv_w0"][0, d])
        pv[:, PV_A0 + 8 * d:PV_A0 + 8 * d + 8] = fm(inp["rwkv_a0"][0, d])
    pv[:, PV_KK:PV_KK + 8] = fm(inp["rwkv_k_k"][0])
    pv[:, PV_KA:PV_KA + 8] = fm(inp["rwkv_k_a"][0])
    pv[:, PV_RK:PV_RK + 8] = fm(np.asarray(inp["rwkv_r_k"][0]).reshape(-1))
    pv[:, PV_QG:PV_QG + 3] = fm(inp["mla_q_norm_g"][0])
    pv[:, PV_KVG:PV_KVG + 2] = fm(inp["mla_kv_norm_g"][0])
    cvec = np.concatenate([fm(inp["c"][b]), fm(inp["c_ctx"])], axis=1)
    w_in2 = np.asarray(inp["mla_w_in"][0], np.float32)
    w_in2 = np.concatenate([w_in2, w_in2[:, 656:672], w_in2[:, 640:656]], axis=1)
    wqb = np.asarray(inp["mla_w_qb"][0], np.float32).reshape(384, 16, 96)
    wqb = np.concatenate([wqb, wqb[:, :, 80:96], wqb[:, :, 64:80]], axis=2).reshape(384, 16 * 128)
    return {
        "x": f32(inp["x"][b][:T]), "ctx": f32(inp["ctx"][b][:L]), "cvec": f32(cvec), "pvec": pv,
        "cst": make_consts(),
        "rows": f32(np.stack([inp["rwkv_lnx_g"][0], inp["rwkv_lnx_b"][0], inp["final_g"]])),
        "mod_w": f32(inp["mod_w"]), "mod_b": f32(inp["mod_b"]),
        "rwkv_w_in": f32(inp["rwkv_w_in"][0]), "rwkv_w1": f32(inp["rwkv_w1"][0]),
        "rwkv_w2": f32(np.asarray(inp["rwkv_w2"][0]).reshape(128, D)),
        "rwkv_a1": f32(inp["rwkv_a1"][0]), "rwkv_a2": f32(np.asarray(inp["rwkv_a2"][0]).reshape(128, D)),
        "rwkv_w_out": f32(inp["rwkv_w_out"][0]),
        "mla_w_in": f32(w_in2), "mla_w_qb": f32(wqb), "mla_w_kvb": f32(inp["mla_w_kvb"][0]),
        "mla_w_out": f32(inp["mla_w_out"][0]), "rope": rope_tables(T),
    }


_CACHE = {}


def kernel(**inputs):
    B, T, _ = inputs["x"].shape
    L = inputs["ctx"].shape[1]
    key = (T, L)
    if key not in _CACHE:
        _CACHE[key] = build(T, L)[0]
    nc = _CACHE[key]
    in_maps = [host_inputs(inputs, b, T, L) for b in range(B)]
    res = run_bass_kernel_spmd(nc, in_maps, core_ids=list(range(B)))
    return np.stack([np.asarray(r["out"], np.float32) for r in res.results], axis=0)


def stage_mla_impl(K, do_attn=True, do_final=True):
    P, A, L, T, NT, NS = K.P, K.A, K.L, K.T, K.NT, K.NS
    pv, cst, pb = K.pv, K.cst, K.pb
    SCL = 1.0 / math.sqrt(96.0)
    qnT = A.alloc("qnT", [128, 3, T], BF16)
    kvnT = A.alloc("kvnT", [128, 2, NT], BF16)
    KT = A.alloc("KT", [128, NT], BF16)
    ones_r = A.alloc("ones_r", [128, 128], F32R)
    ones_f = A.alloc("ones_f", [128, 64])
    P.memset("vector", ones_r, 1.0)
    P.memset("vector", ones_f, 1.0)
    mark = A.off
    Win = A.alloc("mwin", [128, 8, 1728], BF16)
    for f in range(8):
        P.dma("gpsimd", Win[:, f, :], K.mw_in_d[f * 128:(f + 1) * 128, :])
    hbr = A.ring("mhb", 1, [128, 8, 512], BF16)
    qc = A.alloc("qc", [128, 5, 512])
    sqr = A.ring("msq", 1, [128, 512], F32R)
    rsr = A.ring("mrs", 1, [128, 512])
    rpr = A.ring("mrp", 1, [128, 2, 512])
    t1r = A.ring("mt1", 1, [128, 512])
    t2r = A.ring("mt2", 1, [128, 512])
    sgo = A.ring("msg", 1, [128, 8, 512], BF16)
    blocks = [(1, 0, K.CO, L, 0)] + [(0, L + i * 512, K.XO + i * 512, 512, i * 512) for i in range(T // 512)]
    for (wh, t0, c0, nb, xt0) in blocks:
        hb = hbr.next()
        P.dma("sync", hb[:, :, 0:nb], K.hT_d[:, :, c0:c0 + nb])
        tiles = ([] if wh else [0, 1, 2]) + [3, 4]
        for m in tiles:
            for f in range(8):
                P.mm(pb[m][:, 0:nb], Win[:, f, m * 128:(m + 1) * 128], hb[:, f, 0:nb], start=(f == 0), stop=(f == 7))
            P.copy("scalar", qc[:, m, 0:nb], pb[m][:, 0:nb])
        for (ms, nfeat, dst, gcol) in (([] if wh else [0, 1, 2], 384.0, qnT, PV_QG), ([3, 4], 256.0, kvnT, PV_KVG)):
            if not ms:
                continue
            for i, m in enumerate(ms):
                sq = sqr.next()
                P.tt("gpsimd", sq[:, 0:nb], qc[:, m, 0:nb], qc[:, m, 0:nb], ALU.mult)
                P.mm(pb[5][:, 0:nb], ones_r, sq[:, 0:nb], start=(i == 0), stop=(i == len(ms) - 1))
            rs = rsr.next()
            P.act(rs[:, 0:nb], pb[5][:, 0:nb], AF.Ln, bias=1e-6, scale=1.0 / nfeat)
            P.act(rs[:, 0:nb], rs[:, 0:nb], AF.Exp, scale=-0.5)
            for i, m in enumerate(ms):
                o_ = dst[:, i, xt0:xt0 + nb] if dst is qnT else dst[:, i, t0:t0 + nb]
                P.stt("vector", o_, qc[:, m, 0:nb], pv[:, gcol + i:gcol + i + 1], rs[:, 0:nb], ALU.mult, ALU.mult)
        pe, sw = pb[6][64:96, 0:nb], pb[7][64:96, 0:nb]
        for f in range(8):
            P.mm(pe, Win[:, f, 640:672], hb[:, f, 0:nb], start=(f == 0), stop=(f == 7))
        if wh:
            P.copy("scalar", KT[64:96, t0:t0 + nb], pe)
        else:
            for f in range(8):
                P.mm(sw, Win[:, f, 1696:1728], hb[:, f, 0:nb], start=(f == 0), stop=(f == 7))
            rp = rpr.next()
            P.dma("sync", rp[64:96, :, :], K.rope_d[:, :, xt0:xt0 + nb])
            t1, t2 = t1r.next(), t2r.next()
            P.tt("vector", t1[64:96, :], pe, rp[64:96, 0, :], ALU.mult)
            P.tt("vector", t2[64:96, :], sw, rp[64:96, 1, :], ALU.mult)
            P.tt("gpsimd", KT[64:96, t0:t0 + nb], t1[64:96, :], t2[64:96, :], ALU.add)
            so = sgo.next()
            for g in range(8):
                ps = pb[g % 4]
                for f in range(8):
                    P.mm(ps, Win[:, f, 672 + g * 128:672 + (g + 1) * 128], hb[:, f, :], start=(f == 0), stop=(f == 7))
                P.act(so[:, g, :], ps, AF.Silu)
            P.dma("sync", K.sgT_d[:, :, xt0:xt0 + 512], so)
    barrier(P, K.tiny)
    A.off = mark
    if not do_attn:
        return
    QT = A.alloc("QT", [128, T], BF16)
    Vh = A.alloc("Vh", [128, NS, 65], BF16)
    P.memset("vector", Vh, 1.0)
    wqr = A.ring("wq", 2, [128, 3, 128], BF16)
    wkr = A.ring("wk", 2, [128, 2, 128], BF16)
    ptr = A.ring("pt", 3, [128, 512], BF16)
    rpr = A.ring("arp", 2, [128, 2, 512])
    t1r = A.ring("at1", 2, [128, 512])
    t2r = A.ring("at2", 2, [128, 512])
    recr = A.ring("rec", 2, [128, 512])
    bcr = A.ring("bcs", 2, [64, 512])
    otr = A.ring("oth", 2, [64, 512], BF16)
    kblocks = [(k0, min(512, NT - k0)) for k0 in range(0, NT, 512)]
    for h in range(NH):
        wq, wk = wqr.next(), wkr.next()
        P.dma("gpsimd", wq, K.wqb_d.re("(k p) n -> p k n", p=128)[:, :, h * 128:(h + 1) * 128])
        P.dma("gpsimd", wk, K.wkvb_d.re("(k p) n -> p k n", p=128)[:, :, h * 128:(h + 1) * 128])
        for (k0, nk) in kblocks:
            for kt in range(2):
                P.mm(pb[0][0:64, 0:nk], wk[:, kt, 0:64], kvnT[:, kt, k0:k0 + nk], start=(kt == 0), stop=(kt == 1))
            P.copy("scalar", KT[0:64, k0:k0 + nk], pb[0][0:64, 0:nk])
        for j0 in range(0, NS, 8):
            nj = min(8, NS - j0)
            for j in range(nj):
                for kt in range(2):
                    P.mm(pb[1][:, j * 64:(j + 1) * 64], kvnT[:, kt, (j0 + j) * 128:(j0 + j + 1) * 128], wk[:, kt, 64:128],
                         start=(kt == 0), stop=(kt == 1))
            P.copy("vector", Vh[:, j0:j0 + nj, 0:64], pb[1][:, 0:nj * 64].re("p (j c) -> p j c", c=64))
        for qb in range(T // 512):
            qs = slice(qb * 512, (qb + 1) * 512)
            for kt in range(3):
                P.mm(pb[2][0:96, :], wq[:, kt, 0:96], qnT[:, kt, qs], start=(kt == 0), stop=(kt == 2))
            for kt in range(3):
                P.mm(pb[3][64:96, :], wq[:, kt, 96:128], qnT[:, kt, qs], start=(kt == 0), stop=(kt == 2))
            rp = rpr.next()
            P.dma("sync", rp[64:96, :, :], K.rope_d[:, :, qs])
            t1, t2 = t1r.next(), t2r.next()
            P.copy("scalar", QT[0:64, qs], pb[2][0:64, :])
            P.tt("vector", t1[64:96, :], pb[2][64:96, :], rp[64:96, 0, :], ALU.mult)
            P.tt("vector", t2[64:96, :], pb[3][64:96, :], rp[64:96, 1, :], ALU.mult)
            P.tt("gpsimd", QT[64:96, qs], t1[64:96, :], t2[64:96, :], ALU.add)
        for qb in range(T // 512):
            qs = slice(qb * 512, (qb + 1) * 512)
            po = pb[4 + qb % 2]
            LOOK = ATT_LOOK
            for kt in range(min(LOOK, NS)):
                P.mm(pb[kt % 4], KT[0:96, kt * 128:(kt + 1) * 128], QT[0:96, qs])
            for kt in range(NS):
                if kt + LOOK < NS:
                    k2 = kt + LOOK
                    P.mm(pb[k2 % 4], KT[0:96, k2 * 128:(k2 + 1) * 128], QT[0:96, qs])

                pt = ptr.next()
                P.act(pt, pb[kt % 4], AF.Exp, scale=SCL)
                P.mm(po[0:65, :], Vh[:, kt, :], pt, start=(kt == 0), stop=(kt == NS - 1))
            rec, bcs, ot = recr.next(), bcr.next(), otr.next()
            P.recip(rec[64:65, :], po[64:65, :])
            P.mm(pb[6][0:64, :], ones_f[64:65, :], rec[64:65, :])
            P.copy("scalar", bcs, pb[6][0:64, :])
            P.tt("vector", ot, po[0:64, :], bcs, ALU.mult)
            P.dma("sync", K.oT_d[64 * (h % 2):64 * (h % 2) + 64, h // 2, qs], ot)
    barrier(P, K.tiny)
    A.off = A.base
    if not do_final:
        return
    Wo = A.alloc("wo2", [128, 8, D], BF16)
    for f in range(8):
        P.dma("gpsimd", Wo[:, f, :], K.wout2_d[f * 128:(f + 1) * 128, :])
    gx2 = bcast_row(K, "gx2", K.modrow_d[1, 0:1, 2 * D:3 * D])
    fing = bcast_row(K, "fing", K.rows_d[2:3, :])
    obr = A.ring("fo", 2, [128, 8, 512], BF16)
    sbr = A.ring("fs", 2, [128, 8, 512], BF16)
    ogr = A.ring("fg", 2, [128, 8, 512], BF16)
    x1r = A.ring("fx", 2, [128, D])
    x2r = A.ring("fx2", 2, [128, D])
    jr = A.ring("fj", 1, [128, D], BF16)
    ssr = A.ring("fss", 4, [128, 2])
    for qb in range(T // 512):
        qs = slice(qb * 512, (qb + 1) * 512)
        ob, sb, og = obr.next(), sbr.next(), ogr.next()
        P.dma("sync", ob, K.oT_d[:, :, qs])
        P.dma("sync", sb, K.sgT_d[:, :, qs])
        P.tt("gpsimd", og, ob, sb, ALU.mult)
        for tt in range(4):
            r0 = qb * 512 + tt * 128
            x1 = x1r.next()
            P.dma("sync", x1, K.x1_d[r0:r0 + 128, :])
            for ch in range(2):
                for f in range(8):
                    P.mm(pb[ch], og[:, f, tt * 128:(tt + 1) * 128], Wo[:, f, ch * 512:(ch + 1) * 512], start=(f == 0), stop=(f == 7))
            x2 = x2r.next()
            for ch in range(2):
                cs = slice(ch * 512, (ch + 1) * 512)
                P.tt("vector", x2[:, cs], pb[ch], gx2[:, cs], ALU.mult)
            P.tt("gpsimd", x2, x2, x1, ALU.add)
            ss = ssr.next()
            P.act(jr.next(), x2, AF.Square, accum_out=ss[:, 0:1])
            P.act(ss[:, 1:2], ss[:, 0:1], AF.Ln, bias=1e-6, scale=1.0 / D)
            P.act(ss[:, 1:2], ss[:, 1:2], AF.Exp, scale=-0.5)
            P.stt("vector", x2, x2, ss[:, 1:2], fing, ALU.mult, ALU.mult)
            P.dma("sync", K.out_d[r0:r0 + 128, :], x2)
```

```python
import contextlib
import math
import numpy as np
import concourse.bass as bass
import concourse.mybir as mybir
from concourse.bass_utils import run_bass_kernel_spmd

F32 = mybir.dt.float32
F32R = mybir.dt.float32
BF16 = mybir.dt.bfloat16
ALU = mybir.AluOpType
AF = mybir.ActivationFunctionType
AX = mybir.AxisListType

N_DMA_SEMS = 8
DEBUG_SRC = False
R2CUT = 99
ATT_LOOK = 3
R2_SPLIT = False
R2VAR = 0
NO_SELF_SYNC = ("tensor",)

D = 1024
NH = 16
C0 = math.exp(-0.5)


class V:
    __slots__ = ("buf", "ap")

    def __init__(self, buf, ap):
        self.buf = buf
        self.ap = ap

    def __getitem__(self, k):
        return V(self.buf, self.ap[k])

    def re(self, s, **kw):
        return V(self.buf, self.ap.rearrange(s, **kw))

    def bc(self, axis, shape):
        return V(self.buf, self.ap.unsqueeze(axis).to_broadcast(list(shape)))

    def cast(self, dt):
        return V(self.buf, self.ap.bitcast(dt))


class Buf:
    __slots__ = ("name", "w", "r", "ap")

    def __init__(self, name, ap):
        self.name = name
        self.w = None
        self.r = []
        self.ap = ap

    def __getitem__(self, k):
        return V(self, self.ap[k])

    @property
    def v(self):
        return V(self, self.ap)

    def re(self, s, **kw):
        return V(self, self.ap.rearrange(s, **kw))


def _v(x):
    return x.v if isinstance(x, Buf) else x


class Op:
    __slots__ = ("eng", "fn", "deps", "sig", "cnt", "dma_sem", "dma_val", "dma_prev", "src")

    def __init__(self, eng, fn):
        self.src = None
        self.eng = eng
        self.fn = fn
        self.deps = []
        self.sig = False
        self.cnt = 0
        self.dma_sem = None
        self.dma_val = 0
        self.dma_prev = None


class Eng:
    def __init__(self, name):
        self.name = name
        self.ops = []
        self.sem = None
        self.dma_sems = []
        self.dma_uses = [0] * N_DMA_SEMS
        self.dma_last = [None] * N_DMA_SEMS
        self.n_dma = 0


class Prog:
    def __init__(self, nc):
        self.nc = nc
        self.engs = {n: Eng(n) for n in ("tensor", "vector", "scalar", "gpsimd", "sync")}
        self.stack = contextlib.ExitStack()
        self.n_ops = 0
        self._rr = {}

    def sbuf(self, name, shape, dtype=F32):
        t = self.stack.enter_context(self.nc.sbuf_tensor(name, list(shape), dtype))
        return Buf(name, t[:])

    def psum(self, name, shape, dtype=F32):
        t = self.stack.enter_context(self.nc.psum_tensor(name, list(shape), dtype))
        return Buf(name, t[:])

    def dram(self, name, shape, dtype=F32, kind="Internal"):
        t = self.nc.dram_tensor(name, list(shape), dtype, kind=kind)
        return Buf(name, t.ap())

    def sub(self, buf, name, key):
        return Buf(name, buf.ap[key])

    def ring(self, name, n, shape, dtype=F32, space="sbuf"):
        mk = self.sbuf if space == "sbuf" else self.psum
        return Ring([mk("%s%d" % (name, i), shape, dtype) for i in range(n)])

    def op(self, eng, fn, reads=(), writes=()):
        e = self.engs[eng]
        o = Op(e, fn)
        if DEBUG_SRC:
            import sys as _s
            fr = _s._getframe(1)
            while fr.f_code.co_name in ("op", "_c", "mm", "tr", "act", "tt", "ts", "stt", "copy", "recip", "memset", "dma"):
                fr = fr.f_back
            o.src = fr.f_lineno
        deps = {}
        for b in reads:
            if b.w is not None:
                deps[id(b.w)] = b.w
        for b in writes:
            if b.w is not None:
                deps[id(b.w)] = b.w
            for r in b.r:
                deps[id(r)] = r
        for d in deps.values():
            if d.dma_sem is None and d.eng is e and e.name in NO_SELF_SYNC:
                continue
            o.deps.append(d)
            if d.dma_sem is None:
                d.sig = True
        for b in reads:
            b.r.append(o)
        for b in writes:
            b.w = o
            b.r = []
        e.ops.append(o)
        self.n_ops += 1
        return o

    def dma(self, eng, out, in_, **kw):
        out, in_ = _v(out), _v(in_)
        e = self.engs[eng]
        s = e.n_dma % N_DMA_SEMS
        e.n_dma += 1
        o = self.op(eng, ("dma", out.ap, in_.ap, kw), [in_.buf], [out.buf])
        o.dma_sem = s
        e.dma_uses[s] += 1
        o.dma_val = 16 * e.dma_uses[s]
        o.dma_prev = e.dma_last[s]
        e.dma_last[s] = o
        return o

    def _c(self, eng, method, outs, kw, extra_reads=()):
        reads, writes, args = list(extra_reads), [], {}
        for k, a in kw.items():
            if isinstance(a, (V, Buf)):
                a = _v(a)
                (writes if k in outs else reads).append(a.buf)
                args[k] = a.ap
            else:
                args[k] = a
        return self.op(eng, lambda q: getattr(q, method)(**args), reads, writes)

    def mm(self, out, lhsT, rhs, start=True, stop=True, skip=False):
        out, lhsT, rhs = _v(out), _v(lhsT), _v(rhs)
        oa, la, ra = out.ap, lhsT.ap, rhs.ap
        assert len(ra.shape) == 2 and len(la.shape) == 2 and len(oa.shape) == 2, (oa.shape, la.shape, ra.shape)
        return self.op("tensor", lambda q: q.matmul(oa, lhsT=la, rhs=ra, start=start, stop=stop,
                                                    skip_group_check=skip),
                       [lhsT.buf, rhs.buf], [out.buf])

    def tr(self, out, in_, ident):
        out, in_, ident = _v(out), _v(in_), _v(ident)
        oa, ia, da = out.ap, in_.ap, ident.ap
        return self.op("tensor", lambda q: q.transpose(out=oa, in_=ia, identity=da),
                       [in_.buf, ident.buf], [out.buf])

    def act(self, out, in_, func, bias=0.0, scale=1.0, accum_out=None, eng="scalar"):
        kw = dict(out=out, in_=in_, func=func, bias=bias, scale=scale)
        outs = ["out"]
        if accum_out is not None:
            kw["accum_out"] = accum_out
            outs.append("accum_out")
        return self._c("scalar", "activation", outs, kw)

    def tt(self, eng, out, in0, in1, op):
        return self._c(eng, "tensor_tensor", ["out"], dict(out=out, in0=in0, in1=in1, op=op))

    def ts(self, eng, out, in0, s1, op0, s2=None, op1=None):
        kw = dict(out=out, in0=in0, scalar1=s1, scalar2=s2, op0=op0)
        if op1 is not None:
            kw["op1"] = op1
        return self._c(eng, "tensor_scalar", ["out"], kw)

    def stt(self, eng, out, in0, scalar, in1, op0, op1):
        return self._c(eng, "scalar_tensor_tensor", ["out"],
                       dict(out=out, in0=in0, scalar=scalar, in1=in1, op0=op0, op1=op1))

    def copy(self, eng, out, in_):
        if eng == "scalar":
            return self._c("scalar", "copy", ["out"], dict(out=out, in_=in_))
        return self._c(eng, "tensor_copy", ["out"], dict(out=out, in_=in_))

    def recip(self, out, in_):
        return self._c("vector", "reciprocal", ["out"], dict(out=out, in_=in_))

    def memset(self, eng, out, val):
        out = _v(out)
        oa = out.ap
        return self.op(eng, lambda q: q.memset(oa, val), [], [out.buf])

    def rr(self, key, engs):
        i = self._rr.get(key, 0)
        self._rr[key] = i + 1
        return engs[i % len(engs)]

    def finish(self):
        nc = self.nc
        st = self.stack
        for e in self.engs.values():
            e.sem = st.enter_context(nc.semaphore("s_" + e.name))
            if e.n_dma:
                e.dma_sems = [st.enter_context(nc.semaphore("d_%s%d" % (e.name, i)))
                              for i in range(N_DMA_SEMS)]
        for e in self.engs.values():
            c = 0
            for o in e.ops:
                if o.dma_sem is None and o.sig:
                    c += 1
                    o.cnt = c
        block = st.enter_context(nc.Block())
        prog = self

        def replay(e, q):
            waited = {}

            def wait(sem, key, val):
                if waited.get(key, 0) < val:
                    q.wait_ge(sem, val)
                    waited[key] = val

            for o in e.ops:
                for d in o.deps:
                    if d.dma_sem is not None:
                        wait(d.eng.dma_sems[d.dma_sem], (d.eng.name, d.dma_sem), d.dma_val)
                    else:
                        wait(d.eng.sem, d.eng.name, d.cnt)
                if o.dma_sem is not None:
                    p = o.dma_prev
                    if p is not None:
                        wait(e.dma_sems[p.dma_sem], (e.name, p.dma_sem), p.dma_val)
                    _, oa, ia, kw = o.fn
                    q.dma_start(out=oa, in_=ia, **kw).then_inc(e.dma_sems[o.dma_sem], 16)
                elif o.fn is not None:
                    ins = o.fn(q)
                    if o.sig:
                        ins.then_inc(e.sem, 1)
            if e.name == "sync":
                for e2 in prog.engs.values():
                    if e2.n_dma:
                        for s in range(N_DMA_SEMS):
                            lo = e2.dma_last[s]
                            if lo is not None:
                                wait(e2.dma_sems[s], (e2.name, s), lo.dma_val)

        engs = self.engs

        @block.tensor
        def _(q):
            replay(engs["tensor"], q)

        @block.vector
        def _(q):
            replay(engs["vector"], q)

        @block.scalar
        def _(q):
            replay(engs["scalar"], q)

        @block.gpsimd
        def _(q):
            replay(engs["gpsimd"], q)

        @block.sync
        def _(q):
            replay(engs["sync"], q)

        st.close()


class Ring:
    def __init__(self, bufs):
        self.bufs = bufs
        self.i = 0

    def next(self):
        b = self.bufs[self.i % len(self.bufs)]
        self.i += 1
        return b


DT_SIZE = {F32: 4, F32R: 4, BF16: 2}


class Arena:
    def __init__(self, P, nbytes):
        self.P = P
        self.n4 = nbytes // 4
        t = P.stack.enter_context(P.nc.sbuf_tensor("arena", [128, self.n4], F32))
        self.ap = t[:]
        self.base = 0
        self.off = 0
        self.k = 0

    def persist(self):
        self.base = self.off

    def reset(self):
        self.off = self.base

    def alloc(self, name, shape, dtype=F32, parts=128):
        free = int(np.prod(shape[1:]))
        nb = free * DT_SIZE[dtype]
        n4 = (nb + 31) // 32 * 8
        assert self.off + n4 <= self.n4, "arena overflow at %s (%d KB)" % (name, (self.off + n4) * 4 // 1024)
        ap = self.ap[0:shape[0], self.off:self.off + n4]
        self.off += n4
        if dtype != F32:
            ap = ap.bitcast(dtype)
        ap = ap[:, 0:free]
        if len(shape) == 3:
            ap = ap.rearrange("p (a b) -> p a b", b=shape[2])
        elif len(shape) == 4:
            ap = ap.rearrange("p (a b c) -> p a b c", b=shape[2], c=shape[3])
        self.k += 1
        return Buf("%s_%d" % (name, self.k), ap)

    def ring(self, name, n, shape, dtype=F32):
        return Ring([self.alloc(name, shape, dtype) for _ in range(n)])


def barrier(P, tiny):
    firsts = [P.memset("vector", tiny[0], 0.0), P.memset("gpsimd", tiny[1], 0.0), P.act(tiny[2], tiny[3], AF.Copy)]
    dmas = []
    for e in P.engs.values():
        for s in range(N_DMA_SEMS):
            if e.dma_last[s] is not None:
                dmas.append(e.dma_last[s])
    for f in firsts:
        f.sig = True
    for name, e in P.engs.items():
        o = Op(e, None)
        o.deps = [f for f in firsts if f.eng is not e] + dmas
        e.ops.append(o)


CST_IDENT, CST_BONES, CST_M01, CST_MT, CST_HSEL, CST_SCAN, CST_N = 0, 128, 256, 768, 1024, 1152, 1408


def make_consts():
    c = np.zeros((128, CST_N), np.float32)
    p = np.arange(128)
    c[:, CST_IDENT:CST_IDENT + 128] = np.eye(128)
    c[:, CST_BONES:CST_BONES + 128] = (p[:, None] // 64 == p[None, :] // 64)
    s, t = p[:, None], p[None, :]
    c[:, CST_M01 + 0:CST_M01 + 128] = (t > s)
    c[:, CST_M01 + 128:CST_M01 + 256] = (t >= s)
    c[:, CST_M01 + 256:CST_M01 + 384] = (t < s)
    c[:, CST_M01 + 384:CST_M01 + 512] = (t <= s)
    c[:, CST_MT:CST_MT + 128] = (p[None, :] < p[:, None])
    c[:, CST_MT + 128:CST_MT + 256] = (p[None, :] > p[:, None])
    for g in range(8):
        for h in range(16):
            c[:, CST_HSEL + g * 16 + h] = (h == 2 * g + p // 64)
    sm = np.ones((128, 256), np.float32)
    sm[:, 0] = 0
    sm[:, 128] = 0
    c[:, CST_SCAN:CST_SCAN + 256] = sm
    return c


def fm(vec):
    v = np.asarray(vec, np.float32).reshape(-1, 128)
    return np.ascontiguousarray(v.T)


PV_NG0, PV_NG1, PV_MU, PV_W0, PV_A0, PV_KK, PV_KA, PV_RK, PV_QG, PV_KVG, PV_N = 0, 8, 16, 64, 80, 96, 104, 112, 120, 123, 128


def rope_tables(T):
    rows = T // 64
    row = np.repeat(np.arange(rows), 64).astype(np.float32)
    col = np.tile(np.arange(64), rows).astype(np.float32)
    inv = (1.0 / (10000.0 ** (np.arange(0, 16, 2, dtype=np.float32) / 16))).astype(np.float32)
    ang = np.concatenate([row[:, None] * inv, col[:, None] * inv], axis=-1).astype(np.float32)
    cos, sin = np.cos(ang).T.astype(np.float32), np.sin(ang).T.astype(np.float32)
    out = np.zeros((32, 2, T), np.float32)
    out[0:16, 0], out[16:32, 0] = cos, cos
    out[0:16, 1], out[16:32, 1] = -sin, sin
    return out


class Ctx:
    pass


def build(T, L, stages=("mod", "r1a", "r1b", "r2", "r3", "m1a", "m1b", "m2", "m3"), dbg=()):
    nc = bass.Bass("TRN2", target_bir_lowering=False)
    nc.dge_precook = False
    P = Prog(nc)
    K = Ctx()
    K.P, K.T, K.L = P, T, L
    NT = L + T
    NS = NT // 128
    K.NT, K.NS = NT, NS
    K.CO, K.XO, K.NTP = 1, L + 3, L + T + 4

    def ext(name, shape, dt=F32):
        return P.dram(name, shape, dt, kind="ExternalInput")

    K.x_d = ext("x", [T, D])
    K.ctx_d = ext("ctx", [L, D])
    K.cvec_d = ext("cvec", [128, 16])
    K.pvec_d = ext("pvec", [128, PV_N])
    K.cst_d = ext("cst", [128, CST_N])
    K.rows_d = ext("rows", [3, D])
    K.mod_w_d = ext("mod_w", [2, D, 3 * D])
    K.mod_b_d = ext("mod_b", [2, 3 * D])
    K.w_in_d = ext("rwkv_w_in", [4, D, D])
    K.w1_d = ext("rwkv_w1", [2, D, 64])
    K.w2_d = ext("rwkv_w2", [128, D])
    K.a1_d = ext("rwkv_a1", [2, D, 64])
    K.a2_d = ext("rwkv_a2", [128, D])
    K.wout1_d = ext("rwkv_w_out", [D, D])
    K.mw_in_d = ext("mla_w_in", [D, 1696 + 32])
    K.wqb_d = ext("mla_w_qb", [384, 16 * 128])
    K.wkvb_d = ext("mla_w_kvb", [256, 2048])
    K.wout2_d = ext("mla_w_out", [D, D])
    K.rope_d = ext("rope", [32, 2, T])
    K.out_d = P.dram("out", [T, D], F32, kind="ExternalOutput")

    okind = "ExternalOutput" if dbg else "Internal"

    def scr(name, shape, dt=F32):
        return P.dram(name, shape, dt, kind=("ExternalOutput" if name in dbg else "Internal"))

    K.modrow_d = scr("modrow", [2, 2, 3 * D])
    K.hT_d = scr("hT", [128, 8, K.NTP], BF16)
    K.fm_d = [scr("fm%d" % d, [128, 8, NS, 4, 128], BF16) for d in range(2)]
    K.tm_d = [scr("tm%d" % d, [NT, 3, D], F32R) for d in range(2)]
    K.v_d = scr("vtm", [NT, D], F32R)
    K.sg_d = scr("sg", [NT, D], BF16)
    K.bonus_d = scr("bonus", [NT, 16])
    K.y_d = [scr("y%d" % d, [NT, D]) for d in range(2)]
    K.x1_d = scr("x1", [T, D])
    K.ctx1_d = scr("ctx1", [L, D])
    K.sgT_d = scr("sgT", [128, 8, T], BF16)
    K.oT_d = scr("oT", [128, 8, T], BF16)

    A = Arena(P, 190 * 1024)
    K.A = A
    K.pb = [P.psum("pb%d" % i, [128, 512], F32) for i in range(8)]
    K.pbh = Buf("pbh", K.pb[7].ap.bitcast(BF16))
    K.pbh2 = Buf("pbh2", K.pb[6].ap.bitcast(BF16))

    K.cst = A.alloc("cst", [128, CST_N])
    K.pv = A.alloc("pv", [128, PV_N])
    K.identb = A.alloc("identb", [128, 128], BF16)
    K.cst_r = A.alloc("cstr", [128, CST_N], F32R)
    K.tiny = [A.alloc("tiny%d" % i, [128, 8]) for i in range(4)]
    K.modT = [A.alloc("modT%d" % i, [128, 24, 2]) for i in range(2)]
    K.modA = [[A.alloc("modA", [128, 8]) for w in range(2)] for i in range(2)]
    K.modB = [[A.alloc("modB", [128, 8]) for w in range(2)] for i in range(2)]
    K.negw0 = A.alloc("negw0", [128, 16])
    K.nega0 = A.alloc("nega0", [128, 16])
    K.omka = A.alloc("omka", [128, 8])
    K.gam = A.alloc("gam", [128, 2, 8, NS])
    K.zero = A.alloc("zero", [128, 8, 1], BF16)
    A.persist()
    P.dma("sync", K.cst, K.cst_d)
    P.dma("sync", K.pv, K.pvec_d)
    P.copy("vector", K.identb, K.cst[:, CST_IDENT:CST_IDENT + 128])
    P.copy("vector", K.cst_r, K.cst)
    for i in range(4):
        P.memset("vector", K.tiny[i], 0.0)
    P.memset("vector", K.zero, 0.0)
    P.ts("vector", K.negw0, K.pv[:, PV_W0:PV_W0 + 16], -1.0, ALU.mult)
    P.ts("vector", K.nega0, K.pv[:, PV_A0:PV_A0 + 16], -1.0, ALU.mult)
    P.ts("vector", K.omka, K.pv[:, PV_KA:PV_KA + 8], -1.0, ALU.mult, 1.0, ALU.add)

    def stage_end():
        barrier(P, K.tiny)
        A.reset()

    if "mod" in stages:
        stage_mod(K)
        stage_end()
    if "r1a" in stages:
        stage_hT(K, 0, K.x_d, K.ctx_d)
        stage_end()
    if "r1b" in stages:
        stage_r1b(K)
        stage_end()
    if "r2" in stages:
        stage_r2(K)
        stage_end()
    if "r3" in stages:
        stage_r3(K)
        stage_end()
    if "m1a" in stages:
        stage_hT(K, 1, K.x1_d, K.ctx1_d)
        stage_end()
    if "m1b" in stages:
        stage_mla(K, "m2" in stages, "m3" in stages)
    P.finish()
    return nc, P


def stage_mod(K):
    P, A = K.P, K.A
    cT = A.alloc("cT", [128, 16])
    sc = A.alloc("scT", [128, 8, 2])
    P.dma("sync", cT, K.cvec_d)
    sg = A.alloc("sgc", [128, 16])
    P.act(sg, cT, AF.Sigmoid)
    P.tt("vector", sc.re("p f w -> p w f"), cT.re("p (w f) -> p w f", w=2), sg.re("p (w f) -> p w f", w=2), ALU.mult)
    mwr = A.ring("mw", 3, [128, 512])
    mb = A.alloc("mb", [2, 3 * D])
    mrow = A.alloc("mrow", [2, 3 * D])
    for layer in range(2):
        P.dma("gpsimd", mb, V(K.mod_b_d, K.mod_b_d.ap[layer:layer + 1, :].partition_broadcast(2)))
        for cb in range(6):
            ps = K.pb[cb % 2]
            for f in range(8):
                mw = mwr.next()
                P.dma("sync", mw, K.mod_w_d[layer, f * 128:(f + 1) * 128, cb * 512:(cb + 1) * 512])
                P.mm(ps[0:2, :], sc[:, f, :], mw, start=(f == 0), stop=(f == 7))
            P.tt("vector", mrow[:, cb * 512:(cb + 1) * 512], ps[0:2, :], mb[:, cb * 512:(cb + 1) * 512], ALU.add)
        P.dma("sync", K.modrow_d[layer], mrow)
        pt = K.pb[2]
        for c in range(24):
            P.tr(pt[:, 2 * c:2 * c + 2], mrow[:, c * 128:(c + 1) * 128], K.cst[0:2, CST_IDENT:CST_IDENT + 2])
        P.copy("vector", K.modT[layer].re("p c w -> p (c w)"), pt[:, 0:48])
        for w in range(2):
            ng = K.pv[:, PV_NG0 + 8 * layer:PV_NG0 + 8 * layer + 8]
            P.stt("vector", K.modA[layer][w], K.modT[layer][:, 8:16, w], 1.0, ng, ALU.add, ALU.mult)
            P.copy("vector", K.modB[layer][w], K.modT[layer][:, 0:8, w])


def seq_tiles(K):
    out = []
    for i in range(K.L // 128):
        out.append((1, i, i * 128, i * 128, K.CO + i * 128))
    for i in range(K.T // 128):
        out.append((0, i, i * 128, K.L + i * 128, K.XO + i * 128))
    return out


def stage_hT(K, layer, x_d, ctx_d):
    P, A = K.P, K.A
    xr = A.ring("xt", 3, [128, D])
    jr = A.ring("junk", 2, [128, D], BF16)
    xnr = A.ring("xn", 2, [128, D], BF16)
    tmpr = A.ring("tmp", 2, [128, 8, 128])
    hr = A.ring("ht", 2, [128, 8, 128], BF16)
    ssr = A.ring("ss", 4, [128, 2])
    for col in (0, K.L + 1, K.L + 2, K.L + K.T + 3):
        P.dma("gpsimd", K.hT_d[:, :, col:col + 1], K.zero, allow_slow_non_contiguous=True)
    for (w, i, r0, tok, col) in seq_tiles(K):
        src = ctx_d if w else x_d
        xt = xr.next()
        P.dma("sync", xt, src[r0:r0 + 128, :])
        ss = ssr.next()
        P.act(jr.next(), xt, AF.Square, accum_out=ss[:, 0:1])
        P.act(ss[:, 1:2], ss[:, 0:1], AF.Ln, bias=1e-6, scale=1.0 / D)
        P.act(ss[:, 1:2], ss[:, 1:2], AF.Exp, scale=-0.5)
        xn = xnr.next()
        P.ts("gpsimd", xn, xt, ss[:, 1:2], ALU.mult)
        ph = K.pbh if (tok // 128) % 2 == 0 else K.pbh2
        for f in range(8):
            P.tr(ph[:, f * 128:(f + 1) * 128], xn[:, f * 128:(f + 1) * 128], K.identb)
        tmp = tmpr.next()
        ht = hr.next()
        P.tt("vector", tmp, ph.re("p (f t) -> p f t", t=128), K.modA[layer][w].v.bc(2, [128, 8, 128]), ALU.mult)
        P.tt("gpsimd", ht, tmp, K.modB[layer][w].v.bc(2, [128, 8, 128]), ALU.add)
        P.dma("sync", K.hT_d[:, :, col:col + 128], ht)


def load_w_bf16(K, name, src_ap_v, shape):
    t = K.A.alloc(name, shape, BF16)
    K.P.dma("gpsimd", t, src_ap_v)
    return t


def stage_r1b(K):
    P, A, L, T = K.P, K.A, K.L, K.T
    NB = 256
    pv, cst = K.pv, K.cst
    W = [A.alloc("win%d" % j, [128, 8, D], BF16) for j in range(4)]
    for j in range(4):
        for f in range(8):
            P.dma("gpsimd", W[j][:, f, :], K.w_in_d[j, f * 128:(f + 1) * 128, :])
    W1c = A.alloc("w1c", [128, 8, 128], BF16)
    A1c = A.alloc("a1c", [128, 8, 128], BF16)
    for d in range(2):
        P.dma("gpsimd", W1c[:, :, 64 * d:64 * d + 64], K.w1_d.re("d (f p) r -> d p f r", p=128)[d])
        P.dma("gpsimd", A1c[:, :, 64 * d:64 * d + 64], K.a1_d.re("d (f p) r -> d p f r", p=128)[d])
    W2c = load_w_bf16(K, "w2c", K.w2_d, [128, D])
    A2c = load_w_bf16(K, "a2c", K.a2_d, [128, D])
    bones_r = K.cst_r[:, CST_BONES:CST_BONES + 128]
    scanm = cst[:, CST_SCAN:CST_SCAN + 256]
    hbr = A.ring("hb", 1, [128, 8, NB + 2], BF16)
    xx = A.alloc("xx", [128, 8, NB])
    tmpx = A.ring("tmpx", 2, [128, NB])
    lerp = [A.alloc("lerp%d" % j, [128, 8, NB], BF16) for j in range(6)]
    twb = A.alloc("twb", [128, NB], BF16)
    tab = A.alloc("tab", [128, NB], BF16)
    prod = A.alloc("prod", [128, 2, 8, NB], BF16)
    hselb = A.alloc("hselb", [128, 128], BF16)
    P.copy("vector", hselb, cst[:, CST_HSEL:CST_HSEL + 128])
    vtm = A.ring("vtm", 1, [128, D], F32R)
    sgt = A.ring("sgt", 1, [128, D], BF16)
    bon = A.ring("bon", 2, [128, 16])
    w = lambda nm, n=2, dt=F32: A.ring(nm, n, [128, NB], dt)
    r_sb, k_sb, kkf, sq, nmx, rn, kk = w("r_sb"), w("k_sb"), w("kkf", 1), w("sq", 1, F32R), w("nmx", 1), w("rn", 1), w("kk")
    ew, sw, ea, av, cum, epos, eneg, dprev, eprev = (w("ew", 1), w("sw"), w("ea", 1), w("av"), w("cum"), w("epos"),
                                                     w("eneg"), w("dprev", 1), w("eprev"))
    mfac, kd, bv, ktil, btil, atil = w("mfac", 1), w("kd"), w("bv"), w("ktil"), w("btil"), w("atil")
    fmo = A.ring("fmo", 2, [128, 2, 4, 128], BF16)
    tmo = A.ring("tmo", 2, [128, 2, 3, 128], F32R)

    blocks = [(1, 0, 0, K.CO)]
    for i in range(T // NB):
        blocks.append((0, L + i * NB, L + i * NB, K.XO + i * NB))
    blocks[0] = (1, 0, 0, K.CO)
    assert L == NB

    for (wh, t0, _, c0) in blocks:
        s0 = t0 // 128
        hb = hbr.next()
        P.dma("sync", hb, K.hT_d[:, :, c0 - 1:c0 + NB + 1])
        for f in range(8):
            tx = tmpx.next()
            P.tt("gpsimd", tx, hb[:, f, 0:NB], hb[:, f, 2:NB + 2], ALU.add)
            P.ts("gpsimd", tx, tx, 0.5, ALU.mult)
            P.tt("gpsimd", xx[:, f, :], tx, hb[:, f, 1:NB + 1], ALU.subtract)
            for j in range(6):
                if P.rr("lerp", ["vector", "gpsimd", "vector"]) == "vector":
                    P.stt("vector", lerp[j][:, f, :], xx[:, f, :],
                          pv[:, PV_MU + 8 * j + f:PV_MU + 8 * j + f + 1], hb[:, f, 1:NB + 1], ALU.mult, ALU.add)
                else:
                    tl = tmpx.next()
                    P.ts("gpsimd", tl, xx[:, f, :], pv[:, PV_MU + 8 * j + f:PV_MU + 8 * j + f + 1], ALU.mult)
                    P.tt("gpsimd", lerp[j][:, f, :], tl, hb[:, f, 1:NB + 1], ALU.add)
        for tt in range(NB // 128):
            for (j, kind) in ((2, "v"), (3, "g")):
                dst = vtm.next() if kind == "v" else sgt.next()
                for ch in range(2):
                    ps = K.pb[ch]
                    for f in range(8):
                        P.mm(ps, lerp[j][:, f, tt * 128:(tt + 1) * 128], W[j][:, f, ch * 512:(ch + 1) * 512],
                             start=(f == 0), stop=(f == 7))
                    if kind == "v":
                        P.copy("scalar", dst[:, ch * 512:(ch + 1) * 512], ps)
                    else:
                        P.act(dst[:, ch * 512:(ch + 1) * 512], ps, AF.Silu)
                if kind == "v":
                    P.dma("sync", K.v_d[t0 + tt * 128:t0 + (tt + 1) * 128, :], dst)
                else:
                    P.dma("sync", K.sg_d[t0 + tt * 128:t0 + (tt + 1) * 128, :], dst)
        ps = K.pb[2]
        for f in range(8):
            P.mm(ps[:, 0:NB], W1c[:, f, :], lerp[4][:, f, :], start=(f == 0), stop=(f == 7))
        P.act(twb, ps[:, 0:NB], AF.Tanh)
        for f in range(8):
            P.mm(ps[:, NB:2 * NB], A1c[:, f, :], lerp[5][:, f, :], start=(f == 0), stop=(f == 7))
        P.copy("vector", tab, ps[:, NB:2 * NB])
        for g in range(8):
            gs = slice(g * 128, (g + 1) * 128)
            pr = K.pb[3]
            for f in range(8):
                P.mm(pr[:, 0:NB], W[0][:, f, gs], lerp[0][:, f, :], start=(f == 0), stop=(f == 7))
            for f in range(8):
                P.mm(pr[:, NB:2 * NB], W[1][:, f, gs], lerp[1][:, f, :], start=(f == 0), stop=(f == 7))
            r_, k_ = r_sb.next(), k_sb.next()
            P.copy("scalar", r_, pr[:, 0:NB])
            P.copy("scalar", k_, pr[:, NB:2 * NB])
            kf, sq_, nm, rn_, kk_ = kkf.next(), sq.next(), nmx.next(), rn.next(), kk.next()
            P.act(kf, k_, AF.Identity, scale=pv[:, PV_KK + g:PV_KK + g + 1])
            P.tt("gpsimd", sq_, kf, kf, ALU.mult)
            pn = K.pb[4]
            P.mm(pn[:, 0:NB], bones_r, sq_)
            P.ts("vector", nm, pn[:, 0:NB], 1e-24, ALU.max)
            P.act(rn_, nm, AF.Ln)
            P.act(rn_, rn_, AF.Exp, scale=-0.5)
            P.tt("gpsimd", kk_, kf, rn_, ALU.mult)
            for d in range(2):
                hs = slice(64 * d, 64 * d + 64)
                pl = K.pb[5 + (d % 2)]
                P.mm(pl[:, 0:NB], W2c[hs, gs], twb[hs, :])
                P.mm(pl[:, NB:2 * NB], A2c[hs, gs], tab[hs, :])
                ew_, sw_, ea_, a_ = ew.next(), sw.next(), ea.next(), av.next()
                P.act(ew_, pl[:, 0:NB], AF.Exp, bias=K.negw0[:, 8 * d + g:8 * d + g + 1], scale=-1.0)
                P.act(ea_, pl[:, NB:2 * NB], AF.Exp, bias=K.nega0[:, 8 * d + g:8 * d + g + 1], scale=-1.0)
                P.act(ew_, ew_, AF.Identity, bias=1.0)
                P.recip(sw_, ew_)
                P.act(ea_, ea_, AF.Identity, bias=1.0)
                P.recip(a_, ea_)
                cm = cum.next()
                if d == 0:
                    P._c("vector", "tensor_tensor_scan", ["out"],
                         dict(out=cm, data0=scanm, data1=sw_, initial=0.0, op0=ALU.mult, op1=ALU.add))
                else:
                    P._c("vector", "tensor_tensor_scan", ["out"],
                         dict(out=cm[:, NB - 1::-1], data0=scanm, data1=sw_[:, NB - 1::-1], initial=0.0,
                              op0=ALU.mult, op1=ALU.add))
                ep, en, dp, epv = epos.next(), eneg.next(), dprev.next(), eprev.next()
                P.act(ep, cm, AF.Exp, scale=-C0)
                P.act(en, cm, AF.Exp, scale=C0)
                P.tt("gpsimd", dp, cm, sw_, ALU.subtract)
                P.act(epv, dp, AF.Exp, scale=-C0)
                for s in range(NB // 128):
                    cc = s * 128 + (127 if d == 0 else 0)
                    P.copy("gpsimd", K.gam[:, d, g, s0 + s:s0 + s + 1], ep[:, cc:cc + 1])
                mf, kd_, b_ = mfac.next(), kd.next(), bv.next()
                P.act(mf, a_, AF.Identity, scale=pv[:, PV_KA + g:PV_KA + g + 1], bias=K.omka[:, g:g + 1])
                P.tt("gpsimd", kd_, k_, mf, ALU.mult)
                P.tt("gpsimd", b_, kk_, a_, ALU.mult)
                kt_, bt_, at_ = ktil.next(), btil.next(), atil.next()
                P.tt("vector", kt_, kd_, en, ALU.mult)
                P.tt("vector", bt_, b_, en, ALU.mult)
                P.stt("vector", at_, kk_, -1.0, epv, ALU.mult, ALU.mult)
                tp = tmpx.next()
                P.act(tp, r_, AF.Identity, scale=pv[:, PV_RK + g:PV_RK + g + 1])
                P.tt("gpsimd", prod[:, d, g, :], tp, kd_, ALU.mult)
                fo = fmo.next()
                fo3 = lambda arr: fo[:, :, arr, :]
                P.copy("scalar", fo3(0), bt_.re("p (s t) -> p s t", t=128))
                P.copy("scalar", fo3(1), kt_.re("p (s t) -> p s t", t=128))
                P.copy("gpsimd", fo3(2), at_.re("p (s t) -> p s t", t=128))
                P.tt("vector", fo3(3), r_.re("p (s t) -> p s t", t=128), ep.re("p (s t) -> p s t", t=128), ALU.mult)
                P.dma("sync", K.fm_d[d][:, g, s0:s0 + NB // 128, :, :], fo)
                to = tmo.next()
                pt, pq = K.pb[d % 2], K.pb[2 + (d % 2)]
                for tt in range(NB // 128):
                    for ai, src in enumerate((at_, bt_, kt_)):
                        idx = tt * 3 + ai
                        dst = pt[:, idx * 128:(idx + 1) * 128] if idx < 4 else pq[:, (idx - 4) * 128:(idx - 3) * 128]
                        P.tr(dst, src[:, tt * 128:(tt + 1) * 128], cst[:, CST_IDENT:CST_IDENT + 128])
                tof = to.re("p t a f -> p (t a f)")
                P.copy("scalar", tof[:, 0:512], pt)
                P.copy("scalar", tof[:, 512:768], pq[:, 0:256])
                for tt in range(NB // 128):
                    P.dma("sync", K.tm_d[d][t0 + tt * 128:t0 + (tt + 1) * 128, :, gs], to[:, tt, :, :])
        for tt in range(NB // 128):
            pbn = K.pb[4]
            n = 0
            for d in range(2):
                for g in range(8):
                    P.mm(pbn[:, 256 + 16 * tt:256 + 16 * tt + 16], prod[:, d, g, tt * 128:(tt + 1) * 128],
                         hselb[:, 16 * g:16 * g + 16], start=(n == 0), stop=(n == 15))
                    n += 1
            b_t = bon.next()
            P.copy("vector", b_t, pbn[:, 256 + 16 * tt:256 + 16 * tt + 16])
            P.dma("sync", K.bonus_d[t0 + tt * 128:t0 + (tt + 1) * 128, :], b_t)


def lockstep(gens):
    gens = list(gens)
    while gens:
        for g_ in list(gens):
            try:
                next(g_)
            except StopIteration:
                gens.remove(g_)


def stage_r2(K):
    P, A, L, T, NS = K.P, K.A, K.L, K.T, K.NS
    cst = K.cst
    m512 = [A.alloc("m512", [128, 2, 256]) for d in range(2)]
    for d in range(2):
        for r in range(2):
            P.copy("gpsimd", m512[d][:, r, :], cst[:, CST_M01 + 256 * d:CST_M01 + 256 * d + 256])
    i64 = A.alloc("i64", [128, 64])
    for p in range(2):
        P.copy("gpsimd", i64[64 * p:64 * p + 64, :], cst[64 * p:64 * p + 64, CST_IDENT + 64 * p:CST_IDENT + 64 * p + 64])
    ident = cst[:, CST_IDENT:CST_IDENT + 128]
    tmbr = A.ring("tmb", 2, [128, 3, D], F32)
    vbr = A.ring("vb", 2, [128, D], F32)
    fmbr = A.ring("fmb", 4, [128, 4, 128], BF16)
    ysr = A.ring("ys", 2, [128, D])
    Mst = [A.alloc("Mst", [128, 8, 2, 64], F32) for _ in range(2)]
    pb = K.pb
    PAB = [pb[p].v for p in range(2)]
    PC = [pb[2][:, p * 128:(p + 1) * 128] for p in range(2)]
    PXi = [pb[2][:, 256 + p * 64:256 + (p + 1) * 64] for p in range(2)]
    PG = pb[2][:, 384:512]
    PY = [pb[5], pb[6]]
    PM = pb[7][:, 0:128]
    slots = []
    for q in range(2):
        S = Ctx()
        S.sabr = A.ring("sab", 2, [128, 2, 512], F32)
        S.p0r = A.ring("p0", 2, [128, 2, 128], F32)
        S.xr = A.ring("xp", 4, [128, 2, 128], F32)
        S.rpr = A.ring("rp", 3, [128, 2, 256], F32)
        S.igr = A.ring("ig", 2, [128, 128], F32)
        S.rhr = A.ring("rh", 2, [128, 256], F32)
        S.ahr = A.ring("ah", 2, [128, 128], F32)
        for b in S.igr.bufs + S.rhr.bufs:
            P.memset("gpsimd", b, 0.0)
        ba, bb = (pb[3], pb[4]) if q == 0 else (pb[0], pb[1])
        S.PX = [ba[:, p * 128:(p + 1) * 128] for p in range(2)]
        S.PRh = [ba[:, 256 + 128 * p:384 + 128 * p] for p in range(2)]
        S.PRP = [bb[:, p * 256:(p + 1) * 256] for p in range(2)]
        slots.append(S)

    def pair_body(S, d, s, g, tmb, vb, Mcur, Mnew):
        fmb = fmbr.next()
        P.dma("sync", fmb, K.fm_d[d][:, g, s, :, :])
        sab, p0, x = S.sabr.next(), S.p0r.next(), S.xr.next()
        gc = slice(g * 128, (g + 1) * 128)
        for p in range(2):
            bs = slice(64 * p, 64 * p + 64)
            ar = fmb.re("p a t -> p (a t)")[bs, 256:512]
            P.mm(PAB[p][:, 0:256], fmb[bs, 0, :], ar)
            P.mm(PAB[p][:, 256:512], fmb[bs, 1, :], ar)
        for p in range(2):
            hc = slice(64 * (2 * g + p), 64 * (2 * g + p) + 64)
            P.tt("vector", sab[:, p, :], PAB[p], m512[d].re("p r c -> p (r c)"), ALU.mult)
            P.tr(PC[p], sab[:, p, 0:128], ident)
            P.copy("vector", p0[:, p, :], PC[p])
            P.copy("gpsimd", x[:, p, 0:64], tmb[:, 0, hc])
            P.mm(PXi[p], sab[:, p, 256:384], vb[:, hc])
            P.copy("scalar", x[:, p, 64:128], PXi[p])
        yield
        Rk = [sab[:, p, 0:128] for p in range(2)]
        Pk = [p0[:, p, :] for p in range(2)]
        for k in range(7):
            xn = S.xr.next()
            rp = S.rpr.next() if k < 6 else None
            for p in range(2):
                P.mm(S.PX[p], Rk[p], x[:, p, :])
                if k < 6:
                    P.mm(S.PRP[p][:, 0:128], Pk[p], Rk[p])
                    if k < 5:
                        P.mm(S.PRP[p][:, 128:256], Rk[p], Pk[p])
            yield
            for p in range(2):
                P.tt("vector", xn[:, p, :], S.PX[p], x[:, p, :], ALU.add)
                if k < 5:
                    P.copy("scalar", rp[:, p, :], S.PRP[p])
                elif k == 5:
                    P.copy("scalar", rp[:, p, 0:128], S.PRP[p][:, 0:128])
            yield
            x = xn
            if k < 6:
                Rk = [rp[:, p, 0:128] for p in range(2)]
                Pk = [rp[:, p, 128:256] for p in range(2)]
        ig, rh, ah = S.igr.next(), S.rhr.next(), S.ahr.next()
        for p in range(2):
            P.copy("gpsimd", ah[:, 64 * p:64 * p + 64], x[:, p, 0:64])
        P.mm(PG, ah, tmb[:, 1, gc])
        for p in range(2):
            P.mm(S.PRh[p], ah, sab[:, p, 128:256])
        for p in range(2):
            bs = slice(64 * p, 64 * p + 64)
            P.tt("vector", ig[bs, 64 * p:64 * p + 64], PG[bs, 64 * p:64 * p + 64], i64[bs, :], ALU.add)
            P.tt("vector", rh[bs, 128 * p:128 * p + 128], S.PRh[p][bs, :], fmb[bs, 3, :], ALU.add)
        yield
        for p in range(2):
            h = 2 * g + p
            hc = slice(64 * h, 64 * h + 64)
            py = PY[h // 8][:, (h % 8) * 64:(h % 8) * 64 + 64]
            P.mm(py, sab[:, p, 128:256], x[:, p, 64:128], start=True, stop=False)
            P.mm(py, sab[:, p, 384:512], vb[:, hc], start=False, stop=False)
            P.mm(py, rh[:, 128 * p:128 * p + 128], Mcur[:, g, p, :], start=False, stop=True)
        P.mm(PM[:, 0:64], tmb[:, 1, gc], x[:, 0, 64:128], start=True, stop=False, skip=True)
        P.mm(PM[:, 64:128], tmb[:, 1, gc], x[:, 1, 64:128], start=False, stop=False, skip=True)
        P.mm(PM, tmb[:, 2, gc], vb[:, gc], start=False, stop=False, skip=True)
        P.mm(PM, ig, Mcur.re("p g a i -> p g (a i)")[:, g, :], start=False, stop=True, skip=True)
        for p in range(2):
            bs = slice(64 * p, 64 * p + 64)
            P.act(Mnew[bs, g, p, :], PM[bs, 64 * p:64 * p + 64], AF.Identity, scale=K.gam[bs, d, g, s:s + 1])

    nctx = L // 128
    for d in range(2):
        order = list(range(NS)) if d == 0 else (list(range(nctx - 1, -1, -1)) + list(range(NS - 1, nctx - 1, -1)))
        mi = 0
        P.memset("gpsimd", Mst[0], 0.0)
        P.memset("gpsimd", Mst[1], 0.0)
        for s in order:
            Mcur, Mnew = Mst[mi % 2], Mst[(mi + 1) % 2]
            mi += 1
            tmb, vb, ys = tmbr.next(), vbr.next(), ysr.next()
            P.dma("sync", tmb, K.tm_d[d][s * 128:(s + 1) * 128, :, :])
            P.dma("sync", vb, K.v_d[s * 128:(s + 1) * 128, :])
            for gg in range(4):
                lockstep([pair_body(slots[q], d, s, 2 * gg + q, tmb, vb, Mcur, Mnew) for q in range(2)])
            P.copy("scalar", ys[:, 0:512], PY[0])
            P.copy("vector", ys[:, 512:1024], PY[1])
            P.dma("sync", K.y_d[d][s * 128:(s + 1) * 128, :], ys)


def bcast_row(K, name, src_v):
    n = src_v.ap.shape[-1]
    t = K.A.alloc(name, [128, n])
    K.P.dma("gpsimd", t, V(src_v.buf, src_v.ap.partition_broadcast(128)))
    return t


def stage_r3(K):
    P, A, L, T = K.P, K.A, K.L, K.T
    Wo = A.alloc("wo1", [128, 8, D], BF16)
    for f in range(8):
        P.dma("gpsimd", Wo[:, f, :], K.wout1_d[f * 128:(f + 1) * 128, :])
    lng = bcast_row(K, "lng", K.rows_d[0:1, :])
    lnb = bcast_row(K, "lnb", K.rows_d[1:2, :])
    gx = [bcast_row(K, "gres%d" % w, K.modrow_d[0, w:w + 1, 2 * D:3 * D]) for w in range(2)]
    y0r, y1r, vr, xr_ = (A.ring(n, 2, [128, D]) for n in ("y0", "y1", "vv", "xres"))
    sgr = A.ring("sgl", 2, [128, D], BF16)
    bnr = A.ring("bnl", 2, [128, 16])
    yfr, sqr, t1r = A.ring("yf", 2, [128, D]), A.ring("ysq", 2, [128, D]), A.ring("t1", 2, [128, D])
    obr = A.ring("ob", 2, [128, D], BF16)
    otr = A.ring("oT", 2, [128, 8, 128], BF16)
    str_ = A.ring("stat", 2, [128, 4, 16])
    outr = A.ring("xo", 2, [128, D])
    for (w, i, r0, tok, col) in seq_tiles(K):
        y0, y1, vv, xres, sg, bn = y0r.next(), y1r.next(), vr.next(), xr_.next(), sgr.next(), bnr.next()
        ts_ = slice(tok, tok + 128)
        P.dma("sync", y0, K.y_d[0][ts_, :])
        P.dma("sync", y1, K.y_d[1][ts_, :])
        P.dma("sync", vv, V(K.v_d, K.v_d.ap[ts_, :].bitcast(F32)))
        P.dma("sync", sg, K.sg_d[ts_, :])
        P.dma("sync", bn, K.bonus_d[ts_, :])
        P.dma("sync", xres, (K.ctx_d if w else K.x_d)[r0:r0 + 128, :])
        yf, sq, t1, st = yfr.next(), sqr.next(), t1r.next(), str_.next()
        P.tt("gpsimd", yf, y0, y1, ALU.add)
        P.act(sq, yf, AF.Square)
        h3 = lambda b: b.re("p (h k) -> p h k", k=64)
        P._c("vector", "tensor_reduce", ["out"], dict(out=st[:, 0, :], in_=h3(yf), axis=AX.X, op=ALU.add))
        P._c("vector", "tensor_reduce", ["out"], dict(out=st[:, 1, :], in_=h3(sq), axis=AX.X, op=ALU.add))
        P.ts("vector", st[:, 0, :], st[:, 0, :], 1.0 / 64, ALU.mult)
        P.tt("vector", st[:, 2, :], st[:, 0, :], st[:, 0, :], ALU.mult)
        P.stt("vector", st[:, 1, :], st[:, 1, :], 1.0 / 64, st[:, 2, :], ALU.mult, ALU.subtract)
        P.act(st[:, 3, :], st[:, 1, :], AF.Ln, bias=64e-5)
        P.act(st[:, 3, :], st[:, 3, :], AF.Exp, scale=-0.5)
        bc = lambda v_: v_.bc(2, [128, 16, 64])
        P.tt("vector", h3(t1), h3(yf), bc(st[:, 0, :]), ALU.subtract)
        P.tt("gpsimd", h3(t1), h3(t1), bc(st[:, 3, :]), ALU.mult)
        P.tt("vector", t1, t1, lng, ALU.mult)
        P.tt("gpsimd", t1, t1, lnb, ALU.add)
        P.tt("vector", h3(vv), h3(vv), bc(bn.v), ALU.mult)
        P.tt("gpsimd", t1, t1, vv, ALU.add)
        ob = obr.next()
        P.tt("vector", ob, t1, sg, ALU.mult)
        ph = K.pbh if (tok // 128) % 2 == 0 else K.pbh2
        for f in range(8):
            P.tr(ph[:, f * 128:(f + 1) * 128], ob[:, f * 128:(f + 1) * 128], K.identb)
        oT = otr.next()
        P.copy("scalar", oT.re("p f t -> p (f t)"), ph)
        for ch in range(2):
            ps = K.pb[ch]
            for f in range(8):
                P.mm(ps, oT[:, f, :], Wo[:, f, ch * 512:(ch + 1) * 512], start=(f == 0), stop=(f == 7))
        xo = outr.next()
        for ch in range(2):
            cs = slice(ch * 512, (ch + 1) * 512)
            P.tt("vector", xo[:, cs], K.pb[ch], gx[w][:, cs], ALU.mult)
        P.tt("gpsimd", xo, xo, xres, ALU.add)
        P.dma("sync", (K.ctx1_d if w else K.x1_d)[r0:r0 + 128, :], xo)


def stage_mla(K, do_attn=True, do_final=True):
    return stage_mla_impl(K, do_attn, do_final)


def host_inputs(inp, b, T, L):
    f32 = lambda a: np.ascontiguousarray(np.asarray(a, np.float32))
    pv = np.zeros((128, PV_N), np.float32)
    pv[:, PV_NG0:PV_NG0 + 8] = fm(inp["norm_g"][0])
    pv[:, PV_NG1:PV_NG1 + 8] = fm(inp["norm_g"][1])
    for j in range(6):
        pv[:, PV_MU + 8 * j:PV_MU + 8 * j + 8] = fm(inp["rwkv_mu"][0, j])
    for d in range(2):
        pv[:, PV_W0 + 8 * d:PV_W0 + 8 * d + 8] = fm(inp["rwkv_w0"][0, d])
        pv[:, PV_A0 + 8 * d:PV_A0 + 8 * d + 8] = fm(inp["rwkv_a0"][0, d])
    pv[:, PV_KK:PV_KK + 8] = fm(inp["rwkv_k_k"][0])
    pv[:, PV_KA:PV_KA + 8] = fm(inp["rwkv_k_a"][0])
    pv[:, PV_RK:PV_RK + 8] = fm(np.asarray(inp["rwkv_r_k"][0]).reshape(-1))
    pv[:, PV_QG:PV_QG + 3] = fm(inp["mla_q_norm_g"][0])
    pv[:, PV_KVG:PV_KVG + 2] = fm(inp["mla_kv_norm_g"][0])
    cvec = np.concatenate([fm(inp["c"][b]), fm(inp["c_ctx"])], axis=1)
    w_in2 = np.asarray(inp["mla_w_in"][0], np.float32)
    w_in2 = np.concatenate([w_in2, w_in2[:, 656:672], w_in2[:, 640:656]], axis=1)
    wqb = np.asarray(inp["mla_w_qb"][0], np.float32).reshape(384, 16, 96)
    wqb = np.concatenate([wqb, wqb[:, :, 80:96], wqb[:, :, 64:80]], axis=2).reshape(384, 16 * 128)
    return {
        "x": f32(inp["x"][b][:T]), "ctx": f32(inp["ctx"][b][:L]), "cvec": f32(cvec), "pvec": pv,
        "cst": make_consts(),
        "rows": f32(np.stack([inp["rwkv_lnx_g"][0], inp["rwkv_lnx_b"][0], inp["final_g"]])),
        "mod_w": f32(inp["mod_w"]), "mod_b": f32(inp["mod_b"]),
        "rwkv_w_in": f32(inp["rwkv_w_in"][0]), "rwkv_w1": f32(inp["rwkv_w1"][0]),
        "rwkv_w2": f32(np.asarray(inp["rwkv_w2"][0]).reshape(128, D)),
        "rwkv_a1": f32(inp["rwkv_a1"][0]), "rwkv_a2": f32(np.asarray(inp["rwkv_a2"][0]).reshape(128, D)),
        "rwkv_w_out": f32(inp["rwkv_w_out"][0]),
        "mla_w_in": f32(w_in2), "mla_w_qb": f32(wqb), "mla_w_kvb": f32(inp["mla_w_kvb"][0]),
        "mla_w_out": f32(inp["mla_w_out"][0]), "rope": rope_tables(T),
    }


_CACHE = {}


def kernel(**inputs):
    B, T, _ = inputs["x"].shape
    L = inputs["ctx"].shape[1]
    key = (T, L)
    if key not in _CACHE:
        _CACHE[key] = build(T, L)[0]
    nc = _CACHE[key]
    in_maps = [host_inputs(inputs, b, T, L) for b in range(B)]
    res = run_bass_kernel_spmd(nc, in_maps, core_ids=list(range(B)))
    return np.stack([np.asarray(r["out"], np.float32) for r in res.results], axis=0)


def stage_mla_impl(K, do_attn=True, do_final=True):
    P, A, L, T, NT, NS = K.P, K.A, K.L, K.T, K.NT, K.NS
    pv, cst, pb = K.pv, K.cst, K.pb
    SCL = 1.0 / math.sqrt(96.0)
    qnT = A.alloc("qnT", [128, 3, T], BF16)
    kvnT = A.alloc("kvnT", [128, 2, NT], BF16)
    KT = A.alloc("KT", [128, NT], BF16)
    ones_r = A.alloc("ones_r", [128, 128], F32R)
    ones_f = A.alloc("ones_f", [128, 64])
    P.memset("vector", ones_r, 1.0)
    P.memset("vector", ones_f, 1.0)
    mark = A.off
    Win = A.alloc("mwin", [128, 8, 1728], BF16)
    for f in range(8):
        P.dma("gpsimd", Win[:, f, :], K.mw_in_d[f * 128:(f + 1) * 128, :])
    hbr = A.ring("mhb", 1, [128, 8, 512], BF16)
    qc = A.alloc("qc", [128, 5, 512])
    sqr = A.ring("msq", 1, [128, 512], F32R)
    rsr = A.ring("mrs", 1, [128, 512])
    rpr = A.ring("mrp", 1, [128, 2, 512])
    t1r = A.ring("mt1", 1, [128, 512])
    t2r = A.ring("mt2", 1, [128, 512])
    sgo = A.ring("msg", 1, [128, 8, 512], BF16)
    blocks = [(1, 0, K.CO, L, 0)] + [(0, L + i * 512, K.XO + i * 512, 512, i * 512) for i in range(T // 512)]
    for (wh, t0, c0, nb, xt0) in blocks:
        hb = hbr.next()
        P.dma("sync", hb[:, :, 0:nb], K.hT_d[:, :, c0:c0 + nb])
        tiles = ([] if wh else [0, 1, 2]) + [3, 4]
        for m in tiles:
            for f in range(8):
                P.mm(pb[m][:, 0:nb], Win[:, f, m * 128:(m + 1) * 128], hb[:, f, 0:nb], start=(f == 0), stop=(f == 7))
            P.copy("scalar", qc[:, m, 0:nb], pb[m][:, 0:nb])
        for (ms, nfeat, dst, gcol) in (([] if wh else [0, 1, 2], 384.0, qnT, PV_QG), ([3, 4], 256.0, kvnT, PV_KVG)):
            if not ms:
                continue
            for i, m in enumerate(ms):
                sq = sqr.next()
                P.tt("gpsimd", sq[:, 0:nb], qc[:, m, 0:nb], qc[:, m, 0:nb], ALU.mult)
                P.mm(pb[5][:, 0:nb], ones_r, sq[:, 0:nb], start=(i == 0), stop=(i == len(ms) - 1))
            rs = rsr.next()
            P.act(rs[:, 0:nb], pb[5][:, 0:nb], AF.Ln, bias=1e-6, scale=1.0 / nfeat)
            P.act(rs[:, 0:nb], rs[:, 0:nb], AF.Exp, scale=-0.5)
            for i, m in enumerate(ms):
                o_ = dst[:, i, xt0:xt0 + nb] if dst is qnT else dst[:, i, t0:t0 + nb]
                P.stt("vector", o_, qc[:, m, 0:nb], pv[:, gcol + i:gcol + i + 1], rs[:, 0:nb], ALU.mult, ALU.mult)
        pe, sw = pb[6][64:96, 0:nb], pb[7][64:96, 0:nb]
        for f in range(8):
            P.mm(pe, Win[:, f, 640:672], hb[:, f, 0:nb], start=(f == 0), stop=(f == 7))
        if wh:
            P.copy("scalar", KT[64:96, t0:t0 + nb], pe)
        else:
            for f in range(8):
                P.mm(sw, Win[:, f, 1696:1728], hb[:, f, 0:nb], start=(f == 0), stop=(f == 7))
            rp = rpr.next()
            P.dma("sync", rp[64:96, :, :], K.rope_d[:, :, xt0:xt0 + nb])
            t1, t2 = t1r.next(), t2r.next()
            P.tt("vector", t1[64:96, :], pe, rp[64:96, 0, :], ALU.mult)
            P.tt("vector", t2[64:96, :], sw, rp[64:96, 1, :], ALU.mult)
            P.tt("gpsimd", KT[64:96, t0:t0 + nb], t1[64:96, :], t2[64:96, :], ALU.add)
            so = sgo.next()
            for g in range(8):
                ps = pb[g % 4]
                for f in range(8):
                    P.mm(ps, Win[:, f, 672 + g * 128:672 + (g + 1) * 128], hb[:, f, :], start=(f == 0), stop=(f == 7))
                P.act(so[:, g, :], ps, AF.Silu)
            P.dma("sync", K.sgT_d[:, :, xt0:xt0 + 512], so)
    barrier(P, K.tiny)
    A.off = mark
    if not do_attn:
        return
    QT = A.alloc("QT", [128, T], BF16)
    Vh = A.alloc("Vh", [128, NS, 65], BF16)
    P.memset("vector", Vh, 1.0)
    wqr = A.ring("wq", 2, [128, 3, 128], BF16)
    wkr = A.ring("wk", 2, [128, 2, 128], BF16)
    ptr = A.ring("pt", 3, [128, 512], BF16)
    rpr = A.ring("arp", 2, [128, 2, 512])
    t1r = A.ring("at1", 2, [128, 512])
    t2r = A.ring("at2", 2, [128, 512])
    recr = A.ring("rec", 2, [128, 512])
    bcr = A.ring("bcs", 2, [64, 512])
    otr = A.ring("oth", 2, [64, 512], BF16)
    kblocks = [(k0, min(512, NT - k0)) for k0 in range(0, NT, 512)]
    for h in range(NH):
        wq, wk = wqr.next(), wkr.next()
        P.dma("gpsimd", wq, K.wqb_d.re("(k p) n -> p k n", p=128)[:, :, h * 128:(h + 1) * 128])
        P.dma("gpsimd", wk, K.wkvb_d.re("(k p) n -> p k n", p=128)[:, :, h * 128:(h + 1) * 128])
        for (k0, nk) in kblocks:
            for kt in range(2):
                P.mm(pb[0][0:64, 0:nk], wk[:, kt, 0:64], kvnT[:, kt, k0:k0 + nk], start=(kt == 0), stop=(kt == 1))
            P.copy("scalar", KT[0:64, k0:k0 + nk], pb[0][0:64, 0:nk])
        for j0 in range(0, NS, 8):
            nj = min(8, NS - j0)
            for j in range(nj):
                for kt in range(2):
                    P.mm(pb[1][:, j * 64:(j + 1) * 64], kvnT[:, kt, (j0 + j) * 128:(j0 + j + 1) * 128], wk[:, kt, 64:128],
                         start=(kt == 0), stop=(kt == 1))
            P.copy("vector", Vh[:, j0:j0 + nj, 0:64], pb[1][:, 0:nj * 64].re("p (j c) -> p j c", c=64))
        for qb in range(T // 512):
            qs = slice(qb * 512, (qb + 1) * 512)
            for kt in range(3):
                P.mm(pb[2][0:96, :], wq[:, kt, 0:96], qnT[:, kt, qs], start=(kt == 0), stop=(kt == 2))
            for kt in range(3):
                P.mm(pb[3][64:96, :], wq[:, kt, 96:128], qnT[:, kt, qs], start=(kt == 0), stop=(kt == 2))
            rp = rpr.next()
            P.dma("sync", rp[64:96, :, :], K.rope_d[:, :, qs])
            t1, t2 = t1r.next(), t2r.next()
            P.copy("scalar", QT[0:64, qs], pb[2][0:64, :])
            P.tt("vector", t1[64:96, :], pb[2][64:96, :], rp[64:96, 0, :], ALU.mult)
            P.tt("vector", t2[64:96, :], pb[3][64:96, :], rp[64:96, 1, :], ALU.mult)
            P.tt("gpsimd", QT[64:96, qs], t1[64:96, :], t2[64:96, :], ALU.add)
        for qb in range(T // 512):
            qs = slice(qb * 512, (qb + 1) * 512)
            po = pb[4 + qb % 2]
            LOOK = ATT_LOOK
            for kt in range(min(LOOK, NS)):
                P.mm(pb[kt % 4], KT[0:96, kt * 128:(kt + 1) * 128], QT[0:96, qs])
            for kt in range(NS):
                if kt + LOOK < NS:
                    k2 = kt + LOOK
                    P.mm(pb[k2 % 4], KT[0:96, k2 * 128:(k2 + 1) * 128], QT[0:96, qs])

                pt = ptr.next()
                P.act(pt, pb[kt % 4], AF.Exp, scale=SCL)
                P.mm(po[0:65, :], Vh[:, kt, :], pt, start=(kt == 0), stop=(kt == NS - 1))
            rec, bcs, ot = recr.next(), bcr.next(), otr.next()
            P.recip(rec[64:65, :], po[64:65, :])
            P.mm(pb[6][0:64, :], ones_f[64:65, :], rec[64:65, :])
            P.copy("scalar", bcs, pb[6][0:64, :])
            P.tt("vector", ot, po[0:64, :], bcs, ALU.mult)
            P.dma("sync", K.oT_d[64 * (h % 2):64 * (h % 2) + 64, h // 2, qs], ot)
    barrier(P, K.tiny)
    A.off = A.base
    if not do_final:
        return
    Wo = A.alloc("wo2", [128, 8, D], BF16)
    for f in range(8):
        P.dma("gpsimd", Wo[:, f, :], K.wout2_d[f * 128:(f + 1) * 128, :])
    gx2 = bcast_row(K, "gx2", K.modrow_d[1, 0:1, 2 * D:3 * D])
    fing = bcast_row(K, "fing", K.rows_d[2:3, :])
    obr = A.ring("fo", 2, [128, 8, 512], BF16)
    sbr = A.ring("fs", 2, [128, 8, 512], BF16)
    ogr = A.ring("fg", 2, [128, 8, 512], BF16)
    x1r = A.ring("fx", 2, [128, D])
    x2r = A.ring("fx2", 2, [128, D])
    jr = A.ring("fj", 1, [128, D], BF16)
    ssr = A.ring("fss", 4, [128, 2])
    for qb in range(T // 512):
        qs = slice(qb * 512, (qb + 1) * 512)
        ob, sb, og = obr.next(), sbr.next(), ogr.next()
        P.dma("sync", ob, K.oT_d[:, :, qs])
        P.dma("sync", sb, K.sgT_d[:, :, qs])
        P.tt("gpsimd", og, ob, sb, ALU.mult)
        for tt in range(4):
            r0 = qb * 512 + tt * 128
            x1 = x1r.next()
            P.dma("sync", x1, K.x1_d[r0:r0 + 128, :])
            for ch in range(2):
                for f in range(8):
                    P.mm(pb[ch], og[:, f, tt * 128:(tt + 1) * 128], Wo[:, f, ch * 512:(ch + 1) * 512], start=(f == 0), stop=(f == 7))
            x2 = x2r.next()
            for ch in range(2):
                cs = slice(ch * 512, (ch + 1) * 512)
                P.tt("vector", x2[:, cs], pb[ch], gx2[:, cs], ALU.mult)
            P.tt("gpsimd", x2, x2, x1, ALU.add)
            ss = ssr.next()
            P.act(jr.next(), x2, AF.Square, accum_out=ss[:, 0:1])
            P.act(ss[:, 1:2], ss[:, 0:1], AF.Ln, bias=1e-6, scale=1.0 / D)
            P.act(ss[:, 1:2], ss[:, 1:2], AF.Exp, scale=-0.5)
            P.stt("vector", x2, x2, ss[:, 1:2], fing, ALU.mult, ALU.mult)
            P.dma("sync", K.out_d[r0:r0 + 128, :], x2)
```
